# Optimizing a Trainium2 kernel written in Bass

```python
import math
import jax, jax.numpy as jnp
from jax import lax
import numpy as np

D_MODEL = 1024
BATCH = 8
SEQ = 4096
DEPTH = 2

N_META = 16
D_FF = 2816
EPS = 1e-6
FOURIER_GROUPS = 8
FOURIER_GROUP_DIM = 64
D_FOURIER = FOURIER_GROUPS * FOURIER_GROUP_DIM
D_SCONV = 512
SCONV_WIDTH = 3
DN_HEADS = 4
DN_HEAD_DIM = 128
D_DN = DN_HEADS * DN_HEAD_DIM
DN_CONV_WIDTH = 3
CHUNK = 64
N_BRANCH = 3
PROJ_SIZES = (D_FOURIER, 3 * D_SCONV, 3 * D_DN, D_DN, 2 * DN_HEADS, 2 * DN_HEADS, N_BRANCH * D_MODEL)
D_IN_PROJ = D_FOURIER + 3 * D_SCONV + 3 * D_DN + D_DN + 4 * DN_HEADS + N_BRANCH * D_MODEL

kernel_name = "hybrid_fourier_shortconv_gdn_encoder"


def rmsnorm(x, g):
    xf = x.astype(jnp.float32)
    y = xf * lax.rsqrt(jnp.mean(xf * xf, axis=-1, keepdims=True) + EPS)
    return (y * g.astype(jnp.float32)).astype(x.dtype)


def swiglu(x, wi, wo):
    a, b = jnp.split(x @ wi, 2, axis=-1)
    return (jax.nn.silu(a) * b) @ wo


def conv3_centred(x, w):
    xp = jnp.pad(x, ((0, 0), (1, 1), (0, 0)))
    return w[0] * xp[:, :-2] + w[1] * xp[:, 1:-1] + w[2] * xp[:, 2:]


def fourier_mix(u):
    B, L, _ = u.shape
    ug = u.astype(jnp.float32).reshape(B, L, FOURIER_GROUPS, FOURIER_GROUP_DIM)
    y = jnp.fft.fft2(ug, axes=(1, 3), norm="ortho").real
    return y.reshape(B, L, D_FOURIER).astype(u.dtype)


def short_conv_mix(u, w):
    b, c, h = jnp.split(u, 3, axis=-1)
    return b * conv3_centred(c * h, w)


def l2norm(x):
    return x * lax.rsqrt(jnp.sum(x * x, axis=-1, keepdims=True) + EPS)


def chunk_gated_delta(q, k, v, beta, g):
    B, H, N, C, DK = q.shape
    DV = v.shape[-1]
    lower = jnp.tril(jnp.ones((C, C), bool))
    strict = jnp.tril(jnp.ones((C, C), bool), -1)
    gc = jnp.cumsum(g, axis=-1)
    diff = gc[..., :, None] - gc[..., None, :]
    decay_mask = jnp.exp(jnp.where(lower, diff, -jnp.inf))
    k_beta = k * beta[..., None]
    v_beta = v * beta[..., None]
    m = jnp.where(strict, jnp.einsum('bhnck,bhnsk->bhncs', k_beta, k) * decay_mask, 0.0)
    eye = jnp.eye(C, dtype=q.dtype)
    t = lax.linalg.triangular_solve(eye + m, jnp.broadcast_to(eye, m.shape), left_side=True,
                                    lower=True, unit_diagonal=True)
    u = t @ v_beta
    w = t @ (k_beta * jnp.exp(gc)[..., None])
    attn = jnp.einsum('bhnck,bhnsk->bhncs', q, k) * decay_mask
    q_dec = q * jnp.exp(gc)[..., None]
    k_dec = k * jnp.exp(gc[..., -1:] - gc)[..., None]
    g_last = jnp.exp(gc[..., -1])

    def step(S, xs):
        attn_n, u_n, w_n, qd_n, kd_n, gl_n = xs
        v_new = u_n - w_n @ S
        o = qd_n @ S + attn_n @ v_new
        S = S * gl_n[..., None, None] + jnp.swapaxes(kd_n, -1, -2) @ v_new
        return S, o

    xs = tuple(jnp.moveaxis(a, 2, 0) for a in (attn, u, w, q_dec, k_dec, g_last))
    S0 = jnp.zeros((B, H, DK, DV), q.dtype)
    _, o = lax.scan(step, S0, xs)
    return jnp.moveaxis(o, 0, 2)


def deltanet_branch(qkv, z, beta_logit, alpha_logit, conv_w, A_log, dt_bias, norm_w):
    B, L, _ = qkv.shape
    f32 = jnp.float32
    qkv = jax.nn.silu(conv3_centred(qkv, conv_w)).astype(f32)
    q, k, v = jnp.split(qkv, 3, axis=-1)
    q = l2norm(q.reshape(B, L, DN_HEADS, DN_HEAD_DIM)) * (DN_HEAD_DIM ** -0.5)
    k = l2norm(k.reshape(B, L, DN_HEADS, DN_HEAD_DIM))
    v = v.reshape(B, L, DN_HEADS, DN_HEAD_DIM)
    beta = jax.nn.sigmoid(beta_logit.astype(f32)).reshape(B, L, 2, DN_HEADS)
    g = -jnp.exp(A_log.astype(f32)) * jax.nn.softplus(
        alpha_logit.astype(f32).reshape(B, L, 2, DN_HEADS) + dt_bias.astype(f32))
    pad = CHUNK - N_META
    padt = lambda a: jnp.pad(a, ((0, 0), (pad, 0)) + ((0, 0),) * (a.ndim - 2))
    q, k, v, beta, g = padt(q), padt(k), padt(v), padt(beta), padt(g)
    Lp = L + pad
    N = Lp // CHUNK

    def chunks4(a):
        return a.reshape(B, N, CHUNK, DN_HEADS, -1).transpose(0, 3, 1, 2, 4)

    def chunks3(a):
        return a.reshape(B, N, CHUNK, DN_HEADS).transpose(0, 3, 1, 2)

    def unchunk(o):
        return o.transpose(0, 2, 3, 1, 4).reshape(B, Lp, DN_HEADS, DN_HEAD_DIM)

    o_fwd = unchunk(chunk_gated_delta(chunks4(q), chunks4(k), chunks4(v),
                                      chunks3(beta[:, :, 0]), chunks3(g[:, :, 0])))
    fl = lambda a: jnp.flip(a, axis=1)
    o_bwd = fl(unchunk(chunk_gated_delta(chunks4(fl(q)), chunks4(fl(k)), chunks4(fl(v)),
                                         chunks3(fl(beta[:, :, 1])), chunks3(fl(g[:, :, 1])))))
    o = (o_fwd + o_bwd)[:, pad:]
    o = o * lax.rsqrt(jnp.mean(o * o, axis=-1, keepdims=True) + EPS) * norm_w.astype(f32)
    o = o * jax.nn.silu(z.astype(f32).reshape(B, L, DN_HEADS, DN_HEAD_DIM))
    return o.reshape(B, L, D_DN).astype(z.dtype)


def hybrid_mixer(h, w_in, sconv_w, dn_conv_w, dn_A_log, dn_dt_bias, dn_norm,
                 w_fourier, w_sconv_out, w_dn_out, w_out):
    B, L, D = h.shape
    proj = h @ w_in
    idx = [int(s) for s in np.cumsum(PROJ_SIZES)[:-1]]
    u_f, u_sc, u_qkv, z, b_logit, a_logit, gate_logit = jnp.split(proj, idx, axis=-1)
    y_f = fourier_mix(u_f) @ w_fourier
    y_sc = short_conv_mix(u_sc, sconv_w) @ w_sconv_out
    y_dn = deltanet_branch(u_qkv, z, b_logit, a_logit, dn_conv_w, dn_A_log, dn_dt_bias,
                           dn_norm) @ w_dn_out
    gates = jax.nn.sigmoid(gate_logit).reshape(B, L, N_BRANCH, D)
    merged = gates[:, :, 0] * y_f + gates[:, :, 1] * y_sc + gates[:, :, 2] * y_dn
    return merged @ w_out


def setup_inputs(seed: int = 0) -> dict:
    key = jax.random.key(seed)
    ks = jax.random.split(key, 20)
    f32 = jnp.float32

    def dense(k, shape, fan_in):
        return jax.random.normal(k, shape, f32) * fan_in ** -0.5

    x = jax.random.normal(ks[0], (BATCH, SEQ, D_MODEL), f32)
    meta_tokens = jax.random.normal(ks[1], (N_META, D_MODEL), f32)
    norm_gains = 1.0 + 0.05 * jax.random.normal(ks[2], (DEPTH, 6, D_MODEL), f32)
    ffn1_wi = dense(ks[3], (DEPTH, D_MODEL, 2 * D_FF), D_MODEL)
    ffn1_wo = dense(ks[4], (DEPTH, D_FF, D_MODEL), D_FF)
    ffn2_wi = dense(ks[5], (DEPTH, D_MODEL, 2 * D_FF), D_MODEL)
    ffn2_wo = dense(ks[6], (DEPTH, D_FF, D_MODEL), D_FF)
    w_in = dense(ks[7], (DEPTH, D_MODEL, D_IN_PROJ), D_MODEL)
    sconv_w = dense(ks[8], (DEPTH, SCONV_WIDTH, D_SCONV), SCONV_WIDTH)
    dn_conv_w = dense(ks[9], (DEPTH, DN_CONV_WIDTH, 3 * D_DN), DN_CONV_WIDTH)
    dn_A_log = jnp.log(jax.random.uniform(ks[10], (DEPTH, 2, DN_HEADS), f32, 1.0, 16.0))
    dt = jnp.exp(jax.random.uniform(ks[11], (DEPTH, 2, DN_HEADS), f32,
                                    math.log(1e-3), math.log(1e-1)))
    dn_dt_bias = dt + jnp.log(-jnp.expm1(-dt))
    dn_norm = 1.0 + 0.05 * jax.random.normal(ks[12], (DEPTH, DN_HEAD_DIM), f32)
    w_fourier = dense(ks[13], (DEPTH, D_FOURIER, D_MODEL), D_FOURIER)
    w_sconv_out = dense(ks[14], (DEPTH, D_SCONV, D_MODEL), D_SCONV)
    w_dn_out = dense(ks[15], (DEPTH, D_DN, D_MODEL), D_DN)
    w_out = dense(ks[16], (DEPTH, D_MODEL, D_MODEL), D_MODEL)
    return {"x": x, "meta_tokens": meta_tokens, "norm_gains": norm_gains,
            "ffn1_wi": ffn1_wi, "ffn1_wo": ffn1_wo, "ffn2_wi": ffn2_wi, "ffn2_wo": ffn2_wo,
            "w_in": w_in, "sconv_w": sconv_w, "dn_conv_w": dn_conv_w, "dn_A_log": dn_A_log,
            "dn_dt_bias": dn_dt_bias, "dn_norm": dn_norm, "w_fourier": w_fourier,
            "w_sconv_out": w_sconv_out, "w_dn_out": w_dn_out, "w_out": w_out}


def reference(x, meta_tokens, norm_gains, ffn1_wi, ffn1_wo, ffn2_wi, ffn2_wo, w_in, sconv_w,
              dn_conv_w, dn_A_log, dn_dt_bias, dn_norm, w_fourier, w_sconv_out, w_dn_out, w_out):
    B = x.shape[0]
    meta = jnp.broadcast_to(meta_tokens[None].astype(x.dtype), (B, N_META, D_MODEL))
    h = jnp.concatenate([meta, x], axis=1)
    for l in range(DEPTH):
        n = norm_gains[l]
        h = h + 0.5 * rmsnorm(swiglu(rmsnorm(h, n[0]), ffn1_wi[l], ffn1_wo[l]), n[1])
        h = h + rmsnorm(hybrid_mixer(rmsnorm(h, n[2]), w_in[l], sconv_w[l], dn_conv_w[l],
                                     dn_A_log[l], dn_dt_bias[l], dn_norm[l], w_fourier[l],
                                     w_sconv_out[l], w_dn_out[l], w_out[l]), n[3])
        h = h + 0.5 * rmsnorm(swiglu(rmsnorm(h, n[4]), ffn2_wi[l], ffn2_wo[l]), n[5])
    return h[:, N_META:]
```

```python
import numpy as np
import ml_dtypes
import concourse.bass as bass
import concourse.mybir as mybir
from concourse.bass_utils import run_bass_kernel_spmd

F32 = mybir.dt.float32
BF16 = mybir.dt.bfloat16
AF = mybir.ActivationFunctionType
ALU = mybir.AluOpType
EPS = 1e-6
NEG = -30000.0


class Cfg:
    def __init__(s, D=1024, DFF=2816, SEQ=4096, FG=8, DSC=512, H=4, TT=512, DEPTH=2):
        s.D, s.DFF, s.SEQ, s.FG, s.DSC, s.H, s.TT, s.DEPTH = D, DFF, SEQ, FG, DSC, H, TT, DEPTH
        s.NM = 16
        s.L = SEQ + 16
        s.DF = FG * 64
        s.DDN = H * 128
        s.KD = D // 128
        s.KF = DFF // 128
        s.NCH = SEQ // 64 + 1
        s.LP = 64 * s.NCH
        s.NB = (s.L + 127) // 128
        s.o_f = 0
        s.o_sc = s.DF
        s.o_qkv = s.o_sc + 3 * DSC
        s.o_z = s.o_qkv + 3 * s.DDN
        s.o_b = s.o_z + s.DDN
        s.o_a = s.o_b + 2 * H
        s.o_g = s.o_a + 2 * H
        s.DIN = s.o_g + 3 * D
        s.tiles = []
        nt = SEQ // TT
        for i in range(nt):
            sl = [(i * TT, TT)] if TT <= 512 else [(i * TT + k, 512) for k in range(0, TT, 512)]
            if i == nt - 1:
                sl.append((SEQ, 16))
            s.tiles.append(sl)
        s.TW = TT + 16
        s.GS = 512 if (s.DF % 512 == 0 and DSC % 512 == 0 and s.DDN % 512 == 0 and D % 512 == 0) else 128
        s.SW = min(512, TT)
        s.slices = [sl for tl in s.tiles for sl in tl]


FULL = Cfg()


class Region:
    __slots__ = ("name", "w", "r")

    def __init__(self, name):
        self.name = name
        self.w = None
        self.r = {}


class Sched:
    ENG = ("pe", "dve", "act", "pool", "sp")

    def __init__(self, nc):
        self.nc = nc
        self.ops = {e: [] for e in self.ENG}
        self.cnt = {e: 0 for e in self.ENG}
        self.waited = {e: {} for e in self.ENG}
        self.sems = {}
        self.tot = {}
        for e in self.ENG:
            self.sem("E_" + e)
        self.NPOOL = 64
        for i in range(self.NPOOL):
            self.sem("D%d" % i)
        self.lmap = {}
        self.free = ["D%d" % i for i in range(self.NPOOL)]

    def phys(self, lname):
        if lname.startswith("C_") or lname == "OUT":
            return self.sem(lname)
        if lname not in self.lmap:
            self.lmap[lname] = self.free.pop(0)
        return self.lmap[lname]

    def sem(self, name):
        if name not in self.sems:
            self.sems[name] = self.nc.alloc_semaphore(name)
            self.tot[name] = 0
        return name

    def _waits(self, eng, reads, writes):
        need = {}

        def add(ev):
            if ev is None:
                return
            s, v = ev
            if need.get(s, 0) < v:
                need[s] = v
        for R in reads:
            add(R.w)
        for R in writes:
            add(R.w)
            for s, v in R.r.items():
                add((s, v))
        out = []
        wd = self.waited[eng]
        own = "E_" + eng
        for s, v in need.items():
            if s == own and eng == "pe":
                continue
            if wd.get(s, 0) >= v:
                continue
            wd[s] = v
            out.append((s, v))
        return out

    def op(self, eng, fn, reads=(), writes=()):
        waits = self._waits(eng, reads, writes)
        own = "E_" + eng
        self.cnt[eng] += 1
        self.tot[own] = self.cnt[eng]
        ev = (own, self.cnt[eng])
        self.ops[eng].append((waits, fn, (own, 1)))
        for R in reads:
            if R.r.get(own, 0) < ev[1]:
                R.r[own] = ev[1]
        for R in writes:
            R.w = ev
            R.r = {}
        return ev

    def dma(self, q, out, in_, reads=(), writes=(), sem=None, slow=False):
        waits = self._waits(q, reads, writes)
        sem = self.phys(sem)
        self.tot[sem] += 16
        ev = (sem, self.tot[sem])
        if slow:
            self.ops[q].append((waits, lambda e, o=out, i=in_: e.dma_start(out=o, in_=i, allow_slow_non_contiguous=True), (sem, 16)))
        else:
            self.ops[q].append((waits, lambda e, o=out, i=in_: e.dma_start(out=o, in_=i), (sem, 16)))
        for R in reads:
            if R.r.get(sem, 0) < ev[1]:
                R.r[sem] = ev[1]
        for R in writes:
            R.w = ev
            R.r = {}
        return ev

    def barrier(self):
        for e in self.ENG:
            wl = []
            for s, v in self.tot.items():
                if v > 0 and self.waited[e].get(s, 0) < v and s != "E_" + e:
                    self.waited[e][s] = v
                    wl.append((s, v))
            if wl:
                self.ops[e].append((wl, None, None))
        self.lmap = {}
        self.free = ["D%d" % i for i in range(self.NPOOL)]

    def emit(self):
        nc = self.nc
        sems = self.sems

        def replay(engine, lst):
            for waits, fn, inc in lst:
                for s, v in waits:
                    engine.wait_ge(sems[s], v)
                if fn is not None:
                    ins = fn(engine)
                    ins.then_inc(sems[inc[0]], inc[1])

        with nc.Block() as block:
            @block.tensor
            def _(e):
                replay(e, self.ops["pe"])

            @block.vector
            def _(e):
                replay(e, self.ops["dve"])

            @block.scalar
            def _(e):
                replay(e, self.ops["act"])

            @block.gpsimd
            def _(e):
                replay(e, self.ops["pool"])

            @block.sync
            def _(e):
                replay(e, self.ops["sp"])


class RR:
    def __init__(self, items):
        self.items = items
        self.i = 0

    def next(self):
        x = self.items[self.i % len(self.items)]
        self.i += 1
        return x


class Tile:
    def __init__(self, name, ap):
        self.ap = ap
        self.R = Region(name)

    def __getitem__(self, k):
        return self.ap[k]


class Builder:
    def __init__(self, cfg, stop_after=None):
        self.c = cfg
        self.stop_after = stop_after
        self.nc = bass.Bass("TRN2", target_bir_lowering=False)
        self.S = Sched(self.nc)
        self.evi = 0
        self.uid = 0
        self.ARENA = 52800
        self.arena = self.nc.alloc_sbuf_tensor("arena", [128, self.ARENA], F32)
        self.aptr = 0
        self.psum_all = self.nc.alloc_psum_tensor("psum_all", [128, 4096], F32)

    def din(self, name, shape, dt=F32):
        return self.nc.dram_tensor(name, list(shape), dt, kind="ExternalInput").ap()

    def dscr(self, name, shape, dt=F32):
        return self.nc.dram_tensor(name, list(shape), dt, kind="Internal").ap()

    def tile(self, name, shape, dt=F32, at=None):
        n = 1
        for d in shape[1:]:
            n *= d
        cols = n if dt == F32 else (n + 1) // 2
        cols = (cols + 7) // 8 * 8
        if at is not None:
            a = at
            self.last_end = a + cols
        else:
            a = self.aptr
            self.aptr += cols
            assert self.aptr <= self.ARENA, ("SBUF arena overflow", name, self.aptr)
        ap = self.arena[:, a:a + cols]
        if dt != F32:
            ap = ap.bitcast(dt)
        ap = ap[:, 0:n]
        if len(shape) == 3:
            ap = ap.rearrange("p (a b) -> p a b", b=shape[2])
        elif len(shape) == 4:
            ap = ap.rearrange("p (a b c) -> p a b c", b=shape[2], c=shape[3])
        elif len(shape) == 5:
            ap = ap.rearrange("p (a b c d) -> p a b c d", b=shape[2], c=shape[3], d=shape[4])
        self.uid += 1
        return Tile("%s_%d" % (name, self.uid), ap)

    def bank(self, i, nb=1):
        return Tile("pb%d_%d" % (i, nb), self.psum_all[:, i * 512:(i + nb) * 512])

    def load(self, tl, out, in_, q="sp", extra_reads=(), sem=None):
        self.S.dma(q, out, in_, reads=extra_reads, writes=(tl.R,), sem=sem or ("L_" + tl.R.name))

    def store(self, tl, out, in_, q="act", sem=None):
        self.S.dma(q, out, in_, reads=(tl.R,), writes=(), sem=sem or ("L_" + tl.R.name))

    def mm(self, ps, out, lhsT, rhs, start, stop, reads):
        self.S.op("pe", lambda e: e.matmul(out, lhsT=lhsT, rhs=rhs, start=start, stop=stop),
                  reads=reads, writes=(ps.R,))

    def act(self, out, in_, func, reads, writes, bias=None, scale=None):
        kw = {}
        if bias is not None:
            kw["bias"] = bias
        if scale is not None:
            kw["scale"] = scale
        self.S.op("act", lambda e: e.activation(out=out, in_=in_, func=func, **kw), reads=reads, writes=writes)

    def tt(self, out, in0, in1, op, reads, writes, eng="dve"):
        self.S.op(eng, lambda e: e.tensor_tensor(out=out, in0=in0, in1=in1, op=op), reads=reads, writes=writes)

    def stt(self, out, in0, scalar, in1, op0, op1, reads, writes, eng="dve"):
        self.S.op(eng, lambda e: e.scalar_tensor_tensor(out=out, in0=in0, scalar=scalar, in1=in1, op0=op0, op1=op1),
                  reads=reads, writes=writes)

    def ts(self, out, in0, s1, s2, op0, op1, reads, writes, eng="dve"):
        if s2 is None:
            self.S.op(eng, lambda e: e.tensor_scalar(out=out, in0=in0, scalar1=s1, scalar2=None, op0=op0),
                      reads=reads, writes=writes)
        else:
            self.S.op(eng, lambda e: e.tensor_scalar(out=out, in0=in0, scalar1=s1, scalar2=s2, op0=op0, op1=op1),
                      reads=reads, writes=writes)

    def copy(self, out, in_, reads, writes, eng=None):
        if eng is None:
            self.evi += 1
            eng = "act" if self.evi % 2 else "dve"
        if eng == "act":
            self.S.op("act", lambda e: e.activation(out=out, in_=in_, func=AF.Copy), reads=reads, writes=writes)
        else:
            self.S.op(eng, lambda e: e.tensor_copy(out=out, in_=in_), reads=reads, writes=writes)

    def recip(self, out, in_, reads, writes):
        self.S.op("dve", lambda e: e.reciprocal(out=out, in_=in_), reads=reads, writes=writes)

    def memset(self, tl, ap, val, eng="dve"):
        self.S.op(eng, lambda e: e.memset(ap, val), reads=(), writes=(tl.R,))

    def pcol(self, tau):
        return 1 + (tau + 16 if tau < self.c.SEQ else tau - self.c.SEQ)

    def ppos(self, tau):
        return 64 + tau if tau < self.c.SEQ else 48 + (tau - self.c.SEQ)

    def build(self):
        c = self.c
        nc = self.nc
        S = self.S
        D, KD, KF, L, TW, H = c.D, c.KD, c.KF, c.L, c.TW, c.H
        x_in = self.din("x", [c.SEQ, D])
        meta_in = self.din("meta", [16, D])
        gains_in = self.din("gains", [128, c.DEPTH * 6 * KD])
        Wf32 = {}
        wshapes = {"ffn1_wi": (D, 2 * c.DFF), "ffn1_wo": (c.DFF, D), "ffn2_wi": (D, 2 * c.DFF),
                   "ffn2_wo": (c.DFF, D), "w_in": (D, c.DIN), "w_fourier": (c.DF, D),
                   "w_sconv_out": (c.DSC, D), "w_dn_out": (c.DDN, D), "w_out": (D, D)}
        for k, shp in wshapes.items():
            Wf32[k] = self.din(k, [c.DEPTH, shp[0], shp[1]])
        sconv_in = self.din("sconv_w", [128, c.DEPTH * 3 * (c.DSC // 128)])
        dnconv_in = self.din("dn_conv_w", [128, c.DEPTH * 3 * 3 * H])
        dnA_in = self.din("dn_A_log", [128, c.DEPTH * 2 * H])
        dndt_in = self.din("dn_dt_bias", [128, c.DEPTH * 2 * H])
        dnnorm_in = self.din("dn_norm", [128, c.DEPTH * H * 128])
        id32_in = self.din("c_id32", [128, 128])
        idbf_in = self.din("c_idbf", [128, 128], BF16)
        cs64_in = self.din("c_cs64", [128, 256], BF16)
        cls_in = self.din("c_cls", [len(c.slices), 128, c.NB, 2, c.SW], BF16)
        dmask_in = self.din("c_dmask", [128, 2 * 2 * 64 + 2 * 2 * H * 64])
        dsmall_in = self.din("c_dsmall", [64, 2 * 64 * 4 + 128])
        out_d = self.nc.dram_tensor("out", [c.SEQ, D], F32, kind="ExternalOutput").ap()

        Wb = {k: self.dscr("b_" + k, [c.DEPTH, wshapes[k][0], wshapes[k][1]], BF16) for k in ("w_fourier", "w_sconv_out", "w_dn_out")}
        h_scr = self.dscr("h_scr", [D, L])
        self.h_scr = h_scr

        self.gains = self.tile("gains", [128, c.DEPTH * 6 * KD])
        self.load(self.gains, self.gains[:], gains_in)
        self.id32 = self.tile("id32", [128, 128])
        self.load(self.id32, self.id32[:], id32_in)
        self.idbf = self.tile("idbf", [128, 128], BF16)
        self.load(self.idbf, self.idbf[:], idbf_in)
        self.onesb = self.tile("onesb", [128, 128], BF16)
        self.memset(self.onesb, self.onesb[:], 1.0)
        self.epsb = self.tile("epsb", [128, 1])
        self.memset(self.epsb, self.epsb[:], EPS)

        self.banks = RR([self.bank(i) for i in range(8)])

        self.WR = {}
        GS = c.GS
        self.segs = {"ffn1_wi": [(0, c.DFF, GS), (c.DFF, c.DFF, GS)], "ffn2_wi": [(0, c.DFF, GS), (c.DFF, c.DFF, GS)],
                     "w_in": [(0, c.o_b, GS), (c.o_g, 3 * D, GS)], "w_out": [(0, D, GS)],
                     "ffn1_wo": [(0, D, 128)], "ffn2_wo": [(0, D, 128)]}
        self.Wg = {}
        for k, sg in self.segs.items():
            for si, (s0, sn, gs) in enumerate(sg):
                ng = (sn + gs - 1) // gs
                self.Wg[(k, si)] = self.dscr("g_%s_%d" % (k, si), [c.DEPTH, 128, ng, wshapes[k][0] // 128, gs], BF16)
        self.wab_scr = self.dscr("wab_scr", [c.DEPTH, D, 4 * H], BF16)
        for l in range(c.DEPTH):
            for k in ["ffn1_wi", "ffn1_wo", "w_in", "w_fourier", "w_sconv_out", "w_dn_out", "w_out", "ffn2_wi", "ffn2_wo"]:
                R = Region("W_%s_%d" % (k, l))
                self.WR[(k, l)] = R
                sem = "C_%s_%d" % (k, l)
                if k in self.segs:
                    src = Wf32[k][l].rearrange("(k p) n -> p k n", p=128)
                    sg = self.segs[k]
                    ngmax = max((sn + gs - 1) // gs for (_, sn, gs) in sg)
                    for g in range(ngmax):
                        for si, (s0, sn, gs) in enumerate(sg):
                            if g * gs >= sn:
                                continue
                            gw = min(gs, sn - g * gs)
                            if k == "ffn1_wi" and l == 0:
                                gsem = "C_%s_%d_g%d" % (k, l, g)
                                S.dma("pool", self.Wg[(k, si)][l, :, g, :, 0:gw], src[:, :, s0 + g * gs:s0 + g * gs + gw],
                                      reads=(), writes=(), sem=gsem)
                                Rg = self.WR.setdefault((k, l, g), Region("Wg_%s_%d_%d" % (k, l, g)))
                                Rg.w = (S.phys(gsem), S.tot[S.phys(gsem)])
                                continue
                            S.dma("pool", self.Wg[(k, si)][l, :, g, :, 0:gw], src[:, :, s0 + g * gs:s0 + g * gs + gw],
                                  reads=(), writes=(), sem=sem)
                    if k == "w_in":
                        S.dma("pool", self.wab_scr[l], Wf32[k][l, :, c.o_b:c.o_b + 4 * H], reads=(), writes=(), sem=sem, slow=True)
                else:
                    S.dma("pool", Wb[k][l], Wf32[k][l], reads=(), writes=(), sem=sem)
                if not (k == "ffn1_wi" and l == 0):
                    R.w = (S.phys(sem), S.tot[S.phys(sem)])
        self.Wb = Wb

        self.onef = self.tile("onef", [128, 1])
        self.memset(self.onef, self.onef[:], 1.0)
        self.zero = self.tile("zero", [128, 64])
        self.memset(self.zero, self.zero[:], 0.0)
        self.cs64 = self.tile("cs64", [128, 256], BF16)
        self.load(self.cs64, self.cs64[:], cs64_in)
        self.sconv = self.tile("sconv", [128, c.DEPTH * 3 * (c.DSC // 128)])
        self.load(self.sconv, self.sconv[:], sconv_in)
        self.dnconv = self.tile("dnconv", [128, c.DEPTH * 3 * 3 * H])
        self.load(self.dnconv, self.dnconv[:], dnconv_in)
        self.dtb = self.tile("dtb", [128, c.DEPTH * 2 * H])
        self.load(self.dtb, self.dtb[:], dndt_in)
        self.negA = self.tile("negA", [128, c.DEPTH * 2 * H])
        self.load(self.negA, self.negA[:], dnA_in)
        self.act(self.negA[:], self.negA[:], AF.Exp, reads=(self.negA.R,), writes=(self.negA.R,))
        self.ts(self.negA[:], self.negA[:], -1.0, None, ALU.mult, None, reads=(self.negA.R,), writes=(self.negA.R,))
        self.dnnorm = self.tile("dnnorm", [128, c.DEPTH * H * 128])
        self.load(self.dnnorm, self.dnnorm[:], dnnorm_in)
        self.cls_in = cls_in
        self.dmask_in = dmask_in
        self.dsmall_in = dsmall_in

        self.UCS = self.dscr("UCS_scr", [c.NB * 128, 2 * c.DF], BF16)
        self.CH = self.dscr("CH_scr", [c.DSC, L + 2])
        self.BB = self.dscr("BB_scr", [c.DSC, L])
        self.QKV = self.dscr("QKV_scr", [3 * c.DDN, L + 2])
        self.BG = self.dscr("BG_scr", [64, c.NCH, 4 * H])
        self.YF = self.dscr("YF_scr", [c.DF, L], BF16)
        self.OF = self.dscr("OF_scr", [c.LP, c.DDN])
        self.OB = self.dscr("OB_scr", [c.LP, c.DDN])
        for scr, nk in ((self.CH, c.DSC // 128), (self.QKV, 3 * H)):
            v = scr.rearrange("(k p) t -> p k t", p=128)
            for col in (0, L + 1):
                S.dma("sp", v[:, :, col:col + 1], self.zero[:, 0:nk].unsqueeze(2), reads=(self.zero.R,), writes=(), sem="Z", slow=True)
        S.dma("sp", self.BG[0:48, 0, :], self.zero[0:48, 0:4 * H], reads=(self.zero.R,), writes=(), sem="Z")
        if self.stop_after == "nodn":
            for scr in (self.OF, self.OB):
                for r0 in range(0, c.LP, 64):
                    S.dma("sp", scr[r0:r0 + 64, :].rearrange("r (a b) -> r a b", b=64),
                          self.zero[0:64, :].unsqueeze(1).to_broadcast([64, c.DDN // 64, 64]),
                          reads=(self.zero.R,), writes=(), sem="Z")

        self.base_ptr = self.aptr
        self.phase_p0(x_in, meta_in)
        S.barrier()
        for l in range(c.DEPTH):
            if self.stop_after == "ffn":
                self.aptr = self.base_ptr
                self.phase_ffn_only(l)
                S.barrier()
                continue
            self.aptr = self.base_ptr
            self.phase_A(l)
            S.barrier()
            self.aptr = self.base_ptr
            self.phase_B(l)
            S.barrier()
            if self.stop_after != "nodn":
                self.aptr = self.base_ptr
                self.phase_C(l)
                S.barrier()
            self.aptr = self.base_ptr
            self.phase_D(l)
            S.barrier()
        self.aptr = self.base_ptr
        self.phase_pf(out_d)
        S.barrier()
        S.emit()
        return nc

    def alloc_token_tiles(self):
        c = self.c
        KD, KF, TW, D = c.KD, c.KF, c.TW, c.D
        self.hbuf = [self.tile("h0", [128, KD, TW]), self.tile("h1", [128, KD, TW])]
        self.h = self.hbuf[0]
        self.hidx = 0
        self.hn = self.tile("hn", [128, KD, TW], BF16)
        self.rstd = self.tile("rstd", [128, TW])
        self.y = self.tile("y", [128, KD, TW])
        DFc, SCc, H = c.DF // 128, c.DSC // 128, c.H
        gcols = (max(KF, KD) * TW + 1) // 2
        ovcols = ((DFc + SCc + H) * TW + 1) // 2 + SCc * TW + 32
        self.g_off = self.aptr
        self.g = self.tile("g", [128, max(KF, KD), TW], BF16)
        self.aptr = self.g_off + max(gcols, ovcols) + 8
        assert self.aptr <= self.ARENA, ("SBUF arena overflow", "g", self.aptr)
        self.sq = Tile("sq", self.g.ap)
        self.sq.R = self.g.R
        self.woslots = RR([self.tile("wo%d" % i, [128, KF, 128], BF16) for i in range(3)])
        self.wslots = RR([self.tile("ws%d" % i, [128, KD, 512], BF16) for i in range(4)])
        self.sa = RR([self.tile("sa%d" % i, [128, 512]) for i in range(2)])
        self.tmpf = RR([self.tile("tmpf%d" % i, [128, 512]) for i in range(2)])

    def phase_p0(self, x_in, meta_in):
        c = self.c
        KD = c.KD
        xin = RR([self.tile("xin%d" % i, [128, c.D]) for i in range(2)])
        hst = RR([self.tile("hst%d" % i, [128, KD, 128]) for i in range(2)])
        blocks = [(i * 128, 128) for i in range(c.SEQ // 128)] + [(c.SEQ, 16)]
        for (t0, n) in blocks:
            xt = xin.next()
            src = x_in[t0:t0 + n, :] if t0 < c.SEQ else meta_in
            self.load(xt, xt[0:n, :], src)
            st = hst.next()
            for k0 in range(0, KD, 4):
                ps = self.banks.next()
                kn = min(4, KD - k0)
                for kk in range(kn):
                    kc = k0 + kk
                    self.S.op("pe", lambda e, ps=ps, kk=kk, xt=xt, kc=kc, n=n: e.transpose(
                        out=ps[:, kk * 128:kk * 128 + n], in_=xt[0:n, kc * 128:(kc + 1) * 128], identity=self.id32[0:n, 0:n]),
                        reads=(xt.R, self.id32.R), writes=(ps.R,))
                self.copy(st[:, k0:k0 + kn, 0:n], ps[:, 0:kn * 128].rearrange("p (k t) -> p k t", t=128)[:, :, 0:n],
                          reads=(ps.R,), writes=(st.R,))
            self.store(st, self.h_scr.rearrange("(k p) t -> p k t", p=128)[:, :, t0:t0 + n], st[:, :, 0:n])

    def phase_pf(self, out_d):
        c = self.c
        KD = c.KD
        hin = RR([self.tile("hin%d" % i, [128, KD, 128]) for i in range(2)])
        ost = RR([self.tile("ost%d" % i, [128, c.D]) for i in range(2)])
        for b in range(c.SEQ // 128):
            t0 = b * 128
            ht = hin.next()
            self.load(ht, ht[:], self.h_scr.rearrange("(k p) t -> p k t", p=128)[:, :, t0:t0 + 128])
            ot = ost.next()
            for k0 in range(0, KD, 4):
                ps = self.banks.next()
                kn = min(4, KD - k0)
                for kk in range(kn):
                    self.S.op("pe", lambda e, ps=ps, kk=kk, ht=ht, kc=k0 + kk: e.transpose(
                        out=ps[:, kk * 128:(kk + 1) * 128], in_=ht[:, kc, :], identity=self.id32[:]),
                        reads=(ht.R, self.id32.R), writes=(ps.R,))
                self.copy(ot[:, k0 * 128:(k0 + kn) * 128], ps[:, 0:kn * 128], reads=(ps.R,), writes=(ot.R,))
            self.store(ot, out_d[t0:t0 + 128, :], ot[:], sem="OUT")

    def gcol(self, l, s, kc):
        c = self.c
        i = (l * 6 + s) * c.KD + kc
        return self.gains[:, i:i + 1]

    def load_h(self, tile_slices):
        self.hidx += 1
        self.h = self.hbuf[self.hidx % 2]
        off = 0
        hv = self.h_scr.rearrange("(k p) t -> p k t", p=128)
        for (t0, n) in tile_slices:
            self.load(self.h, self.h[:, :, off:off + n], hv[:, :, t0:t0 + n])
            off += n

    def store_h(self, tile_slices):
        off = 0
        hv = self.h_scr.rearrange("(k p) t -> p k t", p=128)
        for (t0, n) in tile_slices:
            self.store(self.h, hv[:, :, t0:t0 + n], self.h[:, :, off:off + n])
            off += n

    def slices_off(self, tile_slices):
        out = []
        off = 0
        for (t0, n) in tile_slices:
            out.append((off, n, t0))
            off += n
        return out, off

    def stats_rstd(self, src, nk, W, slo, scale):
        self.act(self.sq[:, 0:nk, 0:W], src[:, 0:nk, 0:W], AF.Square, reads=(src.R,), writes=(self.sq.R,))
        for (off, n, _) in slo:
            ps = self.banks.next()
            for kc in range(nk):
                self.mm(ps, ps[:, 0:n], self.onesb[:], self.sq[:, kc, off:off + n], kc == 0, kc == nk - 1,
                        reads=(self.onesb.R, self.sq.R))
            self.act(self.rstd[:, off:off + n], ps[:, 0:n], AF.Sqrt, reads=(ps.R, self.epsb.R), writes=(self.rstd.R,),
                     bias=self.epsb[:], scale=scale)
        self.recip(self.rstd[:, 0:W], self.rstd[:, 0:W], reads=(self.rstd.R,), writes=(self.rstd.R,))

    def prenorm(self, l, s, W, slo):
        c = self.c
        self.stats_rstd(self.h, c.KD, W, slo, 1.0 / c.D)
        for kc in range(c.KD):
            self.stt(self.hn[:, kc, 0:W], self.h[:, kc, 0:W], self.gcol(l, s, kc), self.rstd[:, 0:W], ALU.mult, ALU.mult,
                     reads=(self.h.R, self.gains.R, self.rstd.R), writes=(self.hn.R,))

    def postnorm_add(self, l, s, W, slo, scale):
        c = self.c
        self.stats_rstd(self.y, c.KD, W, slo, 1.0 / c.D)
        for kc in range(c.KD):
            tmp = self.y
            self.stt(self.y[:, kc, 0:W], self.y[:, kc, 0:W], self.gcol(l, s, kc), self.rstd[:, 0:W], ALU.mult, ALU.mult,
                     reads=(self.y.R, self.gains.R, self.rstd.R), writes=(self.y.R,))
            self.stt(self.h[:, kc, 0:W], self.y[:, kc, 0:W], float(scale), self.h[:, kc, 0:W], ALU.mult, ALU.add,
                     reads=(self.y.R, self.h.R), writes=(self.h.R,))

    def wload(self, key, l, rows_kd, c0, cn):
        ws = self.wslots.next()
        er = [self.WR[(key, l)]]
        for si, (s0, sn, gs) in enumerate(self.segs[key]):
            if s0 <= c0 < s0 + sn and (key, l, (c0 - s0) // gs) in self.WR:
                er.append(self.WR[(key, l, (c0 - s0) // gs)])
        self.load(ws, ws[:, 0:rows_kd, 0:cn], self.wsrc(key, l, c0, cn), extra_reads=tuple(er))
        return ws

    def wsrc(self, key, l, c0, cn):
        for si, (s0, sn, gs) in enumerate(self.segs[key]):
            if s0 <= c0 < s0 + sn:
                break
        assert (c0 - s0) % gs == 0 and cn <= gs, (key, c0, cn, gs)
        return self.Wg[(key, si)][l, :, (c0 - s0) // gs, :, 0:cn]

    def ffn(self, l, which, slo, W):
        c = self.c
        KD, KF, DFF = c.KD, c.KF, c.DFF
        wi, wo = ("ffn1_wi", "ffn1_wo") if which == 1 else ("ffn2_wi", "ffn2_wo")
        s_pre, s_post = (0, 1) if which == 1 else (4, 5)
        self.prenorm(l, s_pre, W, slo)
        G = c.GS // 128
        for g0 in range(0, KF, G):
            gn = min(G, KF - g0)
            wa = self.wload(wi, l, KD, g0 * 128, gn * 128)
            wb = self.wload(wi, l, KD, DFF + g0 * 128, gn * 128)
            for j in range(gn):
                oc = g0 + j
                for (off, n, _) in slo:
                    pa = self.banks.next()
                    pb = self.banks.next()
                    for kc in range(KD):
                        self.mm(pa, pa[:, 0:n], wa[:, kc, j * 128:(j + 1) * 128], self.hn[:, kc, off:off + n],
                                kc == 0, kc == KD - 1, reads=(wa.R, self.hn.R))
                    for kc in range(KD):
                        self.mm(pb, pb[:, 0:n], wb[:, kc, j * 128:(j + 1) * 128], self.hn[:, kc, off:off + n],
                                kc == 0, kc == KD - 1, reads=(wb.R, self.hn.R))
                    sa = self.sa.next()
                    self.act(sa[:, 0:n], pa[:, 0:n], AF.Silu, reads=(pa.R,), writes=(sa.R,))
                    self.tt(self.g[:, oc, off:off + n], sa[:, 0:n], pb[:, 0:n], ALU.mult, reads=(sa.R, pb.R), writes=(self.g.R,))
        for oc0 in range(0, KD, 1):
            ocn = 1
            wt = self.woslots.next()
            self.load(wt, wt[:, :, 0:ocn * 128], self.wsrc(wo, l, oc0 * 128, ocn * 128), extra_reads=(self.WR[(wo, l)],))
            for q in range(ocn):
                oc = oc0 + q
                for (off, n, _) in slo:
                    ps = self.banks.next()
                    for kc in range(KF):
                        self.mm(ps, ps[:, 0:n], wt[:, kc, q * 128:(q + 1) * 128], self.g[:, kc, off:off + n],
                                kc == 0, kc == KF - 1, reads=(wt.R, self.g.R))
                    self.copy(self.y[:, oc, off:off + n], ps[:, 0:n], reads=(ps.R,), writes=(self.y.R,))
        self.postnorm_add(l, s_post, W, slo, 0.5)

    def load_wo(self, l, which):
        return

    def phase_ffn_only(self, l):
        self.alloc_token_tiles()
        for which in (1, 2):
            self.load_wo(l, which)
            for tl in self.c.tiles:
                slo, W = self.slices_off(tl)
                self.load_h(tl)
                self.ffn(l, which, slo, W)
                self.store_h(tl)
            self.S.barrier()


    def linear(self, key, l, c0, ncols, kd, rhs, slo, evac):
        noc = ncols // 128
        G = self.c.GS // 128
        for g0 in range(0, noc, G):
            gn = min(G, noc - g0)
            ws = self.wload(key, l, kd, c0 + g0 * 128, gn * 128)
            for j in range(gn):
                for (off, n, _) in slo:
                    ps = self.banks.next()
                    for kc in range(kd):
                        self.mm(ps, ps[:, 0:n], ws[:, kc, j * 128:(j + 1) * 128], rhs[:, kc, off:off + n],
                                kc == 0, kc == kd - 1, reads=(ws.R, rhs.R))
                    evac(g0 + j, off, n, ps)

    def phase_A(self, l):
        c = self.c
        S = self.S
        KD, H, DF, DSC, DDN, TW, L = c.KD, c.H, c.DF, c.DSC, c.DDN, c.TW, c.L
        DFc, SCc = DF // 128, DSC // 128
        self.alloc_token_tiles()
        uf = self.tile("uf", [128, DFc, TW], BF16)
        ucs_st = RR([self.tile("ucs_st%d" % i, [128, 2, DF], BF16) for i in range(2)])
        st = RR([self.tile("st%d" % i, [128, 4, TW]) for i in range(2)])
        wab = self.tile("wab", [128, KD, 4 * H], BF16)
        bgst = RR([self.tile("bgst%d" % i, [128, 4 * H]) for i in range(2)])
        xa = self.tile("xa", [128, 2 * H])
        ta = self.tile("ta", [128, 2 * H])
        self.load(wab, wab[:], self.wab_scr[l].rearrange("(k p) n -> p k n", p=128), extra_reads=(self.WR[("w_in", l)],))
        self.load_wo(l, 1)
        chv = self.CH.rearrange("(k p) t -> p k t", p=128)
        bbv = self.BB.rearrange("(k p) t -> p k t", p=128)
        qkvv = self.QKV.rearrange("(k p) t -> p k t", p=128)
        for tl in c.tiles:
            slo, W = self.slices_off(tl)
            self.load_h(tl)
            self.ffn(l, 1, slo, W)
            self.store_h(tl)
            self.prenorm(l, 2, W, slo)
            hn = self.hn
            self.linear("w_in", l, c.o_f, DF, KD, hn, slo,
                        lambda oc, off, n, ps: self.copy(uf[:, oc, off:off + n], ps[:, 0:n], reads=(ps.R,), writes=(uf.R,)))
            for (off, n, t0) in slo:
                for b0 in range(0, n, 128):
                    bn = min(128, n - b0)
                    us = ucs_st.next()
                    for fp in range(0, DFc, 2):
                        npair = min(2, DFc - fp)
                        ps = self.banks.next()
                        for q in range(npair):
                            self.mm(ps, ps[0:bn, q * 256:(q + 1) * 256], uf[:, fp + q, off + b0:off + b0 + bn], self.cs64[:],
                                    True, True, reads=(uf.R, self.cs64.R))
                        self.copy(us[0:bn, :, fp * 128:(fp + npair) * 128].rearrange("p c (f k) -> p c f k", k=128),
                                  ps[0:bn, 0:npair * 256].rearrange("p (f c k) -> p c f k", c=2, k=128),
                                  reads=(ps.R,), writes=(us.R,))
                    self.store(us, self.UCS[t0 + b0:t0 + b0 + bn, :], us[0:bn, :, :].rearrange("p c f -> p (c f)"))
            stb = st.next()
            self.linear("w_in", l, c.o_sc, DSC, KD, hn, slo,
                        lambda oc, off, n, ps: self.copy(stb[:, oc, off:off + n], ps[:, 0:n], reads=(ps.R,), writes=(stb.R,)))
            for (off, n, t0) in slo:
                self.store(stb, bbv[:, :, t0:t0 + n], stb[:, 0:SCc, off:off + n])
            stc = st.next()
            wc = self.wload("w_in", l, KD, c.o_sc + DSC, DSC)
            wh = self.wload("w_in", l, KD, c.o_sc + 2 * DSC, DSC)
            for j in range(SCc):
                for (off, n, _) in slo:
                    pc = self.banks.next()
                    ph = self.banks.next()
                    for kc in range(KD):
                        self.mm(pc, pc[:, 0:n], wc[:, kc, j * 128:(j + 1) * 128], hn[:, kc, off:off + n], kc == 0, kc == KD - 1,
                                reads=(wc.R, hn.R))
                    for kc in range(KD):
                        self.mm(ph, ph[:, 0:n], wh[:, kc, j * 128:(j + 1) * 128], hn[:, kc, off:off + n], kc == 0, kc == KD - 1,
                                reads=(wh.R, hn.R))
                    tf = self.tmpf.next()
                    self.copy(tf[:, 0:n], pc[:, 0:n], reads=(pc.R,), writes=(tf.R,), eng="act")
                    self.tt(stc[:, j, off:off + n], tf[:, 0:n], ph[:, 0:n], ALU.mult, reads=(tf.R, ph.R), writes=(stc.R,))
            for (off, n, t0) in slo:
                pc0 = self.pcol(t0)
                self.store(stc, chv[:, :, pc0:pc0 + n], stc[:, 0:SCc, off:off + n])
            for g0 in range(0, 3 * H, 4):
                gn = min(4, 3 * H - g0)
                stq = st.next()
                self.linear("w_in", l, c.o_qkv + g0 * 128, gn * 128, KD, hn, slo,
                            lambda oc, off, n, ps, stq=stq: self.copy(stq[:, oc, off:off + n], ps[:, 0:n], reads=(ps.R,), writes=(stq.R,)))
                for (off, n, t0) in slo:
                    pc0 = self.pcol(t0)
                    self.store(stq, qkvv[:, g0:g0 + gn, pc0:pc0 + n], stq[:, 0:gn, off:off + n])
            H2 = 2 * H
            for (off, n, t0) in slo:
                for b0 in range(0, n, 128):
                    bn = min(128, n - b0)
                    ps = self.banks.next()
                    for kc in range(KD):
                        self.mm(ps, ps[0:bn, 0:4 * H], hn[:, kc, off + b0:off + b0 + bn], wab[:, kc, :], kc == 0, kc == KD - 1,
                                reads=(hn.R, wab.R))
                    bg = bgst.next()
                    self.act(bg[0:bn, 0:H2], ps[0:bn, 0:H2], AF.Sigmoid, reads=(ps.R,), writes=(bg.R,))
                    self.tt(xa[0:bn, :], ps[0:bn, H2:2 * H2], self.dtb[0:bn, l * H2:(l + 1) * H2], ALU.add,
                            reads=(ps.R, self.dtb.R), writes=(xa.R,))
                    self.ts(ta[0:bn, :], xa[0:bn, :], 30.0, None, ALU.min, None, reads=(xa.R,), writes=(ta.R,))
                    self.tt(xa[0:bn, :], xa[0:bn, :], ta[0:bn, :], ALU.subtract, reads=(xa.R, ta.R), writes=(xa.R,))
                    self.act(ta[0:bn, :], ta[0:bn, :], AF.Exp, reads=(ta.R,), writes=(ta.R,))
                    self.act(ta[0:bn, :], ta[0:bn, :], AF.Ln, reads=(ta.R, self.onef.R), writes=(ta.R,), bias=self.onef[0:bn, :])
                    self.tt(xa[0:bn, :], xa[0:bn, :], ta[0:bn, :], ALU.add, reads=(xa.R, ta.R), writes=(xa.R,))
                    self.tt(bg[0:bn, H2:2 * H2], xa[0:bn, :], self.negA[0:bn, l * H2:(l + 1) * H2], ALU.mult,
                            reads=(xa.R, self.negA.R), writes=(bg.R,))
                    tau0 = t0 + b0
                    if tau0 >= c.SEQ:
                        self.store(bg, self.BG[48:64, 0, :], bg[0:16, :])
                    else:
                        ch0 = 1 + tau0 // 64
                        self.store(bg, self.BG[:, ch0, :], bg[0:64, :])
                        self.store(bg, self.BG[:, ch0 + 1, :], bg[64:128, :])

    def phase_B(self, l):
        c = self.c
        DF, NB, L = c.DF, c.NB, c.L
        DFc = DF // 128
        ucs = self.tile("ucs", [128, NB, 2, DF], BF16)
        uv = self.UCS.rearrange("(b p) c -> p b c", p=128)
        NBF = L // 128
        for b0 in range(0, NBF, 8):
            bn = min(8, NBF - b0)
            self.load(ucs, ucs[:, b0:b0 + bn, :, :].rearrange("p b c f -> p b (c f)"), uv[:, b0:b0 + bn, :])
        if NBF < NB:
            rn = L - NBF * 128
            self.load(ucs, ucs[0:rn, NBF, :, :].rearrange("p c f -> p (c f)"), self.UCS[NBF * 128:NBF * 128 + rn, :])
        GB = 4
        dft = RR([self.tile("dft%d" % i, [128, GB, 2, c.SW], BF16) for i in range(6)])
        sidx = 0
        yst = RR([self.tile("yst%d" % i, [128, DFc, 512], BF16) for i in range(2)])
        yfv = self.YF.rearrange("(k p) t -> p k t", p=128)
        for tl in c.tiles:
            for (t0, n) in tl:
                pss = [self.banks.next() for _ in range(DFc)]
                for i0 in range(0, NB, GB):
                    ni = min(GB, NB - i0)
                    dt_ = dft.next()
                    self.load(dt_, dt_[:, 0:ni, :, :], self.cls_in[sidx, :, i0:i0 + ni, :, :])
                    for fc in range(DFc):
                        for i in range(ni):
                            blk = i0 + i
                            rn = min(128, L - blk * 128)
                            for cs in range(2):
                                first = (blk == 0 and cs == 0)
                                last = (blk == NB - 1 and cs == 1)
                                self.mm(pss[fc], pss[fc][:, 0:n], ucs[0:rn, blk, cs, fc * 128:(fc + 1) * 128], dt_[0:rn, i, cs, 0:n],
                                        first, last, reads=(ucs.R, dt_.R))
                sidx += 1
                ys = yst.next()
                for fc in range(DFc):
                    self.copy(ys[:, fc, 0:n], pss[fc][:, 0:n], reads=(pss[fc].R,), writes=(ys.R,))
                self.store(ys, yfv[:, :, t0:t0 + n], ys[:, :, 0:n])

    def phase_D(self, l):
        c = self.c
        KD, H, DF, DSC, DDN, TW, L, D = c.KD, c.H, c.DF, c.DSC, c.DDN, c.TW, c.L, c.D
        DFc, SCc = DF // 128, DSC // 128
        self.alloc_token_tiles()
        yf = self.tile("yf", [128, DFc, TW], BF16, at=self.g_off)
        scin = self.tile("scin", [128, SCc, TW], BF16, at=self.last_end)
        dnin = self.tile("dnin", [128, H, TW], BF16, at=self.last_end)
        bb = self.tile("bb", [128, SCc, TW], at=self.last_end)
        for t_ in (yf, scin, dnin, bb):
            t_.R = self.g.R
        chw = RR([self.tile("chw%d" % i, [128, SCc, 514]) for i in range(1)])
        wbr = [self.tile("wbr%d" % i, [128, kk, D], BF16) for i, kk in enumerate((DFc, SCc, H))]
        wz = self.tile("wz", [128, KD, DDN], BF16)
        oft = RR([self.tile("oft%d" % i, [128, DDN]) for i in range(2)])
        obt = RR([self.tile("obt%d" % i, [128, DDN]) for i in range(2)])
        zs = self.tile("zs", [128, DDN])
        dtm = self.tile("dtm", [128, DDN], BF16)
        ss = self.tile("ss", [128, H])
        junk = self.tile("junk", [128, 128])
        for i, key in enumerate(("w_fourier", "w_sconv_out", "w_dn_out")):
            self.load(wbr[i], wbr[i][:], self.Wb[key][l].rearrange("(k p) n -> p k n", p=128), extra_reads=(self.WR[(key, l)],))
        for z0 in range(0, DDN, c.GS):
            self.load(wz, wz[:, :, z0:z0 + c.GS], self.wsrc("w_in", l, c.o_z + z0, c.GS), extra_reads=(self.WR[("w_in", l)],))
        self.load_wo(l, 2)
        yfv = self.YF.rearrange("(k p) t -> p k t", p=128)
        chv = self.CH.rearrange("(k p) t -> p k t", p=128)
        bbv = self.BB.rearrange("(k p) t -> p k t", p=128)
        nsc = c.DEPTH * 3 * SCc

        def scw(tap, j):
            i = (l * 3 + tap) * SCc + j
            return self.sconv[:, i:i + 1]
        for tl in c.tiles:
            slo, W = self.slices_off(tl)
            self.load_h(tl)
            self.prenorm(l, 2, W, slo)
            hn = self.hn
            for (off, n, t0) in slo:
                self.load(yf, yf[:, :, off:off + n], yfv[:, :, t0:t0 + n])
                self.load(bb, bb[:, :, off:off + n], bbv[:, :, t0:t0 + n])
                cw = chw.next()
                pc0 = self.pcol(t0)
                self.load(cw, cw[:, :, 0:n + 2], chv[:, :, pc0 - 1:pc0 + n + 1])
                for j in range(SCc):
                    tf = self.tmpf.next()
                    self.ts(tf[:, 0:n], cw[:, j, 1:n + 1], scw(1, j), None, ALU.mult, None, reads=(cw.R, self.sconv.R), writes=(tf.R,))
                    self.stt(tf[:, 0:n], cw[:, j, 0:n], scw(0, j), tf[:, 0:n], ALU.mult, ALU.add,
                             reads=(cw.R, self.sconv.R, tf.R), writes=(tf.R,))
                    self.stt(tf[:, 0:n], cw[:, j, 2:n + 2], scw(2, j), tf[:, 0:n], ALU.mult, ALU.add,
                             reads=(cw.R, self.sconv.R, tf.R), writes=(tf.R,))
                    self.tt(scin[:, j, off:off + n], tf[:, 0:n], bb[:, j, off:off + n], ALU.mult, reads=(tf.R, bb.R), writes=(scin.R,))
                for b0 in range(0, n, 128):
                    bn = min(128, n - b0)
                    p0 = self.ppos(t0 + b0)
                    of_, ob_ = oft.next(), obt.next()
                    self.load(of_, of_[0:bn, :], self.OF[p0:p0 + bn, :])
                    self.load(ob_, ob_[0:bn, :], self.OB[p0:p0 + bn, :])
                    self.tt(of_[0:bn, :], of_[0:bn, :], ob_[0:bn, :], ALU.add, reads=(of_.R, ob_.R), writes=(of_.R,))
                    pz = self.banks.next()
                    for kc in range(KD):
                        self.mm(pz, pz[0:bn, 0:DDN], hn[:, kc, off + b0:off + b0 + bn], wz[:, kc, :], kc == 0, kc == KD - 1,
                                reads=(hn.R, wz.R))
                    self.act(zs[0:bn, :], pz[0:bn, 0:DDN], AF.Silu, reads=(pz.R,), writes=(zs.R,))
                    for hh in range(H):
                        self.S.op("act", lambda e, hh=hh, of_=of_, bn=bn: e.activation(
                            out=junk[0:bn, :], in_=of_[0:bn, hh * 128:(hh + 1) * 128], func=AF.Square, accum_out=ss[0:bn, hh:hh + 1]),
                            reads=(of_.R,), writes=(junk.R, ss.R))
                    self.act(ss[0:bn, :], ss[0:bn, :], AF.Sqrt, reads=(ss.R, self.epsb.R), writes=(ss.R,), bias=self.epsb[0:bn, :],
                             scale=1.0 / 128.0)
                    self.recip(ss[0:bn, :], ss[0:bn, :], reads=(ss.R,), writes=(ss.R,))
                    o3 = of_[0:bn, :].rearrange("p (h d) -> p h d", d=128)
                    self.tt(o3, o3, ss[0:bn, :].unsqueeze(2).to_broadcast([bn, H, 128]), ALU.mult, reads=(of_.R, ss.R), writes=(of_.R,))
                    self.tt(of_[0:bn, :], of_[0:bn, :], self.dnnorm[0:bn, l * DDN:(l + 1) * DDN], ALU.mult,
                            reads=(of_.R, self.dnnorm.R), writes=(of_.R,))
                    self.tt(dtm[0:bn, :], of_[0:bn, :], zs[0:bn, :], ALU.mult, reads=(of_.R, zs.R), writes=(dtm.R,))
                    pt = self.banks.next()
                    for hh in range(H):
                        self.mm(pt, pt[:, hh * 128:hh * 128 + bn], dtm[0:bn, hh * 128:(hh + 1) * 128], self.idbf[0:bn, 0:bn], True, True,
                                reads=(dtm.R, self.idbf.R))
                    self.copy(dnin[:, :, off + b0:off + b0 + bn], pt[:, 0:H * 128].rearrange("p (h t) -> p h t", t=128)[:, :, 0:bn],
                              reads=(pt.R,), writes=(dnin.R,))
            brin = (yf, scin, dnin)
            brk = (DFc, SCc, H)
            for br in range(3):
                G = c.GS // 128
                for g0 in range(0, KD, G):
                    gn = min(G, KD - g0)
                    ws = self.wload("w_in", l, KD, c.o_g + br * D + g0 * 128, gn * 128)
                    for j in range(gn):
                        oc = g0 + j
                        for (off, n, _) in slo:
                            pg = self.banks.next()
                            py = self.banks.next()
                            for kc in range(KD):
                                self.mm(pg, pg[:, 0:n], ws[:, kc, j * 128:(j + 1) * 128], hn[:, kc, off:off + n], kc == 0, kc == KD - 1,
                                        reads=(ws.R, hn.R))
                            for kb in range(brk[br]):
                                self.mm(py, py[:, 0:n], wbr[br][:, kb, oc * 128:(oc + 1) * 128], brin[br][:, kb, off:off + n],
                                        kb == 0, kb == brk[br] - 1, reads=(wbr[br].R, brin[br].R))
                            sg = self.sa.next()
                            self.act(sg[:, 0:n], pg[:, 0:n], AF.Sigmoid, reads=(pg.R,), writes=(sg.R,))
                            if br == 0:
                                self.tt(self.y[:, oc, off:off + n], sg[:, 0:n], py[:, 0:n], ALU.mult, reads=(sg.R, py.R), writes=(self.y.R,))
                            else:
                                self.tt(sg[:, 0:n], sg[:, 0:n], py[:, 0:n], ALU.mult, reads=(sg.R, py.R), writes=(sg.R,))
                                self.tt(self.y[:, oc, off:off + n], self.y[:, oc, off:off + n], sg[:, 0:n], ALU.add,
                                        reads=(self.y.R, sg.R), writes=(self.y.R,))
            self.copy(hn[:, :, 0:W], self.y[:, :, 0:W], reads=(self.y.R,), writes=(hn.R,))
            self.linear("w_out", l, 0, D, KD, hn, slo,
                        lambda oc, off, n, ps: self.copy(self.y[:, oc, off:off + n], ps[:, 0:n], reads=(ps.R,), writes=(self.y.R,)))
            self.postnorm_add(l, 3, W, slo, 1.0)
            self.ffn(l, 2, slo, W)
            self.store_h(tl)


    def phase_C(self, l):
        c = self.c
        S = self.S
        H, LP, NCH, L = c.H, c.LP, c.NCH, c.L
        H3, DH = 3 * H, 2 * H
        QT = self.tile("QT", [128, H, LP], BF16)
        KT = self.tile("KT", [128, H, LP], BF16)
        VT = self.tile("VT", [128, H, LP], BF16)
        bg = self.tile("bg", [128, NCH, 4 * H])
        bgs = self.tile("bgs", [128, NCH, 3, DH])
        self.load(bg, bg[0:64, :, :], self.BG)
        for T_ in (QT, KT, VT):
            self.memset(T_, T_[:, :, 0:48], 0.0, eng="pool")
        self.copy(bgs[0:64, :, 0, 0:H], bg[0:64, :, 0:H], reads=(bg.R,), writes=(bgs.R,), eng="pool")
        self.copy(bgs[0:64, :, 1, 0:H], bg[0:64, :, 2 * H:3 * H], reads=(bg.R,), writes=(bgs.R,), eng="pool")
        for j in range(NCH):
            cb = NCH - 1 - j
            self.copy(bgs[0:64, j, 0:2, H:DH], bg[0:64, cb, :].rearrange("p (q d h) -> p q d h", q=2, d=2)[:, :, 1, :],
                      reads=(bg.R,), writes=(bgs.R,), eng="pool")
        self.ts(bgs[0:64, :, 2, :], bgs[0:64, :, 0, :], -1.0, None, ALU.mult, None, reads=(bgs.R,), writes=(bgs.R,), eng="pool")
        mark = self.aptr
        WW = min(512, c.SEQ)
        raw = RR([self.tile("raw%d" % i, [128, H3, WW + 2]) for i in range(2)])
        acc = self.tile("acc", [128, H3, WW])
        sqq = RR([self.tile("sqq%d" % i, [128, WW], BF16) for i in range(2)])
        rsq = RR([self.tile("rsq%d" % i, [128, WW]) for i in range(2)])
        qkvv = self.QKV.rearrange("(k p) t -> p k t", p=128)
        self.banks = RR([self.bank(i) for i in range(8)])

        def cwt(tap, j):
            i = (l * 3 + tap) * H3 + j
            return self.dnconv[:, i:i + 1]
        wins = [(t0, WW) for t0 in range(0, c.SEQ, WW)] + [(c.SEQ, 16)]
        for (t0, n) in wins:
            rw = raw.next()
            pc0, pp0 = self.pcol(t0), self.ppos(t0)
            self.load(rw, rw[:, :, 0:n + 2], qkvv[:, :, pc0 - 1:pc0 + n + 1])
            for j in range(H3):
                eng = "dve"
                self.act(acc[:, j, 0:n], rw[:, j, 1:n + 1], AF.Copy, reads=(rw.R, self.dnconv.R), writes=(acc.R,), scale=cwt(1, j))
                self.stt(acc[:, j, 0:n], rw[:, j, 0:n], cwt(0, j), acc[:, j, 0:n], ALU.mult, ALU.add,
                         reads=(rw.R, self.dnconv.R, acc.R), writes=(acc.R,), eng=eng)
                self.stt(acc[:, j, 0:n], rw[:, j, 2:n + 2], cwt(2, j), acc[:, j, 0:n], ALU.mult, ALU.add,
                         reads=(rw.R, self.dnconv.R, acc.R), writes=(acc.R,), eng=eng)
            self.act(acc[:, :, 0:n], acc[:, :, 0:n], AF.Silu, reads=(acc.R,), writes=(acc.R,))
            for j in range(2 * H):
                sq_, rs_ = sqq.next(), rsq.next()
                self.act(sq_[:, 0:n], acc[:, j, 0:n], AF.Square, reads=(acc.R,), writes=(sq_.R,))
                ps = self.banks.next()
                self.mm(ps, ps[:, 0:n], self.onesb[:], sq_[:, 0:n], True, True, reads=(self.onesb.R, sq_.R))
                self.act(rs_[:, 0:n], ps[:, 0:n], AF.Sqrt, reads=(ps.R, self.epsb.R), writes=(rs_.R,), bias=self.epsb[:])
                self.recip(rs_[:, 0:n], rs_[:, 0:n], reads=(rs_.R,), writes=(rs_.R,))
                if j < H:
                    self.stt(QT[:, j, pp0:pp0 + n], acc[:, j, 0:n], float(128 ** -0.5), rs_[:, 0:n], ALU.mult, ALU.mult,
                             reads=(acc.R, rs_.R), writes=(QT.R,))
                else:
                    self.tt(KT[:, j - H, pp0:pp0 + n], acc[:, j, 0:n], rs_[:, 0:n], ALU.mult, reads=(acc.R, rs_.R), writes=(KT.R,))
            self.copy(VT[:, :, pp0:pp0 + n], acc[:, 2 * H:3 * H, 0:n], reads=(acc.R,), writes=(VT.R,), eng="pool")
        S.barrier()
        self.aptr = mark
        dm = self.tile("dm", [128, 2 * 2 * 64])
        self.load(dm, dm[:], self.dmask_in[:, 0:256])
        L1 = dm[:, 0:128].rearrange("p (d s) -> p d s", s=64)
        L2 = dm[:, 128:256].rearrange("p (d s) -> p d s", s=64)
        ds_ = self.tile("ds", [128, 2 * 64 * 4 + 128])
        self.load(ds_, ds_[0:64, :], self.dsmall_in)
        U1 = ds_[0:64, 0:128].rearrange("p (d s) -> p d s", s=64)
        U2 = ds_[0:64, 128:256].rearrange("p (d s) -> p d s", s=64)
        TC = ds_[0:64, 256:384].rearrange("p (d s) -> p d s", s=64)
        TN = ds_[0:64, 384:512].rearrange("p (d s) -> p d s", s=64)
        ON = ds_[0:64, 512:640]
        GU = []
        for v in range(2):
            pair = []
            for i in range(2):
                t_ = self.tile("GU%d_%d" % (v, i), [128, DH, 64])
                o0 = 256 + v * (2 * H * 64)
                self.load(t_, t_[64:128, :, :].rearrange("p a s -> p (a s)"), self.dmask_in[64:128, o0:o0 + DH * 64])
                pair.append(t_)
            GU.append(RR(pair))
        Sf = self.tile("Sf", [128, DH, 128])
        Sb = self.tile("Sb", [128, DH, 128], BF16)
        Stmp = self.tile("Stmp", [128, DH, 128])
        self.memset(Sf, Sf[:], 0.0)
        self.memset(Sb, Sb[:], 0.0)
        E12 = RR([self.tile("E12_%d" % i, [128, 2, DH, 64]) for i in range(2)])
        ex = RR([self.tile("ex%d" % i, [128, 3, DH]) for i in range(3)])
        egl = RR([self.tile("egl%d" % i, [128, DH]) for i in range(3)])
        cf0 = RR([self.tile("cf0_%d" % i, [128, DH]) for i in range(2)])
        kbg = RR([self.tile("kbg%d" % i, [128, DH, 128], BF16) for i in range(2)])
        kd = RR([self.tile("kd%d" % i, [128, DH, 128], BF16) for i in range(3)])
        vb = RR([self.tile("vb%d" % i, [128, DH, 128], BF16) for i in range(2)])
        tmpNs = [self.tile("tmpN%d" % i, [128, DH, 64]) for i in range(2)]
        Abufs = [[self.tile("A%d_%d" % (q, i), [128, DH, 64], BF16) for i in range(2)] for q in range(2)]
        BPbufs = [[self.tile("BP%d_%d" % (q, i), [128, DH, 2, 64], BF16) for i in range(2)] for q in range(2)]
        attnT = RR([self.tile("attnT%d" % i, [128, DH, 64], BF16) for i in range(3)])
        u_ = RR([self.tile("u%d" % i, [128, DH, 128]) for i in range(3)])
        wT = RR([self.tile("wT%d" % i, [128, DH, 64], BF16) for i in range(3)])
        vnew = RR([self.tile("vnew%d" % i, [128, DH, 128], BF16) for i in range(2)])
        ost = RR([self.tile("ost%d" % i, [128, DH, 128]) for i in range(2)])
        slots = RR([self.bank(0, 2), self.bank(2, 2), self.bank(4, 2), self.bank(6, 2)])
        idb, id32 = self.idbf, self.id32
        I64b = id32[0:64, 0:64].unsqueeze(1).to_broadcast([64, DH, 64])

        def cols(d, j):
            cd = j if d == 0 else NCH - 1 - j
            return cd * 64

        def prep(j, P):
            tmpN, Abuf, BPbuf = tmpNs[j % 2], Abufs[j % 2], BPbufs[j % 2]
            c0 = [cols(0, j), cols(1, j)]
            g2 = bgs[0:64, j, 1, :]
            b2 = bgs[0:64, j, 0, :]
            nb2 = bgs[0:64, j, 2, :]
            gu1, gu2 = GU[0].next(), GU[1].next()
            for (gu, U) in ((gu1, U1), (gu2, U2)):
                self.tt(gu[0:64, :, :].rearrange("p (d h) s -> p d h s", d=2),
                        g2.rearrange("p (d h) -> p d h", d=2).unsqueeze(3).to_broadcast([64, 2, H, 64]),
                        U.unsqueeze(2).to_broadcast([64, 2, H, 64]), ALU.mult, reads=(bgs.R, ds_.R), writes=(gu.R,))
            dps = slots.next()
            for v, (gu, LL) in enumerate(((gu1, L1), (gu2, L2))):
                for d in range(2):
                    o0 = (v * 2 + d) * H * 64
                    self.mm(dps, dps[0:64, o0:o0 + H * 64], LL[:, d, :], gu[:, d * H:(d + 1) * H, :].rearrange("p h s -> p (h s)"),
                            True, True, reads=(dm.R, gu.R))
            e12 = E12.next()
            self.act(e12[0:64, :, :, :].rearrange("p v a s -> p (v a s)"), dps[0:64, 0:2 * DH * 64], AF.Exp, reads=(dps.R,), writes=(e12.R,))
            gps = slots.next()
            for d in range(2):
                gd = g2[:, d * H:(d + 1) * H]
                for q, LT in enumerate((TC[:, d, :], TN[:, d, :], ON[:, 0:64])):
                    o0 = q * DH + d * H
                    self.mm(gps, gps[0:64, o0:o0 + H], LT, gd, True, True, reads=(ds_.R, bgs.R))
                self.mm(gps, gps[:, 512 + d * H:512 + (d + 1) * H], ON[:, 0:128], gd, True, True, reads=(ds_.R, bgs.R))
            ex_ = ex.next()
            self.act(ex_[0:64, :, :].rearrange("p q a -> p (q a)"), gps[0:64, 0:3 * DH], AF.Exp, reads=(gps.R,), writes=(ex_.R,))
            egl_ = egl.next()
            self.act(egl_[:, :], gps[:, 512:512 + DH], AF.Exp, reads=(gps.R,), writes=(egl_.R,))
            yield
            cf_ = cf0.next()
            self.tt(cf_[0:64, :], b2, ex_[0:64, 0, :], ALU.mult, reads=(bgs.R, ex_.R), writes=(cf_.R,))
            kps, vps = slots.next(), slots.next()
            for (ps, SRC) in ((kps, KT), (vps, VT)):
                for d in range(2):
                    for h in range(H):
                        dh = d * H + h
                        self.mm(ps, ps[0:64, dh * 128:(dh + 1) * 128], SRC[:, h, c0[d]:c0[d] + 64], idb[:], True, True,
                                reads=(SRC.R, idb.R))
            kbg_, kd_, vb_ = kbg.next(), kd.next(), vb.next()
            k3 = kps[0:64, 0:DH * 128].rearrange("p (a k) -> p a k", k=128)
            v3 = vps[0:64, 0:DH * 128].rearrange("p (a k) -> p a k", k=128)
            self.tt(kbg_[0:64, :, :], k3, cf_[0:64, :].unsqueeze(2).to_broadcast([64, DH, 128]), ALU.mult, reads=(kps.R, cf_.R), writes=(kbg_.R,))
            self.tt(kd_[0:64, :, :], k3, ex_[0:64, 1, :].unsqueeze(2).to_broadcast([64, DH, 128]), ALU.mult, reads=(kps.R, ex_.R), writes=(kd_.R,))
            self.tt(vb_[0:64, :, :], v3, b2.unsqueeze(2).to_broadcast([64, DH, 128]), ALU.mult, reads=(vps.R, bgs.R), writes=(vb_.R,))
            yield
            kq = slots.next()
            for d in range(2):
                for h in range(H):
                    dh = d * H + h
                    kk_ = KT[:, h, c0[d]:c0[d] + 64]
                    self.mm(kq, kq[0:64, dh * 64:(dh + 1) * 64], kk_, kk_, True, True, reads=(KT.R,))
                    self.mm(kq, kq[0:64, 512 + dh * 64:512 + (dh + 1) * 64], kk_, QT[:, h, c0[d]:c0[d] + 64], True, True, reads=(KT.R, QT.R))
            A0 = Abuf[0]
            self.tt(tmpN[0:64, :, :], kq[0:64, 0:DH * 64].rearrange("p (a s) -> p a s", s=64), e12[0:64, 0, :, :], ALU.mult,
                    reads=(kq.R, e12.R), writes=(tmpN.R,))
            self.tt(A0[0:64, :, :], tmpN[0:64, :, :], nb2.unsqueeze(2).to_broadcast([64, DH, 64]), ALU.mult,
                    reads=(tmpN.R, bgs.R), writes=(A0.R,))
            at_ = attnT.next()
            self.tt(at_[0:64, :, :], kq[0:64, 512:512 + DH * 64].rearrange("p (a s) -> p a s", s=64), e12[0:64, 1, :, :], ALU.mult,
                    reads=(kq.R, e12.R), writes=(at_.R,))
            yield
            bps = slots.next()
            for dh in range(DH):
                self.mm(bps, bps[0:64, dh * 64:(dh + 1) * 64], A0[0:64, dh, :], idb[0:64, 0:64], True, True, reads=(A0.R, idb.R))
            BP0 = BPbuf[0]
            self.copy(BP0[0:64, :, 0, :], bps[0:64, 0:DH * 64].rearrange("p (a s) -> p a s", s=64), reads=(bps.R,), writes=(BP0.R,), eng="act")
            self.copy(BP0[0:64, :, 1, :], I64b, reads=(id32.R,), writes=(BP0.R,), eng="pool")
            for i in range(6):
                yield
                Ac, BPc = Abuf[i % 2], BPbuf[i % 2]
                An, BPn = Abuf[(i + 1) % 2], BPbuf[(i + 1) % 2]
                xps = slots.next()
                if i < 5:
                    for dh in range(DH):
                        self.mm(xps, xps[0:64, dh * 128:(dh + 1) * 128], Ac[0:64, dh, :], BPc[0:64, dh, :, :].rearrange("p q s -> p (q s)"),
                                True, True, reads=(Ac.R, BPc.R))
                    yps = slots.next()
                    for dh in range(DH):
                        self.mm(yps, yps[0:64, dh * 64:(dh + 1) * 64], BPc[0:64, dh, 0, :], Ac[0:64, dh, :], True, True, reads=(Ac.R, BPc.R))
                    x4 = xps[0:64, 0:DH * 128].rearrange("p (a q s) -> p a q s", q=2, s=64)
                    self.copy(BPn[0:64, :, 0, :], x4[:, :, 0, :], reads=(xps.R,), writes=(BPn.R,), eng="act")
                    self.tt(BPn[0:64, :, 1, :], BPc[0:64, :, 1, :], x4[:, :, 1, :], ALU.add, reads=(BPc.R, xps.R), writes=(BPn.R,))
                    self.copy(An[0:64, :, :], yps[0:64, 0:DH * 64].rearrange("p (a s) -> p a s", s=64), reads=(yps.R,), writes=(An.R,), eng="act")
                else:
                    for dh in range(DH):
                        self.mm(xps, xps[0:64, dh * 64:(dh + 1) * 64], Ac[0:64, dh, :], BPc[0:64, dh, 1, :], True, True, reads=(Ac.R, BPc.R))
                    self.tt(BPn[0:64, :, 1, :], BPc[0:64, :, 1, :], xps[0:64, 0:DH * 64].rearrange("p (a s) -> p a s", s=64), ALU.add,
                            reads=(BPc.R, xps.R), writes=(BPn.R,))
            TT = BPbuf[0]
            yield
            ups = slots.next()
            for dh in range(DH):
                self.mm(ups, ups[0:64, dh * 128:(dh + 1) * 128], TT[0:64, dh, 1, :], vb_[0:64, dh, :], True, True, reads=(TT.R, vb_.R))
            uu = u_.next()
            self.copy(uu[0:64, :, :].rearrange("p a k -> p (a k)"), ups[0:64, 0:DH * 128], reads=(ups.R,), writes=(uu.R,), eng="act")
            wps = slots.next()
            for dh in range(DH):
                self.mm(wps, wps[:, dh * 64:(dh + 1) * 64], kbg_[0:64, dh, :], TT[0:64, dh, 1, :], True, True, reads=(kbg_.R, TT.R))
            wt_ = wT.next()
            self.copy(wt_[:, :, :].rearrange("p a s -> p (a s)"), wps[:, 0:DH * 64], reads=(wps.R,), writes=(wt_.R,), eng="act")
            P.update(c0=c0, ex=ex_, egl=egl_, kd=kd_, at=at_, u=uu, wT=wt_)

        def scan(j, P):
            c0 = P["c0"]
            wsp = slots.next()
            for dh in range(DH):
                self.mm(wsp, wsp[0:64, dh * 128:(dh + 1) * 128], P["wT"][:, dh, :], Sb[:, dh, :], True, True, reads=(P["wT"].R, Sb.R))
            vn = vnew.next()
            self.tt(vn[0:64, :, :].rearrange("p a k -> p (a k)"), P["u"][0:64, :, :].rearrange("p a k -> p (a k)"), wsp[0:64, 0:DH * 128],
                    ALU.subtract, reads=(P["u"].R, wsp.R), writes=(vn.R,))
            yield
            o1, o2 = slots.next(), slots.next()
            for d in range(2):
                for h in range(H):
                    dh = d * H + h
                    self.mm(o1, o1[0:64, dh * 128:(dh + 1) * 128], QT[:, h, c0[d]:c0[d] + 64], Sb[:, dh, :], True, True, reads=(QT.R, Sb.R))
            for dh in range(DH):
                self.mm(o2, o2[0:64, dh * 128:(dh + 1) * 128], P["at"][0:64, dh, :], vn[0:64, dh, :], True, True, reads=(P["at"].R, vn.R))
            os_ = ost.next()
            self.tt(os_[0:64, :, :], o1[0:64, 0:DH * 128].rearrange("p (a k) -> p a k", k=128),
                    P["ex"][0:64, 0, :].unsqueeze(2).to_broadcast([64, DH, 128]), ALU.mult, reads=(o1.R, P["ex"].R), writes=(os_.R,))
            self.tt(os_[0:64, :, :].rearrange("p a k -> p (a k)"), os_[0:64, :, :].rearrange("p a k -> p (a k)"), o2[0:64, 0:DH * 128],
                    ALU.add, reads=(os_.R, o2.R), writes=(os_.R,))
            self.store(os_, self.OF[c0[0]:c0[0] + 64, :], os_[0:64, 0:H, :].rearrange("p a k -> p (a k)"))
            self.store(os_, self.OB[c0[1]:c0[1] + 64, :], os_[0:64, H:DH, :].rearrange("p a k -> p (a k)"))
            yield
            dsp = slots.next()
            for dh in range(DH):
                self.mm(dsp, dsp[:, dh * 128:(dh + 1) * 128], P["kd"][0:64, dh, :], vn[0:64, dh, :], True, True, reads=(P["kd"].R, vn.R))
            self.tt(Stmp[:, :, :], Sf[:, :, :], P["egl"][:, :].unsqueeze(2).to_broadcast([128, DH, 128]), ALU.mult,
                    reads=(Sf.R, P["egl"].R), writes=(Stmp.R,))
            self.tt(Sf[:, :, :].rearrange("p a k -> p (a k)"), Stmp[:, :, :].rearrange("p a k -> p (a k)"), dsp[:, 0:DH * 128], ALU.add,
                    reads=(Stmp.R, dsp.R), writes=(Sf.R,))
            self.copy(Sb[:, :, :], Sf[:, :, :], reads=(Sf.R,), writes=(Sb.R,), eng="act")

        NST1 = 5
        Ps = {}
        gens = {}

        def start(j):
            if j < NCH:
                Ps[j] = {}
                gens[j] = prep(j, Ps[j])

        start(0)
        for _ in gens[0]:
            pass
        start(1)
        if 1 in gens:
            for _ in range(NST1):
                next(gens[1], None)
        for j in range(NCH):
            gB = gens.get(j + 1)
            start(j + 2)
            gA = gens.get(j + 2)
            gs = scan(j, Ps[j])
            for k in range(8):
                if gB is not None:
                    next(gB, None)
                if gA is not None and k < NST1:
                    next(gA, None)
                if k in (0, 2, 4):
                    next(gs, None)
            if gB is not None:
                for _ in gB:
                    pass
            for _ in gs:
                pass
        self.banks = RR([self.bank(i) for i in range(8)])


def host_consts(c):
    out = {}
    out["c_id32"] = np.eye(128, dtype=np.float32)
    out["c_idbf"] = np.eye(128, dtype=np.float32).astype(ml_dtypes.bfloat16)
    cc = np.arange(64)
    ang = 2 * np.pi * np.outer(cc, cc) / 64.0
    C64 = np.cos(ang) / 8.0
    S64 = np.sin(ang) / 8.0
    cs = np.zeros((128, 256), np.float64)
    for g in range(2):
        cs[g * 64:(g + 1) * 64, g * 64:(g + 1) * 64] = C64
        cs[g * 64:(g + 1) * 64, 128 + g * 64:128 + (g + 1) * 64] = S64
    out["c_cs64"] = cs.astype(np.float32).astype(ml_dtypes.bfloat16)
    L = c.L
    tau = np.arange(L)
    pos = np.where(tau < c.SEQ, tau + 16, tau - c.SEQ).astype(np.int64)
    m = (np.outer(pos, pos) % L).astype(np.float64)
    ang = 2 * np.pi * m / L
    cls = np.stack([np.cos(ang), -np.sin(ang)]) / np.sqrt(L)
    clsb = cls.astype(np.float32).astype(ml_dtypes.bfloat16)
    arr = np.zeros((len(c.slices), 128, c.NB, 2, c.SW), dtype=ml_dtypes.bfloat16)
    rows = np.zeros((2, c.NB * 128, L), dtype=ml_dtypes.bfloat16)
    rows[:, :L, :] = clsb
    rows = rows.reshape(2, c.NB, 128, L)
    for si, (t0, n) in enumerate(c.slices):
        arr[si, :, :, :, 0:n] = rows[:, :, :, t0:t0 + n].transpose(2, 1, 0, 3)
    out["c_cls"] = arr
    H = c.H
    p = np.arange(64)[:, None]
    q = np.arange(64)[None, :]
    tric = [(p <= q), (p >= q)]
    a2 = [(p > q), (p < q)]
    u1 = [(p > q), (p < q)]
    u2 = [(p <= q), (p >= q)]
    negm1 = [np.where(p > q, 0.0, NEG), np.where(p < q, 0.0, NEG)]
    negm2 = [np.where(q >= p, 0.0, NEG), np.where(q <= p, 0.0, NEG)]
    eye = np.eye(64)
    L1 = np.zeros((128, 2, 64))
    L2 = np.zeros((128, 2, 64))
    N1 = np.zeros((128, 2, H, 64))
    N2 = np.zeros((128, 2, H, 64))
    for d in range(2):
        L1[:64, d] = tric[d]
        L1[64:, d] = eye
        L2[:64, d] = a2[d]
        L2[64:, d] = eye
        for h in range(H):
            N1[64:, d, h] = negm1[d]
            N2[64:, d, h] = negm2[d]
    out["c_dmask"] = np.concatenate([L1.reshape(128, -1), L2.reshape(128, -1), N1.reshape(128, -1), N2.reshape(128, -1)],
                                    axis=1).astype(np.float32)
    U1 = np.stack(u1, 1).astype(np.float64)
    U2 = np.stack(u2, 1).astype(np.float64)
    TC = np.stack(tric, 1).astype(np.float64)
    TN = 1.0 - TC
    out["c_dsmall"] = np.concatenate([U1.reshape(64, -1), U2.reshape(64, -1), TC.reshape(64, -1), TN.reshape(64, -1),
                                      np.ones((64, 128))], axis=1).astype(np.float32)
    return out


def host_layout(c, inp):
    m = {}
    g = np.asarray(inp["norm_gains"], np.float32)
    m["gains"] = np.ascontiguousarray(g.reshape(c.DEPTH, 6, c.KD, 128).transpose(3, 0, 1, 2).reshape(128, -1))
    sw = np.asarray(inp["sconv_w"], np.float32)
    m["sconv_w"] = np.ascontiguousarray(sw.reshape(c.DEPTH, 3, c.DSC // 128, 128).transpose(3, 0, 1, 2).reshape(128, -1))
    dw = np.asarray(inp["dn_conv_w"], np.float32)
    m["dn_conv_w"] = np.ascontiguousarray(dw.reshape(c.DEPTH, 3, 3 * c.H, 128).transpose(3, 0, 1, 2).reshape(128, -1))
    m["dn_A_log"] = np.ascontiguousarray(np.broadcast_to(np.asarray(inp["dn_A_log"], np.float32).reshape(1, -1), (128, c.DEPTH * 2 * c.H)))
    m["dn_dt_bias"] = np.ascontiguousarray(np.broadcast_to(np.asarray(inp["dn_dt_bias"], np.float32).reshape(1, -1), (128, c.DEPTH * 2 * c.H)))
    dn = np.asarray(inp["dn_norm"], np.float32)
    dnt = np.tile(dn[:, None, :], (1, c.H, 1)).reshape(1, -1)
    m["dn_norm"] = np.ascontiguousarray(np.broadcast_to(dnt, (128, c.DEPTH * c.H * 128)))
    m["meta"] = np.ascontiguousarray(np.asarray(inp["meta_tokens"], np.float32))
    for k in ["ffn1_wi", "ffn1_wo", "ffn2_wi", "ffn2_wo", "w_in", "w_fourier", "w_sconv_out", "w_dn_out", "w_out"]:
        m[k] = np.ascontiguousarray(np.asarray(inp[k], np.float32))
    m.update(host_consts(c))
    return m


_CACHE = {}


def kernel(**inputs):
    c = FULL
    if "nc" not in _CACHE:
        _CACHE["nc"] = Builder(c).build()
    nc = _CACHE["nc"]
    shared = host_layout(c, inputs)
    x = np.asarray(inputs["x"], np.float32)
    B = x.shape[0]
    in_maps = []
    for b in range(B):
        mm_ = dict(shared)
        mm_["x"] = np.ascontiguousarray(x[b])
        in_maps.append(mm_)
    res = run_bass_kernel_spmd(nc, in_maps, core_ids=list(range(B)))
    return np.stack([np.asarray(r["out"], np.float32) for r in res.results], axis=0)
```

```python
import numpy as np
import ml_dtypes
import concourse.bass as bass
import concourse.mybir as mybir
from concourse.bass_utils import run_bass_kernel_spmd

F32 = mybir.dt.float32
BF16 = mybir.dt.bfloat16
AF = mybir.ActivationFunctionType
ALU = mybir.AluOpType
EPS = 1e-6
NEG = -30000.0


class Cfg:
    def __init__(s, D=1024, DFF=2816, SEQ=4096, FG=8, DSC=512, H=4, TT=512, DEPTH=2):
        s.D, s.DFF, s.SEQ, s.FG, s.DSC, s.H, s.TT, s.DEPTH = D, DFF, SEQ, FG, DSC, H, TT, DEPTH
        s.NM = 16
        s.L = SEQ + 16
        s.DF = FG * 64
        s.DDN = H * 128
        s.KD = D // 128
        s.KF = DFF // 128
        s.NCH = SEQ // 64 + 1
        s.LP = 64 * s.NCH
        s.NB = (s.L + 127) // 128
        s.o_f = 0
        s.o_sc = s.DF
        s.o_qkv = s.o_sc + 3 * DSC
        s.o_z = s.o_qkv + 3 * s.DDN
        s.o_b = s.o_z + s.DDN
        s.o_a = s.o_b + 2 * H
        s.o_g = s.o_a + 2 * H
        s.DIN = s.o_g + 3 * D
        s.tiles = []
        nt = SEQ // TT
        for i in range(nt):
            sl = [(i * TT, TT)] if TT <= 512 else [(i * TT + k, 512) for k in range(0, TT, 512)]
            if i == nt - 1:
                sl.append((SEQ, 16))
            s.tiles.append(sl)
        s.TW = TT + 16
        s.GS = 512 if (s.DF % 512 == 0 and DSC % 512 == 0 and s.DDN % 512 == 0 and D % 512 == 0) else 128
        s.SW = min(512, TT)
        s.slices = [sl for tl in s.tiles for sl in tl]


FULL = Cfg()


class Region:
    __slots__ = ("name", "w", "r")

    def __init__(self, name):
        self.name = name
        self.w = None
        self.r = {}


class Sched:
    ENG = ("pe", "dve", "act", "pool", "sp")

    def __init__(self, nc):
        self.nc = nc
        self.ops = {e: [] for e in self.ENG}
        self.cnt = {e: 0 for e in self.ENG}
        self.waited = {e: {} for e in self.ENG}
        self.sems = {}
        self.tot = {}
        for e in self.ENG:
            self.sem("E_" + e)
        self.NPOOL = 64
        for i in range(self.NPOOL):
            self.sem("D%d" % i)
        self.lmap = {}
        self.free = ["D%d" % i for i in range(self.NPOOL)]

    def phys(self, lname):
        if lname.startswith("C_") or lname == "OUT":
            return self.sem(lname)
        if lname not in self.lmap:
            self.lmap[lname] = self.free.pop(0)
        return self.lmap[lname]

    def sem(self, name):
        if name not in self.sems:
            self.sems[name] = self.nc.alloc_semaphore(name)
            self.tot[name] = 0
        return name

    def _waits(self, eng, reads, writes):
        need = {}

        def add(ev):
            if ev is None:
                return
            s, v = ev
            if need.get(s, 0) < v:
                need[s] = v
        for R in reads:
            add(R.w)
        for R in writes:
            add(R.w)
            for s, v in R.r.items():
                add((s, v))
        out = []
        wd = self.waited[eng]
        own = "E_" + eng
        for s, v in need.items():
            if s == own and eng == "pe":
                continue
            if wd.get(s, 0) >= v:
                continue
            wd[s] = v
            out.append((s, v))
        return out

    def op(self, eng, fn, reads=(), writes=()):
        waits = self._waits(eng, reads, writes)
        own = "E_" + eng
        self.cnt[eng] += 1
        self.tot[own] = self.cnt[eng]
        ev = (own, self.cnt[eng])
        self.ops[eng].append((waits, fn, (own, 1)))
        for R in reads:
            if R.r.get(own, 0) < ev[1]:
                R.r[own] = ev[1]
        for R in writes:
            R.w = ev
            R.r = {}
        return ev

    def dma(self, q, out, in_, reads=(), writes=(), sem=None, slow=False):
        waits = self._waits(q, reads, writes)
        sem = self.phys(sem)
        self.tot[sem] += 16
        ev = (sem, self.tot[sem])
        if slow:
            self.ops[q].append((waits, lambda e, o=out, i=in_: e.dma_start(out=o, in_=i, allow_slow_non_contiguous=True), (sem, 16)))
        else:
            self.ops[q].append((waits, lambda e, o=out, i=in_: e.dma_start(out=o, in_=i), (sem, 16)))
        for R in reads:
            if R.r.get(sem, 0) < ev[1]:
                R.r[sem] = ev[1]
        for R in writes:
            R.w = ev
            R.r = {}
        return ev

    def barrier(self):
        for e in self.ENG:
            wl = []
            for s, v in self.tot.items():
                if s.startswith("C_"):
                    continue
                if v > 0 and self.waited[e].get(s, 0) < v and s != "E_" + e:
                    self.waited[e][s] = v
                    wl.append((s, v))
            if wl:
                self.ops[e].append((wl, None, None))
        self.lmap = {}
        self.free = ["D%d" % i for i in range(self.NPOOL)]

    def emit(self):
        nc = self.nc
        sems = self.sems

        def replay(engine, lst):
            for waits, fn, inc in lst:
                for s, v in waits:
                    engine.wait_ge(sems[s], v)
                if fn is not None:
                    ins = fn(engine)
                    ins.then_inc(sems[inc[0]], inc[1])

        with nc.Block() as block:
            @block.tensor
            def _(e):
                replay(e, self.ops["pe"])

            @block.vector
            def _(e):
                replay(e, self.ops["dve"])

            @block.scalar
            def _(e):
                replay(e, self.ops["act"])

            @block.gpsimd
            def _(e):
                replay(e, self.ops["pool"])

            @block.sync
            def _(e):
                replay(e, self.ops["sp"])


class RR:
    def __init__(self, items):
        self.items = items
        self.i = 0

    def next(self):
        x = self.items[self.i % len(self.items)]
        self.i += 1
        return x


class Tile:
    def __init__(self, name, ap):
        self.ap = ap
        self.R = Region(name)

    def __getitem__(self, k):
        return self.ap[k]


class Builder:
    def __init__(self, cfg, stop_after=None):
        self.c = cfg
        self.stop_after = stop_after
        self.nc = bass.Bass("TRN2", target_bir_lowering=False)
        self.S = Sched(self.nc)
        self.evi = 0
        self.uid = 0
        self.ARENA = 52800
        self.arena = self.nc.alloc_sbuf_tensor("arena", [128, self.ARENA], F32)
        self.aptr = 0
        self.psum_all = self.nc.alloc_psum_tensor("psum_all", [128, 4096], F32)

    def din(self, name, shape, dt=F32):
        return self.nc.dram_tensor(name, list(shape), dt, kind="ExternalInput").ap()

    def dscr(self, name, shape, dt=F32):
        return self.nc.dram_tensor(name, list(shape), dt, kind="Internal").ap()

    def tile(self, name, shape, dt=F32, at=None):
        n = 1
        for d in shape[1:]:
            n *= d
        cols = n if dt == F32 else (n + 1) // 2
        cols = (cols + 7) // 8 * 8
        if at is not None:
            a = at
            self.last_end = a + cols
        else:
            a = self.aptr
            self.aptr += cols
            assert self.aptr <= self.ARENA, ("SBUF arena overflow", name, self.aptr)
        ap = self.arena[:, a:a + cols]
        if dt != F32:
            ap = ap.bitcast(dt)
        ap = ap[:, 0:n]
        if len(shape) == 3:
            ap = ap.rearrange("p (a b) -> p a b", b=shape[2])
        elif len(shape) == 4:
            ap = ap.rearrange("p (a b c) -> p a b c", b=shape[2], c=shape[3])
        elif len(shape) == 5:
            ap = ap.rearrange("p (a b c d) -> p a b c d", b=shape[2], c=shape[3], d=shape[4])
        self.uid += 1
        return Tile("%s_%d" % (name, self.uid), ap)

    def bank(self, i, nb=1):
        return Tile("pb%d_%d" % (i, nb), self.psum_all[:, i * 512:(i + nb) * 512])

    def load(self, tl, out, in_, q="sp", extra_reads=(), sem=None):
        self.S.dma(q, out, in_, reads=extra_reads, writes=(tl.R,), sem=sem or ("L_" + tl.R.name))

    def store(self, tl, out, in_, q="act", sem=None):
        self.S.dma(q, out, in_, reads=(tl.R,), writes=(), sem=sem or ("L_" + tl.R.name))

    def mm(self, ps, out, lhsT, rhs, start, stop, reads):
        self.S.op("pe", lambda e: e.matmul(out, lhsT=lhsT, rhs=rhs, start=start, stop=stop),
                  reads=reads, writes=(ps.R,))

    def act(self, out, in_, func, reads, writes, bias=None, scale=None):
        kw = {}
        if bias is not None:
            kw["bias"] = bias
        if scale is not None:
            kw["scale"] = scale
        self.S.op("act", lambda e: e.activation(out=out, in_=in_, func=func, **kw), reads=reads, writes=writes)

    def tt(self, out, in0, in1, op, reads, writes, eng="dve"):
        self.S.op(eng, lambda e: e.tensor_tensor(out=out, in0=in0, in1=in1, op=op), reads=reads, writes=writes)

    def stt(self, out, in0, scalar, in1, op0, op1, reads, writes, eng="dve"):
        self.S.op(eng, lambda e: e.scalar_tensor_tensor(out=out, in0=in0, scalar=scalar, in1=in1, op0=op0, op1=op1),
                  reads=reads, writes=writes)

    def ts(self, out, in0, s1, s2, op0, op1, reads, writes, eng="dve"):
        if s2 is None:
            self.S.op(eng, lambda e: e.tensor_scalar(out=out, in0=in0, scalar1=s1, scalar2=None, op0=op0),
                      reads=reads, writes=writes)
        else:
            self.S.op(eng, lambda e: e.tensor_scalar(out=out, in0=in0, scalar1=s1, scalar2=s2, op0=op0, op1=op1),
                      reads=reads, writes=writes)

    def copy(self, out, in_, reads, writes, eng=None):
        if eng is None:
            self.evi += 1
            eng = "act" if self.evi % 2 else "dve"
        if eng == "act":
            self.S.op("act", lambda e: e.activation(out=out, in_=in_, func=AF.Copy), reads=reads, writes=writes)
        else:
            self.S.op(eng, lambda e: e.tensor_copy(out=out, in_=in_), reads=reads, writes=writes)

    def recip(self, out, in_, reads, writes):
        self.S.op("dve", lambda e: e.reciprocal(out=out, in_=in_), reads=reads, writes=writes)

    def memset(self, tl, ap, val, eng="dve"):
        self.S.op(eng, lambda e: e.memset(ap, val), reads=(), writes=(tl.R,))

    def pcol(self, tau):
        return 1 + (tau + 16 if tau < self.c.SEQ else tau - self.c.SEQ)

    def ppos(self, tau):
        return 64 + tau if tau < self.c.SEQ else 48 + (tau - self.c.SEQ)

    def build(self):
        c = self.c
        nc = self.nc
        S = self.S
        D, KD, KF, L, TW, H = c.D, c.KD, c.KF, c.L, c.TW, c.H
        x_in = self.din("x", [c.SEQ, D])
        meta_in = self.din("meta", [16, D])
        gains_in = self.din("gains", [128, c.DEPTH * 6 * KD])
        Wf32 = {}
        wshapes = {"ffn1_wi": (D, 2 * c.DFF), "ffn1_wo": (c.DFF, D), "ffn2_wi": (D, 2 * c.DFF),
                   "ffn2_wo": (c.DFF, D), "w_in": (D, c.DIN), "w_fourier": (c.DF, D),
                   "w_sconv_out": (c.DSC, D), "w_dn_out": (c.DDN, D), "w_out": (D, D)}
        for k, shp in wshapes.items():
            Wf32[k] = self.din(k, [c.DEPTH, shp[0], shp[1]])
        sconv_in = self.din("sconv_w", [128, c.DEPTH * 3 * (c.DSC // 128)])
        dnconv_in = self.din("dn_conv_w", [128, c.DEPTH * 3 * 3 * H])
        dnA_in = self.din("dn_A_log", [128, c.DEPTH * 2 * H])
        dndt_in = self.din("dn_dt_bias", [128, c.DEPTH * 2 * H])
        dnnorm_in = self.din("dn_norm", [128, c.DEPTH * H * 128])
        id32_in = self.din("c_id32", [128, 128])
        idbf_in = self.din("c_idbf", [128, 128], BF16)
        cs64_in = self.din("c_cs64", [128, 256], BF16)
        cls_in = self.din("c_cls", [len(c.slices), 128, c.NB, 2, c.SW], BF16)
        dmask_in = self.din("c_dmask", [128, 2 * 2 * 64 + 2 * 2 * H * 64])
        dsmall_in = self.din("c_dsmall", [64, 2 * 64 * 4 + 128])
        out_d = self.nc.dram_tensor("out", [c.SEQ, D], F32, kind="ExternalOutput").ap()

        Wb = {k: self.dscr("b_" + k, [c.DEPTH, wshapes[k][0], wshapes[k][1]], BF16) for k in ("w_fourier", "w_sconv_out", "w_dn_out")}
        h_scr = self.dscr("h_scr", [D, L])
        self.h_scr = h_scr

        self.gains = self.tile("gains", [128, c.DEPTH * 6 * KD])
        self.load(self.gains, self.gains[:], gains_in)
        self.id32 = self.tile("id32", [128, 128])
        self.load(self.id32, self.id32[:], id32_in)
        self.idbf = self.tile("idbf", [128, 128], BF16)
        self.load(self.idbf, self.idbf[:], idbf_in)
        self.onesb = self.tile("onesb", [128, 128], BF16)
        self.memset(self.onesb, self.onesb[:], 1.0)
        self.epsb = self.tile("epsb", [128, 1])
        self.memset(self.epsb, self.epsb[:], EPS)

        self.banks = RR([self.bank(i) for i in range(8)])

        self.WR = {}
        GS = c.GS
        self.segs = {"ffn1_wi": [(0, c.DFF, GS), (c.DFF, c.DFF, GS)], "ffn2_wi": [(0, c.DFF, GS), (c.DFF, c.DFF, GS)],
                     "w_in": [(0, c.o_b, GS), (c.o_g, 3 * D, GS)], "w_out": [(0, D, GS)],
                     "ffn1_wo": [(0, D, 128)], "ffn2_wo": [(0, D, 128)]}
        self.Wg = {}
        for k, sg in self.segs.items():
            for si, (s0, sn, gs) in enumerate(sg):
                ng = (sn + gs - 1) // gs
                self.Wg[(k, si)] = self.dscr("g_%s_%d" % (k, si), [c.DEPTH, 128, ng, wshapes[k][0] // 128, gs], BF16)
        self.wab_scr = self.dscr("wab_scr", [c.DEPTH, D, 4 * H], BF16)
        for l in range(c.DEPTH):
            for k in ["ffn1_wi", "ffn1_wo", "w_in", "w_fourier", "w_sconv_out", "w_dn_out", "w_out", "ffn2_wi", "ffn2_wo"]:
                R = Region("W_%s_%d" % (k, l))
                self.WR[(k, l)] = R
                sem = "C_%s_%d" % (k, l)
                if k in self.segs:
                    src = Wf32[k][l].rearrange("(k p) n -> p k n", p=128)
                    sg = self.segs[k]
                    ngmax = max((sn + gs - 1) // gs for (_, sn, gs) in sg)
                    for g in range(ngmax):
                        for si, (s0, sn, gs) in enumerate(sg):
                            if g * gs >= sn:
                                continue
                            gw = min(gs, sn - g * gs)
                            if k == "ffn1_wi" and l == 0:
                                gsem = "C_%s_%d_g%d" % (k, l, g)
                                S.dma("pool", self.Wg[(k, si)][l, :, g, :, 0:gw], src[:, :, s0 + g * gs:s0 + g * gs + gw],
                                      reads=(), writes=(), sem=gsem)
                                Rg = self.WR.setdefault((k, l, g), Region("Wg_%s_%d_%d" % (k, l, g)))
                                Rg.w = (S.phys(gsem), S.tot[S.phys(gsem)])
                                continue
                            S.dma("pool", self.Wg[(k, si)][l, :, g, :, 0:gw], src[:, :, s0 + g * gs:s0 + g * gs + gw],
                                  reads=(), writes=(), sem=sem)
                    if k == "w_in":
                        S.dma("pool", self.wab_scr[l], Wf32[k][l, :, c.o_b:c.o_b + 4 * H], reads=(), writes=(), sem=sem, slow=True)
                else:
                    S.dma("pool", Wb[k][l], Wf32[k][l], reads=(), writes=(), sem=sem)
                if not (k == "ffn1_wi" and l == 0):
                    R.w = (S.phys(sem), S.tot[S.phys(sem)])
        self.Wb = Wb

        self.onef = self.tile("onef", [128, 1])
        self.memset(self.onef, self.onef[:], 1.0)
        self.zero = self.tile("zero", [128, 64])
        self.memset(self.zero, self.zero[:], 0.0)
        self.cs64 = self.tile("cs64", [128, 256], BF16)
        self.load(self.cs64, self.cs64[:], cs64_in)
        self.sconv = self.tile("sconv", [128, c.DEPTH * 3 * (c.DSC // 128)])
        self.load(self.sconv, self.sconv[:], sconv_in)
        self.dnconv = self.tile("dnconv", [128, c.DEPTH * 3 * 3 * H])
        self.load(self.dnconv, self.dnconv[:], dnconv_in)
        self.dtb = self.tile("dtb", [128, c.DEPTH * 2 * H])
        self.load(self.dtb, self.dtb[:], dndt_in)
        self.negA = self.tile("negA", [128, c.DEPTH * 2 * H])
        self.load(self.negA, self.negA[:], dnA_in)
        self.act(self.negA[:], self.negA[:], AF.Exp, reads=(self.negA.R,), writes=(self.negA.R,))
        self.ts(self.negA[:], self.negA[:], -1.0, None, ALU.mult, None, reads=(self.negA.R,), writes=(self.negA.R,))
        self.dnnorm = self.tile("dnnorm", [128, c.DEPTH * H * 128])
        self.load(self.dnnorm, self.dnnorm[:], dnnorm_in)
        self.cls_in = cls_in
        self.dmask_in = dmask_in
        self.dsmall_in = dsmall_in

        self.UCS = self.dscr("UCS_scr", [c.NB * 128, 2 * c.DF], BF16)
        self.CH = self.dscr("CH_scr", [c.DSC, L + 2])
        self.BB = self.dscr("BB_scr", [c.DSC, L])
        self.QKV = self.dscr("QKV_scr", [3 * c.DDN, L + 2])
        self.BG = self.dscr("BG_scr", [64, c.NCH, 4 * H])
        self.YF = self.dscr("YF_scr", [c.DF, L], BF16)
        self.OF = self.dscr("OF_scr", [c.LP, c.DDN])
        self.OB = self.dscr("OB_scr", [c.LP, c.DDN])
        for scr, nk in ((self.CH, c.DSC // 128), (self.QKV, 3 * H)):
            v = scr.rearrange("(k p) t -> p k t", p=128)
            for col in (0, L + 1):
                S.dma("sp", v[:, :, col:col + 1], self.zero[:, 0:nk].unsqueeze(2), reads=(self.zero.R,), writes=(), sem="Z", slow=True)
        S.dma("sp", self.BG[0:48, 0, :], self.zero[0:48, 0:4 * H], reads=(self.zero.R,), writes=(), sem="Z")
        if self.stop_after == "nodn":
            for scr in (self.OF, self.OB):
                for r0 in range(0, c.LP, 64):
                    S.dma("sp", scr[r0:r0 + 64, :].rearrange("r (a b) -> r a b", b=64),
                          self.zero[0:64, :].unsqueeze(1).to_broadcast([64, c.DDN // 64, 64]),
                          reads=(self.zero.R,), writes=(), sem="Z")

        self.base_ptr = self.aptr
        self.phase_p0(x_in, meta_in)
        S.barrier()
        for l in range(c.DEPTH):
            if self.stop_after == "ffn":
                self.aptr = self.base_ptr
                self.phase_ffn_only(l)
                S.barrier()
                continue
            self.aptr = self.base_ptr
            self.phase_A(l)
            S.barrier()
            self.aptr = self.base_ptr
            self.phase_B(l)
            S.barrier()
            if self.stop_after != "nodn":
                self.aptr = self.base_ptr
                self.phase_C(l)
                S.barrier()
            self.aptr = self.base_ptr
            self.phase_D(l)
            S.barrier()
        self.aptr = self.base_ptr
        self.phase_pf(out_d)
        S.barrier()
        S.emit()
        return nc

    def alloc_token_tiles(self):
        c = self.c
        KD, KF, TW, D = c.KD, c.KF, c.TW, c.D
        self.hbuf = [self.tile("h0", [128, KD, TW]), self.tile("h1", [128, KD, TW])]
        self.h = self.hbuf[0]
        self.hidx = 0
        self.hn = self.tile("hn", [128, KD, TW], BF16)
        self.rstd = self.tile("rstd", [128, TW])
        self.y = self.tile("y", [128, KD, TW])
        DFc, SCc, H = c.DF // 128, c.DSC // 128, c.H
        gcols = (max(KF, KD) * TW + 1) // 2
        ovcols = ((DFc + SCc + H) * TW + 1) // 2 + SCc * TW + 32
        self.g_off = self.aptr
        self.g = self.tile("g", [128, max(KF, KD), TW], BF16)
        self.aptr = self.g_off + max(gcols, ovcols) + 8
        assert self.aptr <= self.ARENA, ("SBUF arena overflow", "g", self.aptr)
        self.sq = Tile("sq", self.g.ap)
        self.sq.R = self.g.R
        self.woslots = RR([self.tile("wo%d" % i, [128, KF, 128], BF16) for i in range(3)])
        self.wslots = RR([self.tile("ws%d" % i, [128, KD, 512], BF16) for i in range(4)])
        self.sa = RR([self.tile("sa%d" % i, [128, 512]) for i in range(2)])
        self.tmpf = RR([self.tile("tmpf%d" % i, [128, 512]) for i in range(2)])

    def phase_p0(self, x_in, meta_in):
        c = self.c
        KD = c.KD
        xin = RR([self.tile("xin%d" % i, [128, c.D]) for i in range(2)])
        hst = RR([self.tile("hst%d" % i, [128, KD, 128]) for i in range(2)])
        blocks = [(i * 128, 128) for i in range(c.SEQ // 128)] + [(c.SEQ, 16)]
        for (t0, n) in blocks:
            xt = xin.next()
            src = x_in[t0:t0 + n, :] if t0 < c.SEQ else meta_in
            self.load(xt, xt[0:n, :], src)
            st = hst.next()
            for k0 in range(0, KD, 4):
                ps = self.banks.next()
                kn = min(4, KD - k0)
                for kk in range(kn):
                    kc = k0 + kk
                    self.S.op("pe", lambda e, ps=ps, kk=kk, xt=xt, kc=kc, n=n: e.transpose(
                        out=ps[:, kk * 128:kk * 128 + n], in_=xt[0:n, kc * 128:(kc + 1) * 128], identity=self.id32[0:n, 0:n]),
                        reads=(xt.R, self.id32.R), writes=(ps.R,))
                self.copy(st[:, k0:k0 + kn, 0:n], ps[:, 0:kn * 128].rearrange("p (k t) -> p k t", t=128)[:, :, 0:n],
                          reads=(ps.R,), writes=(st.R,))
            self.store(st, self.h_scr.rearrange("(k p) t -> p k t", p=128)[:, :, t0:t0 + n], st[:, :, 0:n])

    def phase_pf(self, out_d):
        c = self.c
        KD = c.KD
        hin = RR([self.tile("hin%d" % i, [128, KD, 128]) for i in range(2)])
        ost = RR([self.tile("ost%d" % i, [128, c.D]) for i in range(2)])
        for b in range(c.SEQ // 128):
            t0 = b * 128
            ht = hin.next()
            self.load(ht, ht[:], self.h_scr.rearrange("(k p) t -> p k t", p=128)[:, :, t0:t0 + 128])
            ot = ost.next()
            for k0 in range(0, KD, 4):
                ps = self.banks.next()
                kn = min(4, KD - k0)
                for kk in range(kn):
                    self.S.op("pe", lambda e, ps=ps, kk=kk, ht=ht, kc=k0 + kk: e.transpose(
                        out=ps[:, kk * 128:(kk + 1) * 128], in_=ht[:, kc, :], identity=self.id32[:]),
                        reads=(ht.R, self.id32.R), writes=(ps.R,))
                self.copy(ot[:, k0 * 128:(k0 + kn) * 128], ps[:, 0:kn * 128], reads=(ps.R,), writes=(ot.R,))
            self.store(ot, out_d[t0:t0 + 128, :], ot[:], sem="OUT")

    def gcol(self, l, s, kc):
        c = self.c
        i = (l * 6 + s) * c.KD + kc
        return self.gains[:, i:i + 1]

    def load_h(self, tile_slices):
        self.hidx += 1
        self.h = self.hbuf[self.hidx % 2]
        off = 0
        hv = self.h_scr.rearrange("(k p) t -> p k t", p=128)
        for (t0, n) in tile_slices:
            self.load(self.h, self.h[:, :, off:off + n], hv[:, :, t0:t0 + n])
            off += n

    def store_h(self, tile_slices):
        off = 0
        hv = self.h_scr.rearrange("(k p) t -> p k t", p=128)
        for (t0, n) in tile_slices:
            self.store(self.h, hv[:, :, t0:t0 + n], self.h[:, :, off:off + n])
            off += n

    def slices_off(self, tile_slices):
        out = []
        off = 0
        for (t0, n) in tile_slices:
            out.append((off, n, t0))
            off += n
        return out, off

    def stats_rstd(self, src, nk, W, slo, scale):
        self.act(self.sq[:, 0:nk, 0:W], src[:, 0:nk, 0:W], AF.Square, reads=(src.R,), writes=(self.sq.R,))
        for (off, n, _) in slo:
            ps = self.banks.next()
            for kc in range(nk):
                self.mm(ps, ps[:, 0:n], self.onesb[:], self.sq[:, kc, off:off + n], kc == 0, kc == nk - 1,
                        reads=(self.onesb.R, self.sq.R))
            self.act(self.rstd[:, off:off + n], ps[:, 0:n], AF.Sqrt, reads=(ps.R, self.epsb.R), writes=(self.rstd.R,),
                     bias=self.epsb[:], scale=scale)
        self.recip(self.rstd[:, 0:W], self.rstd[:, 0:W], reads=(self.rstd.R,), writes=(self.rstd.R,))

    def prenorm(self, l, s, W, slo):
        c = self.c
        self.stats_rstd(self.h, c.KD, W, slo, 1.0 / c.D)
        for kc in range(c.KD):
            self.stt(self.hn[:, kc, 0:W], self.h[:, kc, 0:W], self.gcol(l, s, kc), self.rstd[:, 0:W], ALU.mult, ALU.mult,
                     reads=(self.h.R, self.gains.R, self.rstd.R), writes=(self.hn.R,))

    def postnorm_add(self, l, s, W, slo, scale):
        c = self.c
        self.stats_rstd(self.y, c.KD, W, slo, 1.0 / c.D)
        for kc in range(c.KD):
            tmp = self.y
            self.stt(self.y[:, kc, 0:W], self.y[:, kc, 0:W], self.gcol(l, s, kc), self.rstd[:, 0:W], ALU.mult, ALU.mult,
                     reads=(self.y.R, self.gains.R, self.rstd.R), writes=(self.y.R,))
            self.stt(self.h[:, kc, 0:W], self.y[:, kc, 0:W], float(scale), self.h[:, kc, 0:W], ALU.mult, ALU.add,
                     reads=(self.y.R, self.h.R), writes=(self.h.R,))

    def wload(self, key, l, rows_kd, c0, cn):
        ws = self.wslots.next()
        er = [self.WR[(key, l)]]
        for si, (s0, sn, gs) in enumerate(self.segs[key]):
            if s0 <= c0 < s0 + sn and (key, l, (c0 - s0) // gs) in self.WR:
                er.append(self.WR[(key, l, (c0 - s0) // gs)])
        self.load(ws, ws[:, 0:rows_kd, 0:cn], self.wsrc(key, l, c0, cn), extra_reads=tuple(er))
        return ws

    def wsrc(self, key, l, c0, cn):
        for si, (s0, sn, gs) in enumerate(self.segs[key]):
            if s0 <= c0 < s0 + sn:
                break
        assert (c0 - s0) % gs == 0 and cn <= gs, (key, c0, cn, gs)
        return self.Wg[(key, si)][l, :, (c0 - s0) // gs, :, 0:cn]

    def ffn(self, l, which, slo, W):
        c = self.c
        KD, KF, DFF = c.KD, c.KF, c.DFF
        wi, wo = ("ffn1_wi", "ffn1_wo") if which == 1 else ("ffn2_wi", "ffn2_wo")
        s_pre, s_post = (0, 1) if which == 1 else (4, 5)
        self.prenorm(l, s_pre, W, slo)
        G = c.GS // 128
        for g0 in range(0, KF, G):
            gn = min(G, KF - g0)
            wa = self.wload(wi, l, KD, g0 * 128, gn * 128)
            wb = self.wload(wi, l, KD, DFF + g0 * 128, gn * 128)
            for j in range(gn):
                oc = g0 + j
                for (off, n, _) in slo:
                    pa = self.banks.next()
                    pb = self.banks.next()
                    for kc in range(KD):
                        self.mm(pa, pa[:, 0:n], wa[:, kc, j * 128:(j + 1) * 128], self.hn[:, kc, off:off + n],
                                kc == 0, kc == KD - 1, reads=(wa.R, self.hn.R))
                    for kc in range(KD):
                        self.mm(pb, pb[:, 0:n], wb[:, kc, j * 128:(j + 1) * 128], self.hn[:, kc, off:off + n],
                                kc == 0, kc == KD - 1, reads=(wb.R, self.hn.R))
                    sa = self.sa.next()
                    self.act(sa[:, 0:n], pa[:, 0:n], AF.Silu, reads=(pa.R,), writes=(sa.R,))
                    self.tt(self.g[:, oc, off:off + n], sa[:, 0:n], pb[:, 0:n], ALU.mult, reads=(sa.R, pb.R), writes=(self.g.R,))
        for oc0 in range(0, KD, 1):
            ocn = 1
            wt = self.woslots.next()
            self.load(wt, wt[:, :, 0:ocn * 128], self.wsrc(wo, l, oc0 * 128, ocn * 128), extra_reads=(self.WR[(wo, l)],))
            for q in range(ocn):
                oc = oc0 + q
                for (off, n, _) in slo:
                    ps = self.banks.next()
                    for kc in range(KF):
                        self.mm(ps, ps[:, 0:n], wt[:, kc, q * 128:(q + 1) * 128], self.g[:, kc, off:off + n],
                                kc == 0, kc == KF - 1, reads=(wt.R, self.g.R))
                    self.copy(self.y[:, oc, off:off + n], ps[:, 0:n], reads=(ps.R,), writes=(self.y.R,))
        self.postnorm_add(l, s_post, W, slo, 0.5)

    def load_wo(self, l, which):
        return

    def phase_ffn_only(self, l):
        self.alloc_token_tiles()
        for which in (1, 2):
            self.load_wo(l, which)
            for tl in self.c.tiles:
                slo, W = self.slices_off(tl)
                self.load_h(tl)
                self.ffn(l, which, slo, W)
                self.store_h(tl)
            self.S.barrier()


    def linear(self, key, l, c0, ncols, kd, rhs, slo, evac):
        noc = ncols // 128
        G = self.c.GS // 128
        for g0 in range(0, noc, G):
            gn = min(G, noc - g0)
            ws = self.wload(key, l, kd, c0 + g0 * 128, gn * 128)
            for j in range(gn):
                for (off, n, _) in slo:
                    ps = self.banks.next()
                    for kc in range(kd):
                        self.mm(ps, ps[:, 0:n], ws[:, kc, j * 128:(j + 1) * 128], rhs[:, kc, off:off + n],
                                kc == 0, kc == kd - 1, reads=(ws.R, rhs.R))
                    evac(g0 + j, off, n, ps)

    def phase_A(self, l):
        c = self.c
        S = self.S
        KD, H, DF, DSC, DDN, TW, L = c.KD, c.H, c.DF, c.DSC, c.DDN, c.TW, c.L
        DFc, SCc = DF // 128, DSC // 128
        self.alloc_token_tiles()
        uf = self.tile("uf", [128, DFc, TW], BF16)
        ucs_st = RR([self.tile("ucs_st%d" % i, [128, 2, DF], BF16) for i in range(2)])
        st = RR([self.tile("st%d" % i, [128, 4, TW]) for i in range(2)])
        wab = self.tile("wab", [128, KD, 4 * H], BF16)
        bgst = RR([self.tile("bgst%d" % i, [128, 4 * H]) for i in range(2)])
        xa = self.tile("xa", [128, 2 * H])
        ta = self.tile("ta", [128, 2 * H])
        self.load(wab, wab[:], self.wab_scr[l].rearrange("(k p) n -> p k n", p=128), extra_reads=(self.WR[("w_in", l)],))
        self.load_wo(l, 1)
        chv = self.CH.rearrange("(k p) t -> p k t", p=128)
        bbv = self.BB.rearrange("(k p) t -> p k t", p=128)
        qkvv = self.QKV.rearrange("(k p) t -> p k t", p=128)
        for tl in c.tiles:
            slo, W = self.slices_off(tl)
            self.load_h(tl)
            self.ffn(l, 1, slo, W)
            self.store_h(tl)
            self.prenorm(l, 2, W, slo)
            hn = self.hn
            self.linear("w_in", l, c.o_f, DF, KD, hn, slo,
                        lambda oc, off, n, ps: self.copy(uf[:, oc, off:off + n], ps[:, 0:n], reads=(ps.R,), writes=(uf.R,)))
            for (off, n, t0) in slo:
                for b0 in range(0, n, 128):
                    bn = min(128, n - b0)
                    us = ucs_st.next()
                    for fp in range(0, DFc, 2):
                        npair = min(2, DFc - fp)
                        ps = self.banks.next()
                        for q in range(npair):
                            self.mm(ps, ps[0:bn, q * 256:(q + 1) * 256], uf[:, fp + q, off + b0:off + b0 + bn], self.cs64[:],
                                    True, True, reads=(uf.R, self.cs64.R))
                        self.copy(us[0:bn, :, fp * 128:(fp + npair) * 128].rearrange("p c (f k) -> p c f k", k=128),
                                  ps[0:bn, 0:npair * 256].rearrange("p (f c k) -> p c f k", c=2, k=128),
                                  reads=(ps.R,), writes=(us.R,))
                    self.store(us, self.UCS[t0 + b0:t0 + b0 + bn, :], us[0:bn, :, :].rearrange("p c f -> p (c f)"))
            stb = st.next()
            self.linear("w_in", l, c.o_sc, DSC, KD, hn, slo,
                        lambda oc, off, n, ps: self.copy(stb[:, oc, off:off + n], ps[:, 0:n], reads=(ps.R,), writes=(stb.R,)))
            for (off, n, t0) in slo:
                self.store(stb, bbv[:, :, t0:t0 + n], stb[:, 0:SCc, off:off + n])
            stc = st.next()
            wc = self.wload("w_in", l, KD, c.o_sc + DSC, DSC)
            wh = self.wload("w_in", l, KD, c.o_sc + 2 * DSC, DSC)
            for j in range(SCc):
                for (off, n, _) in slo:
                    pc = self.banks.next()
                    ph = self.banks.next()
                    for kc in range(KD):
                        self.mm(pc, pc[:, 0:n], wc[:, kc, j * 128:(j + 1) * 128], hn[:, kc, off:off + n], kc == 0, kc == KD - 1,
                                reads=(wc.R, hn.R))
                    for kc in range(KD):
                        self.mm(ph, ph[:, 0:n], wh[:, kc, j * 128:(j + 1) * 128], hn[:, kc, off:off + n], kc == 0, kc == KD - 1,
                                reads=(wh.R, hn.R))
                    tf = self.tmpf.next()
                    self.copy(tf[:, 0:n], pc[:, 0:n], reads=(pc.R,), writes=(tf.R,), eng="act")
                    self.tt(stc[:, j, off:off + n], tf[:, 0:n], ph[:, 0:n], ALU.mult, reads=(tf.R, ph.R), writes=(stc.R,))
            for (off, n, t0) in slo:
                pc0 = self.pcol(t0)
                self.store(stc, chv[:, :, pc0:pc0 + n], stc[:, 0:SCc, off:off + n])
            for g0 in range(0, 3 * H, 4):
                gn = min(4, 3 * H - g0)
                stq = st.next()
                self.linear("w_in", l, c.o_qkv + g0 * 128, gn * 128, KD, hn, slo,
                            lambda oc, off, n, ps, stq=stq: self.copy(stq[:, oc, off:off + n], ps[:, 0:n], reads=(ps.R,), writes=(stq.R,)))
                for (off, n, t0) in slo:
                    pc0 = self.pcol(t0)
                    self.store(stq, qkvv[:, g0:g0 + gn, pc0:pc0 + n], stq[:, 0:gn, off:off + n])
            H2 = 2 * H
            for (off, n, t0) in slo:
                for b0 in range(0, n, 128):
                    bn = min(128, n - b0)
                    ps = self.banks.next()
                    for kc in range(KD):
                        self.mm(ps, ps[0:bn, 0:4 * H], hn[:, kc, off + b0:off + b0 + bn], wab[:, kc, :], kc == 0, kc == KD - 1,
                                reads=(hn.R, wab.R))
                    bg = bgst.next()
                    self.act(bg[0:bn, 0:H2], ps[0:bn, 0:H2], AF.Sigmoid, reads=(ps.R,), writes=(bg.R,))
                    self.tt(xa[0:bn, :], ps[0:bn, H2:2 * H2], self.dtb[0:bn, l * H2:(l + 1) * H2], ALU.add,
                            reads=(ps.R, self.dtb.R), writes=(xa.R,))
                    self.ts(ta[0:bn, :], xa[0:bn, :], 30.0, None, ALU.min, None, reads=(xa.R,), writes=(ta.R,))
                    self.tt(xa[0:bn, :], xa[0:bn, :], ta[0:bn, :], ALU.subtract, reads=(xa.R, ta.R), writes=(xa.R,))
                    self.act(ta[0:bn, :], ta[0:bn, :], AF.Exp, reads=(ta.R,), writes=(ta.R,))
                    self.act(ta[0:bn, :], ta[0:bn, :], AF.Ln, reads=(ta.R, self.onef.R), writes=(ta.R,), bias=self.onef[0:bn, :])
                    self.tt(xa[0:bn, :], xa[0:bn, :], ta[0:bn, :], ALU.add, reads=(xa.R, ta.R), writes=(xa.R,))
                    self.tt(bg[0:bn, H2:2 * H2], xa[0:bn, :], self.negA[0:bn, l * H2:(l + 1) * H2], ALU.mult,
                            reads=(xa.R, self.negA.R), writes=(bg.R,))
                    tau0 = t0 + b0
                    if tau0 >= c.SEQ:
                        self.store(bg, self.BG[48:64, 0, :], bg[0:16, :])
                    else:
                        ch0 = 1 + tau0 // 64
                        self.store(bg, self.BG[:, ch0, :], bg[0:64, :])
                        self.store(bg, self.BG[:, ch0 + 1, :], bg[64:128, :])

    def phase_B(self, l):
        c = self.c
        DF, NB, L = c.DF, c.NB, c.L
        DFc = DF // 128
        ucs = self.tile("ucs", [128, NB, 2, DF], BF16)
        uv = self.UCS.rearrange("(b p) c -> p b c", p=128)
        NBF = L // 128
        for b0 in range(0, NBF, 8):
            bn = min(8, NBF - b0)
            self.load(ucs, ucs[:, b0:b0 + bn, :, :].rearrange("p b c f -> p b (c f)"), uv[:, b0:b0 + bn, :])
        if NBF < NB:
            rn = L - NBF * 128
            self.load(ucs, ucs[0:rn, NBF, :, :].rearrange("p c f -> p (c f)"), self.UCS[NBF * 128:NBF * 128 + rn, :])
        GB = 4
        dft = RR([self.tile("dft%d" % i, [128, GB, 2, c.SW], BF16) for i in range(6)])
        sidx = 0
        yst = RR([self.tile("yst%d" % i, [128, DFc, 512], BF16) for i in range(2)])
        yfv = self.YF.rearrange("(k p) t -> p k t", p=128)
        for tl in c.tiles:
            for (t0, n) in tl:
                pss = [self.banks.next() for _ in range(DFc)]
                for i0 in range(0, NB, GB):
                    ni = min(GB, NB - i0)
                    dt_ = dft.next()
                    self.load(dt_, dt_[:, 0:ni, :, :], self.cls_in[sidx, :, i0:i0 + ni, :, :])
                    for fc in range(DFc):
                        for i in range(ni):
                            blk = i0 + i
                            rn = min(128, L - blk * 128)
                            for cs in range(2):
                                first = (blk == 0 and cs == 0)
                                last = (blk == NB - 1 and cs == 1)
                                self.mm(pss[fc], pss[fc][:, 0:n], ucs[0:rn, blk, cs, fc * 128:(fc + 1) * 128], dt_[0:rn, i, cs, 0:n],
                                        first, last, reads=(ucs.R, dt_.R))
                sidx += 1
                ys = yst.next()
                for fc in range(DFc):
                    self.copy(ys[:, fc, 0:n], pss[fc][:, 0:n], reads=(pss[fc].R,), writes=(ys.R,))
                self.store(ys, yfv[:, :, t0:t0 + n], ys[:, :, 0:n])

    def phase_D(self, l):
        c = self.c
        KD, H, DF, DSC, DDN, TW, L, D = c.KD, c.H, c.DF, c.DSC, c.DDN, c.TW, c.L, c.D
        DFc, SCc = DF // 128, DSC // 128
        self.alloc_token_tiles()
        yf = self.tile("yf", [128, DFc, TW], BF16, at=self.g_off)
        scin = self.tile("scin", [128, SCc, TW], BF16, at=self.last_end)
        dnin = self.tile("dnin", [128, H, TW], BF16, at=self.last_end)
        bb = self.tile("bb", [128, SCc, TW], at=self.last_end)
        for t_ in (yf, scin, dnin, bb):
            t_.R = self.g.R
        chw = RR([self.tile("chw%d" % i, [128, SCc, 514]) for i in range(1)])
        wbr = [self.tile("wbr%d" % i, [128, kk, D], BF16) for i, kk in enumerate((DFc, SCc, H))]
        wz = self.tile("wz", [128, KD, DDN], BF16)
        oft = RR([self.tile("oft%d" % i, [128, DDN]) for i in range(2)])
        obt = RR([self.tile("obt%d" % i, [128, DDN]) for i in range(2)])
        zs = self.tile("zs", [128, DDN])
        dtm = self.tile("dtm", [128, DDN], BF16)
        ss = self.tile("ss", [128, H])
        junk = self.tile("junk", [128, 128])
        for i, key in enumerate(("w_fourier", "w_sconv_out", "w_dn_out")):
            self.load(wbr[i], wbr[i][:], self.Wb[key][l].rearrange("(k p) n -> p k n", p=128), extra_reads=(self.WR[(key, l)],))
        for z0 in range(0, DDN, c.GS):
            self.load(wz, wz[:, :, z0:z0 + c.GS], self.wsrc("w_in", l, c.o_z + z0, c.GS), extra_reads=(self.WR[("w_in", l)],))
        self.load_wo(l, 2)
        yfv = self.YF.rearrange("(k p) t -> p k t", p=128)
        chv = self.CH.rearrange("(k p) t -> p k t", p=128)
        bbv = self.BB.rearrange("(k p) t -> p k t", p=128)
        nsc = c.DEPTH * 3 * SCc

        def scw(tap, j):
            i = (l * 3 + tap) * SCc + j
            return self.sconv[:, i:i + 1]
        for tl in c.tiles:
            slo, W = self.slices_off(tl)
            self.load_h(tl)
            self.prenorm(l, 2, W, slo)
            hn = self.hn
            blocks = []
            for (off, n, t0) in slo:
                self.load(yf, yf[:, :, off:off + n], yfv[:, :, t0:t0 + n])
                self.load(bb, bb[:, :, off:off + n], bbv[:, :, t0:t0 + n])
                cw = chw.next()
                pc0 = self.pcol(t0)
                self.load(cw, cw[:, :, 0:n + 2], chv[:, :, pc0 - 1:pc0 + n + 1])
                for j in range(SCc):
                    tf = self.tmpf.next()
                    self.ts(tf[:, 0:n], cw[:, j, 1:n + 1], scw(1, j), None, ALU.mult, None, reads=(cw.R, self.sconv.R), writes=(tf.R,))
                    self.stt(tf[:, 0:n], cw[:, j, 0:n], scw(0, j), tf[:, 0:n], ALU.mult, ALU.add,
                             reads=(cw.R, self.sconv.R, tf.R), writes=(tf.R,))
                    self.stt(tf[:, 0:n], cw[:, j, 2:n + 2], scw(2, j), tf[:, 0:n], ALU.mult, ALU.add,
                             reads=(cw.R, self.sconv.R, tf.R), writes=(tf.R,))
                    self.tt(scin[:, j, off:off + n], tf[:, 0:n], bb[:, j, off:off + n], ALU.mult, reads=(tf.R, bb.R), writes=(scin.R,))
                for b0 in range(0, n, 128):
                    blocks.append((off, b0, min(128, n - b0), t0))

            def dn_block(off, b0, bn, t0):
                p0 = self.ppos(t0 + b0)
                of_, ob_ = oft.next(), obt.next()
                self.load(of_, of_[0:bn, :], self.OF[p0:p0 + bn, :])
                self.load(ob_, ob_[0:bn, :], self.OB[p0:p0 + bn, :])
                self.tt(of_[0:bn, :], of_[0:bn, :], ob_[0:bn, :], ALU.add, reads=(of_.R, ob_.R), writes=(of_.R,))
                pz = self.banks.next()
                for kc in range(KD):
                    self.mm(pz, pz[0:bn, 0:DDN], hn[:, kc, off + b0:off + b0 + bn], wz[:, kc, :], kc == 0, kc == KD - 1,
                            reads=(hn.R, wz.R))
                self.act(zs[0:bn, :], pz[0:bn, 0:DDN], AF.Silu, reads=(pz.R,), writes=(zs.R,))
                for hh in range(H):
                    self.S.op("act", lambda e, hh=hh, of_=of_, bn=bn: e.activation(
                        out=junk[0:bn, :], in_=of_[0:bn, hh * 128:(hh + 1) * 128], func=AF.Square, accum_out=ss[0:bn, hh:hh + 1]),
                        reads=(of_.R,), writes=(junk.R, ss.R))
                self.act(ss[0:bn, :], ss[0:bn, :], AF.Sqrt, reads=(ss.R, self.epsb.R), writes=(ss.R,), bias=self.epsb[0:bn, :],
                         scale=1.0 / 128.0)
                self.recip(ss[0:bn, :], ss[0:bn, :], reads=(ss.R,), writes=(ss.R,))
                o3 = of_[0:bn, :].rearrange("p (h d) -> p h d", d=128)
                self.tt(o3, o3, ss[0:bn, :].unsqueeze(2).to_broadcast([bn, H, 128]), ALU.mult, reads=(of_.R, ss.R), writes=(of_.R,))
                self.tt(of_[0:bn, :], of_[0:bn, :], self.dnnorm[0:bn, l * DDN:(l + 1) * DDN], ALU.mult,
                        reads=(of_.R, self.dnnorm.R), writes=(of_.R,))
                self.tt(dtm[0:bn, :], of_[0:bn, :], zs[0:bn, :], ALU.mult, reads=(of_.R, zs.R), writes=(dtm.R,))
                yield
                pt = self.banks.next()
                for hh in range(H):
                    self.mm(pt, pt[:, hh * 128:hh * 128 + bn], dtm[0:bn, hh * 128:(hh + 1) * 128], self.idbf[0:bn, 0:bn], True, True,
                            reads=(dtm.R, self.idbf.R))
                self.copy(dnin[:, :, off + b0:off + b0 + bn], pt[:, 0:H * 128].rearrange("p (h t) -> p h t", t=128)[:, :, 0:bn],
                          reads=(pt.R,), writes=(dnin.R,))

            brin = (yf, scin, dnin)
            brk = (DFc, SCc, H)
            G = c.GS // 128

            def gate_unit(br, g0):
                gn = min(G, KD - g0)
                ws = self.wload("w_in", l, KD, c.o_g + br * D + g0 * 128, gn * 128)
                for j in range(gn):
                    oc = g0 + j
                    for (off, n, _) in slo:
                        pg = self.banks.next()
                        py = self.banks.next()
                        for kc in range(KD):
                            self.mm(pg, pg[:, 0:n], ws[:, kc, j * 128:(j + 1) * 128], hn[:, kc, off:off + n], kc == 0, kc == KD - 1,
                                    reads=(ws.R, hn.R))
                        for kb in range(brk[br]):
                            self.mm(py, py[:, 0:n], wbr[br][:, kb, oc * 128:(oc + 1) * 128], brin[br][:, kb, off:off + n],
                                    kb == 0, kb == brk[br] - 1, reads=(wbr[br].R, brin[br].R))
                        sg = self.sa.next()
                        self.act(sg[:, 0:n], pg[:, 0:n], AF.Sigmoid, reads=(pg.R,), writes=(sg.R,))
                        if br == 0:
                            self.tt(self.y[:, oc, off:off + n], sg[:, 0:n], py[:, 0:n], ALU.mult, reads=(sg.R, py.R), writes=(self.y.R,))
                        else:
                            self.tt(sg[:, 0:n], sg[:, 0:n], py[:, 0:n], ALU.mult, reads=(sg.R, py.R), writes=(sg.R,))
                            self.tt(self.y[:, oc, off:off + n], self.y[:, oc, off:off + n], sg[:, 0:n], ALU.add,
                                    reads=(self.y.R, sg.R), writes=(self.y.R,))

            units01 = [(br, g0) for br in (0, 1) for g0 in range(0, KD, G)]
            ui = 0
            for blk in blocks:
                gd = dn_block(*blk)
                next(gd)
                if ui < len(units01):
                    gate_unit(*units01[ui])
                    ui += 1
                for _ in gd:
                    pass
            while ui < len(units01):
                gate_unit(*units01[ui])
                ui += 1
            for g0 in range(0, KD, G):
                gate_unit(2, g0)
            self.copy(hn[:, :, 0:W], self.y[:, :, 0:W], reads=(self.y.R,), writes=(hn.R,))
            self.linear("w_out", l, 0, D, KD, hn, slo,
                        lambda oc, off, n, ps: self.copy(self.y[:, oc, off:off + n], ps[:, 0:n], reads=(ps.R,), writes=(self.y.R,)))
            self.postnorm_add(l, 3, W, slo, 1.0)
            self.ffn(l, 2, slo, W)
            self.store_h(tl)


    def phase_C(self, l):
        c = self.c
        S = self.S
        H, LP, NCH, L = c.H, c.LP, c.NCH, c.L
        H3, DH = 3 * H, 2 * H
        QT = self.tile("QT", [128, H, LP], BF16)
        KT = self.tile("KT", [128, H, LP], BF16)
        VT = self.tile("VT", [128, H, LP], BF16)
        bg = self.tile("bg", [128, NCH, 4 * H])
        bgs = self.tile("bgs", [128, NCH, 3, DH])
        self.load(bg, bg[0:64, :, :], self.BG)
        for T_ in (QT, KT, VT):
            self.memset(T_, T_[:, :, 0:48], 0.0, eng="pool")
        self.copy(bgs[0:64, :, 0, 0:H], bg[0:64, :, 0:H], reads=(bg.R,), writes=(bgs.R,), eng="pool")
        self.copy(bgs[0:64, :, 1, 0:H], bg[0:64, :, 2 * H:3 * H], reads=(bg.R,), writes=(bgs.R,), eng="pool")
        for j in range(NCH):
            cb = NCH - 1 - j
            self.copy(bgs[0:64, j, 0:2, H:DH], bg[0:64, cb, :].rearrange("p (q d h) -> p q d h", q=2, d=2)[:, :, 1, :],
                      reads=(bg.R,), writes=(bgs.R,), eng="pool")
        self.ts(bgs[0:64, :, 2, :], bgs[0:64, :, 0, :], -1.0, None, ALU.mult, None, reads=(bgs.R,), writes=(bgs.R,), eng="pool")
        mark = self.aptr
        WW = min(512, c.SEQ)
        raw = RR([self.tile("raw%d" % i, [128, H3, WW + 2]) for i in range(2)])
        acc = self.tile("acc", [128, H3, WW])
        sqq = RR([self.tile("sqq%d" % i, [128, WW], BF16) for i in range(2)])
        rsq = RR([self.tile("rsq%d" % i, [128, WW]) for i in range(2)])
        qkvv = self.QKV.rearrange("(k p) t -> p k t", p=128)
        self.banks = RR([self.bank(i) for i in range(8)])

        def cwt(tap, j):
            i = (l * 3 + tap) * H3 + j
            return self.dnconv[:, i:i + 1]
        wins = [(t0, WW) for t0 in range(0, c.SEQ, WW)] + [(c.SEQ, 16)]
        for (t0, n) in wins:
            rw = raw.next()
            pc0, pp0 = self.pcol(t0), self.ppos(t0)
            self.load(rw, rw[:, :, 0:n + 2], qkvv[:, :, pc0 - 1:pc0 + n + 1])
            for j in range(H3):
                eng = "dve"
                self.act(acc[:, j, 0:n], rw[:, j, 1:n + 1], AF.Copy, reads=(rw.R, self.dnconv.R), writes=(acc.R,), scale=cwt(1, j))
                self.stt(acc[:, j, 0:n], rw[:, j, 0:n], cwt(0, j), acc[:, j, 0:n], ALU.mult, ALU.add,
                         reads=(rw.R, self.dnconv.R, acc.R), writes=(acc.R,), eng=eng)
                self.stt(acc[:, j, 0:n], rw[:, j, 2:n + 2], cwt(2, j), acc[:, j, 0:n], ALU.mult, ALU.add,
                         reads=(rw.R, self.dnconv.R, acc.R), writes=(acc.R,), eng=eng)
            self.act(acc[:, :, 0:n], acc[:, :, 0:n], AF.Silu, reads=(acc.R,), writes=(acc.R,))
            for j in range(2 * H):
                sq_, rs_ = sqq.next(), rsq.next()
                self.act(sq_[:, 0:n], acc[:, j, 0:n], AF.Square, reads=(acc.R,), writes=(sq_.R,))
                ps = self.banks.next()
                self.mm(ps, ps[:, 0:n], self.onesb[:], sq_[:, 0:n], True, True, reads=(self.onesb.R, sq_.R))
                self.act(rs_[:, 0:n], ps[:, 0:n], AF.Sqrt, reads=(ps.R, self.epsb.R), writes=(rs_.R,), bias=self.epsb[:])
                self.recip(rs_[:, 0:n], rs_[:, 0:n], reads=(rs_.R,), writes=(rs_.R,))
                if j < H:
                    self.stt(QT[:, j, pp0:pp0 + n], acc[:, j, 0:n], float(128 ** -0.5), rs_[:, 0:n], ALU.mult, ALU.mult,
                             reads=(acc.R, rs_.R), writes=(QT.R,))
                else:
                    self.tt(KT[:, j - H, pp0:pp0 + n], acc[:, j, 0:n], rs_[:, 0:n], ALU.mult, reads=(acc.R, rs_.R), writes=(KT.R,))
            self.copy(VT[:, :, pp0:pp0 + n], acc[:, 2 * H:3 * H, 0:n], reads=(acc.R,), writes=(VT.R,), eng="pool")
        S.barrier()
        self.aptr = mark
        dm = self.tile("dm", [128, 2 * 2 * 64])
        self.load(dm, dm[:], self.dmask_in[:, 0:256])
        L1 = dm[:, 0:128].rearrange("p (d s) -> p d s", s=64)
        L2 = dm[:, 128:256].rearrange("p (d s) -> p d s", s=64)
        ds_ = self.tile("ds", [128, 2 * 64 * 4 + 128])
        self.load(ds_, ds_[0:64, :], self.dsmall_in)
        U1 = ds_[0:64, 0:128].rearrange("p (d s) -> p d s", s=64)
        U2 = ds_[0:64, 128:256].rearrange("p (d s) -> p d s", s=64)
        TC = ds_[0:64, 256:384].rearrange("p (d s) -> p d s", s=64)
        TN = ds_[0:64, 384:512].rearrange("p (d s) -> p d s", s=64)
        ON = ds_[0:64, 512:640]
        GU = []
        for v in range(2):
            pair = []
            for i in range(2):
                t_ = self.tile("GU%d_%d" % (v, i), [128, DH, 64])
                o0 = 256 + v * (2 * H * 64)
                self.load(t_, t_[64:128, :, :].rearrange("p a s -> p (a s)"), self.dmask_in[64:128, o0:o0 + DH * 64])
                pair.append(t_)
            GU.append(RR(pair))
        Sf = self.tile("Sf", [128, DH, 128])
        Sb = self.tile("Sb", [128, DH, 128], BF16)
        Stmp = self.tile("Stmp", [128, DH, 128])
        self.memset(Sf, Sf[:], 0.0)
        self.memset(Sb, Sb[:], 0.0)
        E12 = RR([self.tile("E12_%d" % i, [128, 2, DH, 64]) for i in range(2)])
        ex = RR([self.tile("ex%d" % i, [128, 3, DH]) for i in range(3)])
        egl = RR([self.tile("egl%d" % i, [128, DH]) for i in range(3)])
        cf0 = RR([self.tile("cf0_%d" % i, [128, DH]) for i in range(2)])
        kbg = RR([self.tile("kbg%d" % i, [128, DH, 128], BF16) for i in range(2)])
        kd = RR([self.tile("kd%d" % i, [128, DH, 128], BF16) for i in range(3)])
        vb = RR([self.tile("vb%d" % i, [128, DH, 128], BF16) for i in range(2)])
        tmpNs = [self.tile("tmpN%d" % i, [128, DH, 64]) for i in range(2)]
        Abufs = [[self.tile("A%d_%d" % (q, i), [128, DH, 64], BF16) for i in range(2)] for q in range(2)]
        BPbufs = [[self.tile("BP%d_%d" % (q, i), [128, DH, 2, 64], BF16) for i in range(2)] for q in range(2)]
        attnT = RR([self.tile("attnT%d" % i, [128, DH, 64], BF16) for i in range(3)])
        u_ = RR([self.tile("u%d" % i, [128, DH, 128]) for i in range(3)])
        wT = RR([self.tile("wT%d" % i, [128, DH, 64], BF16) for i in range(3)])
        vnew = RR([self.tile("vnew%d" % i, [128, DH, 128], BF16) for i in range(2)])
        ost = RR([self.tile("ost%d" % i, [128, DH, 128]) for i in range(2)])
        slots = RR([self.bank(0, 2), self.bank(2, 2), self.bank(4, 2), self.bank(6, 2)])
        idb, id32 = self.idbf, self.id32
        I64b = id32[0:64, 0:64].unsqueeze(1).to_broadcast([64, DH, 64])

        def cols(d, j):
            cd = j if d == 0 else NCH - 1 - j
            return cd * 64

        def prep(j, P):
            tmpN, Abuf, BPbuf = tmpNs[j % 2], Abufs[j % 2], BPbufs[j % 2]
            c0 = [cols(0, j), cols(1, j)]
            g2 = bgs[0:64, j, 1, :]
            b2 = bgs[0:64, j, 0, :]
            nb2 = bgs[0:64, j, 2, :]
            gu1, gu2 = GU[0].next(), GU[1].next()
            for (gu, U) in ((gu1, U1), (gu2, U2)):
                self.tt(gu[0:64, :, :].rearrange("p (d h) s -> p d h s", d=2),
                        g2.rearrange("p (d h) -> p d h", d=2).unsqueeze(3).to_broadcast([64, 2, H, 64]),
                        U.unsqueeze(2).to_broadcast([64, 2, H, 64]), ALU.mult, reads=(bgs.R, ds_.R), writes=(gu.R,))
            dps = slots.next()
            for v, (gu, LL) in enumerate(((gu1, L1), (gu2, L2))):
                for d in range(2):
                    o0 = (v * 2 + d) * H * 64
                    self.mm(dps, dps[0:64, o0:o0 + H * 64], LL[:, d, :], gu[:, d * H:(d + 1) * H, :].rearrange("p h s -> p (h s)"),
                            True, True, reads=(dm.R, gu.R))
            e12 = E12.next()
            self.act(e12[0:64, :, :, :].rearrange("p v a s -> p (v a s)"), dps[0:64, 0:2 * DH * 64], AF.Exp, reads=(dps.R,), writes=(e12.R,))
            gps = slots.next()
            for d in range(2):
                gd = g2[:, d * H:(d + 1) * H]
                for q, LT in enumerate((TC[:, d, :], TN[:, d, :], ON[:, 0:64])):
                    o0 = q * DH + d * H
                    self.mm(gps, gps[0:64, o0:o0 + H], LT, gd, True, True, reads=(ds_.R, bgs.R))
                self.mm(gps, gps[:, 512 + d * H:512 + (d + 1) * H], ON[:, 0:128], gd, True, True, reads=(ds_.R, bgs.R))
            ex_ = ex.next()
            self.act(ex_[0:64, :, :].rearrange("p q a -> p (q a)"), gps[0:64, 0:3 * DH], AF.Exp, reads=(gps.R,), writes=(ex_.R,))
            egl_ = egl.next()
            self.act(egl_[:, :], gps[:, 512:512 + DH], AF.Exp, reads=(gps.R,), writes=(egl_.R,))
            yield
            cf_ = cf0.next()
            self.tt(cf_[0:64, :], b2, ex_[0:64, 0, :], ALU.mult, reads=(bgs.R, ex_.R), writes=(cf_.R,))
            kps, vps = slots.next(), slots.next()
            for (ps, SRC) in ((kps, KT), (vps, VT)):
                for d in range(2):
                    for h in range(H):
                        dh = d * H + h
                        self.mm(ps, ps[0:64, dh * 128:(dh + 1) * 128], SRC[:, h, c0[d]:c0[d] + 64], idb[:], True, True,
                                reads=(SRC.R, idb.R))
            kbg_, kd_, vb_ = kbg.next(), kd.next(), vb.next()
            k3 = kps[0:64, 0:DH * 128].rearrange("p (a k) -> p a k", k=128)
            v3 = vps[0:64, 0:DH * 128].rearrange("p (a k) -> p a k", k=128)
            self.tt(kbg_[0:64, :, :], k3, cf_[0:64, :].unsqueeze(2).to_broadcast([64, DH, 128]), ALU.mult, reads=(kps.R, cf_.R), writes=(kbg_.R,))
            self.tt(kd_[0:64, :, :], k3, ex_[0:64, 1, :].unsqueeze(2).to_broadcast([64, DH, 128]), ALU.mult, reads=(kps.R, ex_.R), writes=(kd_.R,))
            self.tt(vb_[0:64, :, :], v3, b2.unsqueeze(2).to_broadcast([64, DH, 128]), ALU.mult, reads=(vps.R, bgs.R), writes=(vb_.R,))
            yield
            kq = slots.next()
            for d in range(2):
                for h in range(H):
                    dh = d * H + h
                    kk_ = KT[:, h, c0[d]:c0[d] + 64]
                    self.mm(kq, kq[0:64, dh * 64:(dh + 1) * 64], kk_, kk_, True, True, reads=(KT.R,))
                    self.mm(kq, kq[0:64, 512 + dh * 64:512 + (dh + 1) * 64], kk_, QT[:, h, c0[d]:c0[d] + 64], True, True, reads=(KT.R, QT.R))
            A0 = Abuf[0]
            self.tt(tmpN[0:64, :, :], kq[0:64, 0:DH * 64].rearrange("p (a s) -> p a s", s=64), e12[0:64, 0, :, :], ALU.mult,
                    reads=(kq.R, e12.R), writes=(tmpN.R,))
            self.tt(A0[0:64, :, :], tmpN[0:64, :, :], nb2.unsqueeze(2).to_broadcast([64, DH, 64]), ALU.mult,
                    reads=(tmpN.R, bgs.R), writes=(A0.R,))
            at_ = attnT.next()
            self.tt(at_[0:64, :, :], kq[0:64, 512:512 + DH * 64].rearrange("p (a s) -> p a s", s=64), e12[0:64, 1, :, :], ALU.mult,
                    reads=(kq.R, e12.R), writes=(at_.R,))
            yield
            bps = slots.next()
            for dh in range(DH):
                self.mm(bps, bps[0:64, dh * 64:(dh + 1) * 64], A0[0:64, dh, :], idb[0:64, 0:64], True, True, reads=(A0.R, idb.R))
            BP0 = BPbuf[0]
            self.copy(BP0[0:64, :, 0, :], bps[0:64, 0:DH * 64].rearrange("p (a s) -> p a s", s=64), reads=(bps.R,), writes=(BP0.R,), eng="act")
            self.copy(BP0[0:64, :, 1, :], I64b, reads=(id32.R,), writes=(BP0.R,), eng="pool")
            for i in range(6):
                yield
                Ac, BPc = Abuf[i % 2], BPbuf[i % 2]
                An, BPn = Abuf[(i + 1) % 2], BPbuf[(i + 1) % 2]
                xps = slots.next()
                if i < 5:
                    for dh in range(DH):
                        self.mm(xps, xps[0:64, dh * 128:(dh + 1) * 128], Ac[0:64, dh, :], BPc[0:64, dh, :, :].rearrange("p q s -> p (q s)"),
                                True, True, reads=(Ac.R, BPc.R))
                    yps = slots.next()
                    for dh in range(DH):
                        self.mm(yps, yps[0:64, dh * 64:(dh + 1) * 64], BPc[0:64, dh, 0, :], Ac[0:64, dh, :], True, True, reads=(Ac.R, BPc.R))
                    x4 = xps[0:64, 0:DH * 128].rearrange("p (a q s) -> p a q s", q=2, s=64)
                    self.copy(BPn[0:64, :, 0, :], x4[:, :, 0, :], reads=(xps.R,), writes=(BPn.R,), eng="act")
                    self.tt(BPn[0:64, :, 1, :], BPc[0:64, :, 1, :], x4[:, :, 1, :], ALU.add, reads=(BPc.R, xps.R), writes=(BPn.R,))
                    self.copy(An[0:64, :, :], yps[0:64, 0:DH * 64].rearrange("p (a s) -> p a s", s=64), reads=(yps.R,), writes=(An.R,), eng="act")
                else:
                    for dh in range(DH):
                        self.mm(xps, xps[0:64, dh * 64:(dh + 1) * 64], Ac[0:64, dh, :], BPc[0:64, dh, 1, :], True, True, reads=(Ac.R, BPc.R))
                    self.tt(BPn[0:64, :, 1, :], BPc[0:64, :, 1, :], xps[0:64, 0:DH * 64].rearrange("p (a s) -> p a s", s=64), ALU.add,
                            reads=(BPc.R, xps.R), writes=(BPn.R,))
            TT = BPbuf[0]
            yield
            ups = slots.next()
            for dh in range(DH):
                self.mm(ups, ups[0:64, dh * 128:(dh + 1) * 128], TT[0:64, dh, 1, :], vb_[0:64, dh, :], True, True, reads=(TT.R, vb_.R))
            uu = u_.next()
            self.copy(uu[0:64, :, :].rearrange("p a k -> p (a k)"), ups[0:64, 0:DH * 128], reads=(ups.R,), writes=(uu.R,), eng="act")
            wps = slots.next()
            for dh in range(DH):
                self.mm(wps, wps[:, dh * 64:(dh + 1) * 64], kbg_[0:64, dh, :], TT[0:64, dh, 1, :], True, True, reads=(kbg_.R, TT.R))
            wt_ = wT.next()
            self.copy(wt_[:, :, :].rearrange("p a s -> p (a s)"), wps[:, 0:DH * 64], reads=(wps.R,), writes=(wt_.R,), eng="act")
            P.update(c0=c0, ex=ex_, egl=egl_, kd=kd_, at=at_, u=uu, wT=wt_)

        def scan(j, P):
            c0 = P["c0"]
            wsp = slots.next()
            for dh in range(DH):
                self.mm(wsp, wsp[0:64, dh * 128:(dh + 1) * 128], P["wT"][:, dh, :], Sb[:, dh, :], True, True, reads=(P["wT"].R, Sb.R))
            vn = vnew.next()
            self.tt(vn[0:64, :, :].rearrange("p a k -> p (a k)"), P["u"][0:64, :, :].rearrange("p a k -> p (a k)"), wsp[0:64, 0:DH * 128],
                    ALU.subtract, reads=(P["u"].R, wsp.R), writes=(vn.R,))
            yield
            o1, o2 = slots.next(), slots.next()
            for d in range(2):
                for h in range(H):
                    dh = d * H + h
                    self.mm(o1, o1[0:64, dh * 128:(dh + 1) * 128], QT[:, h, c0[d]:c0[d] + 64], Sb[:, dh, :], True, True, reads=(QT.R, Sb.R))
            for dh in range(DH):
                self.mm(o2, o2[0:64, dh * 128:(dh + 1) * 128], P["at"][0:64, dh, :], vn[0:64, dh, :], True, True, reads=(P["at"].R, vn.R))
            os_ = ost.next()
            self.tt(os_[0:64, :, :], o1[0:64, 0:DH * 128].rearrange("p (a k) -> p a k", k=128),
                    P["ex"][0:64, 0, :].unsqueeze(2).to_broadcast([64, DH, 128]), ALU.mult, reads=(o1.R, P["ex"].R), writes=(os_.R,))
            self.tt(os_[0:64, :, :].rearrange("p a k -> p (a k)"), os_[0:64, :, :].rearrange("p a k -> p (a k)"), o2[0:64, 0:DH * 128],
                    ALU.add, reads=(os_.R, o2.R), writes=(os_.R,))
            self.store(os_, self.OF[c0[0]:c0[0] + 64, :], os_[0:64, 0:H, :].rearrange("p a k -> p (a k)"))
            self.store(os_, self.OB[c0[1]:c0[1] + 64, :], os_[0:64, H:DH, :].rearrange("p a k -> p (a k)"))
            yield
            dsp = slots.next()
            for dh in range(DH):
                self.mm(dsp, dsp[:, dh * 128:(dh + 1) * 128], P["kd"][0:64, dh, :], vn[0:64, dh, :], True, True, reads=(P["kd"].R, vn.R))
            self.tt(Stmp[:, :, :], Sf[:, :, :], P["egl"][:, :].unsqueeze(2).to_broadcast([128, DH, 128]), ALU.mult,
                    reads=(Sf.R, P["egl"].R), writes=(Stmp.R,))
            self.tt(Sf[:, :, :].rearrange("p a k -> p (a k)"), Stmp[:, :, :].rearrange("p a k -> p (a k)"), dsp[:, 0:DH * 128], ALU.add,
                    reads=(Stmp.R, dsp.R), writes=(Sf.R,))
            self.copy(Sb[:, :, :], Sf[:, :, :], reads=(Sf.R,), writes=(Sb.R,), eng="act")

        NST1 = 5
        Ps = {}
        gens = {}

        def start(j):
            if j < NCH:
                Ps[j] = {}
                gens[j] = prep(j, Ps[j])

        start(0)
        for _ in gens[0]:
            pass
        start(1)
        if 1 in gens:
            for _ in range(NST1):
                next(gens[1], None)
        for j in range(NCH):
            gB = gens.get(j + 1)
            start(j + 2)
            gA = gens.get(j + 2)
            gs = scan(j, Ps[j])
            for k in range(8):
                if gB is not None:
                    next(gB, None)
                if gA is not None and k < NST1:
                    next(gA, None)
                if k in (0, 2, 4):
                    next(gs, None)
            if gB is not None:
                for _ in gB:
                    pass
            for _ in gs:
                pass
        self.banks = RR([self.bank(i) for i in range(8)])


def host_consts(c):
    out = {}
    out["c_id32"] = np.eye(128, dtype=np.float32)
    out["c_idbf"] = np.eye(128, dtype=np.float32).astype(ml_dtypes.bfloat16)
    cc = np.arange(64)
    ang = 2 * np.pi * np.outer(cc, cc) / 64.0
    C64 = np.cos(ang) / 8.0
    S64 = np.sin(ang) / 8.0
    cs = np.zeros((128, 256), np.float64)
    for g in range(2):
        cs[g * 64:(g + 1) * 64, g * 64:(g + 1) * 64] = C64
        cs[g * 64:(g + 1) * 64, 128 + g * 64:128 + (g + 1) * 64] = S64
    out["c_cs64"] = cs.astype(np.float32).astype(ml_dtypes.bfloat16)
    L = c.L
    tau = np.arange(L)
    pos = np.where(tau < c.SEQ, tau + 16, tau - c.SEQ).astype(np.int64)
    m = (np.outer(pos, pos) % L).astype(np.float64)
    ang = 2 * np.pi * m / L
    cls = np.stack([np.cos(ang), -np.sin(ang)]) / np.sqrt(L)
    clsb = cls.astype(np.float32).astype(ml_dtypes.bfloat16)
    arr = np.zeros((len(c.slices), 128, c.NB, 2, c.SW), dtype=ml_dtypes.bfloat16)
    rows = np.zeros((2, c.NB * 128, L), dtype=ml_dtypes.bfloat16)
    rows[:, :L, :] = clsb
    rows = rows.reshape(2, c.NB, 128, L)
    for si, (t0, n) in enumerate(c.slices):
        arr[si, :, :, :, 0:n] = rows[:, :, :, t0:t0 + n].transpose(2, 1, 0, 3)
    out["c_cls"] = arr
    H = c.H
    p = np.arange(64)[:, None]
    q = np.arange(64)[None, :]
    tric = [(p <= q), (p >= q)]
    a2 = [(p > q), (p < q)]
    u1 = [(p > q), (p < q)]
    u2 = [(p <= q), (p >= q)]
    negm1 = [np.where(p > q, 0.0, NEG), np.where(p < q, 0.0, NEG)]
    negm2 = [np.where(q >= p, 0.0, NEG), np.where(q <= p, 0.0, NEG)]
    eye = np.eye(64)
    L1 = np.zeros((128, 2, 64))
    L2 = np.zeros((128, 2, 64))
    N1 = np.zeros((128, 2, H, 64))
    N2 = np.zeros((128, 2, H, 64))
    for d in range(2):
        L1[:64, d] = tric[d]
        L1[64:, d] = eye
        L2[:64, d] = a2[d]
        L2[64:, d] = eye
        for h in range(H):
            N1[64:, d, h] = negm1[d]
            N2[64:, d, h] = negm2[d]
    out["c_dmask"] = np.concatenate([L1.reshape(128, -1), L2.reshape(128, -1), N1.reshape(128, -1), N2.reshape(128, -1)],
                                    axis=1).astype(np.float32)
    U1 = np.stack(u1, 1).astype(np.float64)
    U2 = np.stack(u2, 1).astype(np.float64)
    TC = np.stack(tric, 1).astype(np.float64)
    TN = 1.0 - TC
    out["c_dsmall"] = np.concatenate([U1.reshape(64, -1), U2.reshape(64, -1), TC.reshape(64, -1), TN.reshape(64, -1),
                                      np.ones((64, 128))], axis=1).astype(np.float32)
    return out


def host_layout(c, inp):
    m = {}
    g = np.asarray(inp["norm_gains"], np.float32)
    m["gains"] = np.ascontiguousarray(g.reshape(c.DEPTH, 6, c.KD, 128).transpose(3, 0, 1, 2).reshape(128, -1))
    sw = np.asarray(inp["sconv_w"], np.float32)
    m["sconv_w"] = np.ascontiguousarray(sw.reshape(c.DEPTH, 3, c.DSC // 128, 128).transpose(3, 0, 1, 2).reshape(128, -1))
    dw = np.asarray(inp["dn_conv_w"], np.float32)
    m["dn_conv_w"] = np.ascontiguousarray(dw.reshape(c.DEPTH, 3, 3 * c.H, 128).transpose(3, 0, 1, 2).reshape(128, -1))
    m["dn_A_log"] = np.ascontiguousarray(np.broadcast_to(np.asarray(inp["dn_A_log"], np.float32).reshape(1, -1), (128, c.DEPTH * 2 * c.H)))
    m["dn_dt_bias"] = np.ascontiguousarray(np.broadcast_to(np.asarray(inp["dn_dt_bias"], np.float32).reshape(1, -1), (128, c.DEPTH * 2 * c.H)))
    dn = np.asarray(inp["dn_norm"], np.float32)
    dnt = np.tile(dn[:, None, :], (1, c.H, 1)).reshape(1, -1)
    m["dn_norm"] = np.ascontiguousarray(np.broadcast_to(dnt, (128, c.DEPTH * c.H * 128)))
    m["meta"] = np.ascontiguousarray(np.asarray(inp["meta_tokens"], np.float32))
    for k in ["ffn1_wi", "ffn1_wo", "ffn2_wi", "ffn2_wo", "w_in", "w_fourier", "w_sconv_out", "w_dn_out", "w_out"]:
        m[k] = np.ascontiguousarray(np.asarray(inp[k], np.float32))
    m.update(host_consts(c))
    return m


_CACHE = {}


def kernel(**inputs):
    c = FULL
    if "nc" not in _CACHE:
        _CACHE["nc"] = Builder(c).build()
    nc = _CACHE["nc"]
    shared = host_layout(c, inputs)
    x = np.asarray(inputs["x"], np.float32)
    B = x.shape[0]
    in_maps = []
    for b in range(B):
        mm_ = dict(shared)
        mm_["x"] = np.ascontiguousarray(x[b])
        in_maps.append(mm_)
    res = run_bass_kernel_spmd(nc, in_maps, core_ids=list(range(B)))
    return np.stack([np.asarray(r["out"], np.float32) for r in res.results], axis=0)
```

```python
import numpy as np
import ml_dtypes
import concourse.bass as bass
import concourse.mybir as mybir
from concourse.bass_utils import run_bass_kernel_spmd

F32 = mybir.dt.float32
BF16 = mybir.dt.bfloat16
AF = mybir.ActivationFunctionType
ALU = mybir.AluOpType
EPS = 1e-6
NEG = -30000.0


class Cfg:
    def __init__(s, D=1024, DFF=2816, SEQ=4096, FG=8, DSC=512, H=4, TT=512, DEPTH=2):
        s.D, s.DFF, s.SEQ, s.FG, s.DSC, s.H, s.TT, s.DEPTH = D, DFF, SEQ, FG, DSC, H, TT, DEPTH
        s.NM = 16
        s.L = SEQ + 16
        s.DF = FG * 64
        s.DDN = H * 128
        s.KD = D // 128
        s.KF = DFF // 128
        s.NCH = SEQ // 64 + 1
        s.LP = 64 * s.NCH
        s.NB = (s.L + 127) // 128
        s.o_f = 0
        s.o_sc = s.DF
        s.o_qkv = s.o_sc + 3 * DSC
        s.o_z = s.o_qkv + 3 * s.DDN
        s.o_b = s.o_z + s.DDN
        s.o_a = s.o_b + 2 * H
        s.o_g = s.o_a + 2 * H
        s.DIN = s.o_g + 3 * D
        s.tiles = []
        nt = SEQ // TT
        for i in range(nt):
            sl = [(i * TT, TT)] if TT <= 512 else [(i * TT + k, 512) for k in range(0, TT, 512)]
            if i == nt - 1:
                sl.append((SEQ, 16))
            s.tiles.append(sl)
        s.TW = TT + 16
        s.GS = 512 if (s.DF % 512 == 0 and DSC % 512 == 0 and s.DDN % 512 == 0 and D % 512 == 0) else 128
        s.SW = min(512, TT)
        s.slices = [sl for tl in s.tiles for sl in tl]


FULL = Cfg()


class Region:
    __slots__ = ("name", "w", "r")

    def __init__(self, name):
        self.name = name
        self.w = None
        self.r = {}


class Sched:
    ENG = ("pe", "dve", "act", "pool", "sp")

    def __init__(self, nc):
        self.nc = nc
        self.ops = {e: [] for e in self.ENG}
        self.cnt = {e: 0 for e in self.ENG}
        self.waited = {e: {} for e in self.ENG}
        self.sems = {}
        self.tot = {}
        for e in self.ENG:
            self.sem("E_" + e)
        self.NPOOL = 64
        for i in range(self.NPOOL):
            self.sem("D%d" % i)
        self.lmap = {}
        self.free = ["D%d" % i for i in range(self.NPOOL)]

    def phys(self, lname):
        if lname.startswith("C_") or lname == "OUT":
            return self.sem(lname)
        if lname not in self.lmap:
            self.lmap[lname] = self.free.pop(0)
        return self.lmap[lname]

    def sem(self, name):
        if name not in self.sems:
            self.sems[name] = self.nc.alloc_semaphore(name)
            self.tot[name] = 0
        return name

    def _waits(self, eng, reads, writes):
        need = {}

        def add(ev):
            if ev is None:
                return
            s, v = ev
            if need.get(s, 0) < v:
                need[s] = v
        for R in reads:
            add(R.w)
        for R in writes:
            add(R.w)
            for s, v in R.r.items():
                add((s, v))
        out = []
        wd = self.waited[eng]
        own = "E_" + eng
        for s, v in need.items():
            if s == own and eng == "pe":
                continue
            if wd.get(s, 0) >= v:
                continue
            wd[s] = v
            out.append((s, v))
        return out

    def op(self, eng, fn, reads=(), writes=()):
        waits = self._waits(eng, reads, writes)
        own = "E_" + eng
        self.cnt[eng] += 1
        self.tot[own] = self.cnt[eng]
        ev = (own, self.cnt[eng])
        self.ops[eng].append((waits, fn, (own, 1)))
        for R in reads:
            if R.r.get(own, 0) < ev[1]:
                R.r[own] = ev[1]
        for R in writes:
            R.w = ev
            R.r = {}
        return ev

    def dma(self, q, out, in_, reads=(), writes=(), sem=None, slow=False):
        waits = self._waits(q, reads, writes)
        sem = self.phys(sem)
        self.tot[sem] += 16
        ev = (sem, self.tot[sem])
        if slow:
            self.ops[q].append((waits, lambda e, o=out, i=in_: e.dma_start(out=o, in_=i, allow_slow_non_contiguous=True), (sem, 16)))
        else:
            self.ops[q].append((waits, lambda e, o=out, i=in_: e.dma_start(out=o, in_=i), (sem, 16)))
        for R in reads:
            if R.r.get(sem, 0) < ev[1]:
                R.r[sem] = ev[1]
        for R in writes:
            R.w = ev
            R.r = {}
        return ev

    def barrier(self):
        for e in self.ENG:
            wl = []
            for s, v in self.tot.items():
                if s.startswith("C_"):
                    continue
                if v > 0 and self.waited[e].get(s, 0) < v and s != "E_" + e:
                    self.waited[e][s] = v
                    wl.append((s, v))
            if wl:
                self.ops[e].append((wl, None, None))
        self.lmap = {}
        self.free = ["D%d" % i for i in range(self.NPOOL)]

    def emit(self):
        nc = self.nc
        sems = self.sems

        def replay(engine, lst):
            for waits, fn, inc in lst:
                for s, v in waits:
                    engine.wait_ge(sems[s], v)
                if fn is not None:
                    ins = fn(engine)
                    ins.then_inc(sems[inc[0]], inc[1])

        with nc.Block() as block:
            @block.tensor
            def _(e):
                replay(e, self.ops["pe"])

            @block.vector
            def _(e):
                replay(e, self.ops["dve"])

            @block.scalar
            def _(e):
                replay(e, self.ops["act"])

            @block.gpsimd
            def _(e):
                replay(e, self.ops["pool"])

            @block.sync
            def _(e):
                replay(e, self.ops["sp"])


class RR:
    def __init__(self, items):
        self.items = items
        self.i = 0

    def next(self):
        x = self.items[self.i % len(self.items)]
        self.i += 1
        return x


class Tile:
    def __init__(self, name, ap):
        self.ap = ap
        self.R = Region(name)

    def __getitem__(self, k):
        return self.ap[k]


class Builder:
    def __init__(self, cfg, stop_after=None):
        self.c = cfg
        self.stop_after = stop_after
        self.nc = bass.Bass("TRN2", target_bir_lowering=False)
        self.S = Sched(self.nc)
        self.evi = 0
        self.uid = 0
        self.ARENA = 52800
        self.arena = self.nc.alloc_sbuf_tensor("arena", [128, self.ARENA], F32)
        self.aptr = 0
        self.psum_all = self.nc.alloc_psum_tensor("psum_all", [128, 4096], F32)

    def din(self, name, shape, dt=F32):
        return self.nc.dram_tensor(name, list(shape), dt, kind="ExternalInput").ap()

    def dscr(self, name, shape, dt=F32):
        return self.nc.dram_tensor(name, list(shape), dt, kind="Internal").ap()

    def tile(self, name, shape, dt=F32, at=None):
        n = 1
        for d in shape[1:]:
            n *= d
        cols = n if dt == F32 else (n + 1) // 2
        cols = (cols + 7) // 8 * 8
        if at is not None:
            a = at
            self.last_end = a + cols
        else:
            a = self.aptr
            self.aptr += cols
            assert self.aptr <= self.ARENA, ("SBUF arena overflow", name, self.aptr)
        ap = self.arena[:, a:a + cols]
        if dt != F32:
            ap = ap.bitcast(dt)
        ap = ap[:, 0:n]
        if len(shape) == 3:
            ap = ap.rearrange("p (a b) -> p a b", b=shape[2])
        elif len(shape) == 4:
            ap = ap.rearrange("p (a b c) -> p a b c", b=shape[2], c=shape[3])
        elif len(shape) == 5:
            ap = ap.rearrange("p (a b c d) -> p a b c d", b=shape[2], c=shape[3], d=shape[4])
        self.uid += 1
        return Tile("%s_%d" % (name, self.uid), ap)

    def bank(self, i, nb=1):
        return Tile("pb%d_%d" % (i, nb), self.psum_all[:, i * 512:(i + nb) * 512])

    def load(self, tl, out, in_, q="sp", extra_reads=(), sem=None):
        self.S.dma(q, out, in_, reads=extra_reads, writes=(tl.R,), sem=sem or ("L_" + tl.R.name))

    def store(self, tl, out, in_, q="act", sem=None):
        self.S.dma(q, out, in_, reads=(tl.R,), writes=(), sem=sem or ("L_" + tl.R.name))

    def mm(self, ps, out, lhsT, rhs, start, stop, reads):
        self.S.op("pe", lambda e: e.matmul(out, lhsT=lhsT, rhs=rhs, start=start, stop=stop),
                  reads=reads, writes=(ps.R,))

    def act(self, out, in_, func, reads, writes, bias=None, scale=None):
        kw = {}
        if bias is not None:
            kw["bias"] = bias
        if scale is not None:
            kw["scale"] = scale
        self.S.op("act", lambda e: e.activation(out=out, in_=in_, func=func, **kw), reads=reads, writes=writes)

    def tt(self, out, in0, in1, op, reads, writes, eng="dve"):
        self.S.op(eng, lambda e: e.tensor_tensor(out=out, in0=in0, in1=in1, op=op), reads=reads, writes=writes)

    def stt(self, out, in0, scalar, in1, op0, op1, reads, writes, eng="dve"):
        self.S.op(eng, lambda e: e.scalar_tensor_tensor(out=out, in0=in0, scalar=scalar, in1=in1, op0=op0, op1=op1),
                  reads=reads, writes=writes)

    def ts(self, out, in0, s1, s2, op0, op1, reads, writes, eng="dve"):
        if s2 is None:
            self.S.op(eng, lambda e: e.tensor_scalar(out=out, in0=in0, scalar1=s1, scalar2=None, op0=op0),
                      reads=reads, writes=writes)
        else:
            self.S.op(eng, lambda e: e.tensor_scalar(out=out, in0=in0, scalar1=s1, scalar2=s2, op0=op0, op1=op1),
                      reads=reads, writes=writes)

    def copy(self, out, in_, reads, writes, eng=None):
        if eng is None:
            self.evi += 1
            eng = "act" if self.evi % 2 else "dve"
        if eng == "act":
            self.S.op("act", lambda e: e.activation(out=out, in_=in_, func=AF.Copy), reads=reads, writes=writes)
        else:
            self.S.op(eng, lambda e: e.tensor_copy(out=out, in_=in_), reads=reads, writes=writes)

    def recip(self, out, in_, reads, writes):
        self.S.op("dve", lambda e: e.reciprocal(out=out, in_=in_), reads=reads, writes=writes)

    def memset(self, tl, ap, val, eng="dve"):
        self.S.op(eng, lambda e: e.memset(ap, val), reads=(), writes=(tl.R,))

    def pcol(self, tau):
        return 1 + (tau + 16 if tau < self.c.SEQ else tau - self.c.SEQ)

    def ppos(self, tau):
        return 64 + tau if tau < self.c.SEQ else 48 + (tau - self.c.SEQ)

    def build(self):
        c = self.c
        nc = self.nc
        S = self.S
        D, KD, KF, L, TW, H = c.D, c.KD, c.KF, c.L, c.TW, c.H
        x_in = self.din("x", [c.SEQ, D])
        meta_in = self.din("meta", [16, D])
        gains_in = self.din("gains", [128, c.DEPTH * 6 * KD])
        Wf32 = {}
        wshapes = {"ffn1_wi": (D, 2 * c.DFF), "ffn1_wo": (c.DFF, D), "ffn2_wi": (D, 2 * c.DFF),
                   "ffn2_wo": (c.DFF, D), "w_in": (D, c.DIN), "w_fourier": (c.DF, D),
                   "w_sconv_out": (c.DSC, D), "w_dn_out": (c.DDN, D), "w_out": (D, D)}
        for k, shp in wshapes.items():
            Wf32[k] = self.din(k, [c.DEPTH, shp[0], shp[1]])
        sconv_in = self.din("sconv_w", [128, c.DEPTH * 3 * (c.DSC // 128)])
        dnconv_in = self.din("dn_conv_w", [128, c.DEPTH * 3 * 3 * H])
        dnA_in = self.din("dn_A_log", [128, c.DEPTH * 2 * H])
        dndt_in = self.din("dn_dt_bias", [128, c.DEPTH * 2 * H])
        dnnorm_in = self.din("dn_norm", [128, c.DEPTH * H * 128])
        id32_in = self.din("c_id32", [128, 128])
        idbf_in = self.din("c_idbf", [128, 128], BF16)
        cs64_in = self.din("c_cs64", [128, 256], BF16)
        cls_in = self.din("c_cls", [len(c.slices), 128, c.NB, 2, c.SW], BF16)
        dmask_in = self.din("c_dmask", [128, 2 * 2 * 64 + 2 * 2 * H * 64])
        dsmall_in = self.din("c_dsmall", [64, 2 * 64 * 4 + 128])
        out_d = self.nc.dram_tensor("out", [c.SEQ, D], F32, kind="ExternalOutput").ap()

        self.out_d = out_d
        Wb = {k: self.dscr("b_" + k, [c.DEPTH, wshapes[k][0], wshapes[k][1]], BF16) for k in ("w_fourier", "w_sconv_out", "w_dn_out")}
        h_scr = self.dscr("h_scr", [D, L])
        self.h_scr = h_scr

        self.gains = self.tile("gains", [128, c.DEPTH * 6 * KD])
        self.load(self.gains, self.gains[:], gains_in)
        self.id32 = self.tile("id32", [128, 128])
        self.load(self.id32, self.id32[:], id32_in)
        self.idbf = self.tile("idbf", [128, 128], BF16)
        self.load(self.idbf, self.idbf[:], idbf_in)
        self.onesb = self.tile("onesb", [128, 128], BF16)
        self.memset(self.onesb, self.onesb[:], 1.0)
        self.epsb = self.tile("epsb", [128, 1])
        self.memset(self.epsb, self.epsb[:], EPS)

        self.banks = RR([self.bank(i) for i in range(8)])

        self.WR = {}
        GS = c.GS
        self.segs = {"ffn1_wi": [(0, c.DFF, GS), (c.DFF, c.DFF, GS)], "ffn2_wi": [(0, c.DFF, GS), (c.DFF, c.DFF, GS)],
                     "w_in": [(0, c.o_b, GS), (c.o_g, 3 * D, GS)], "w_out": [(0, D, GS)],
                     "ffn1_wo": [(0, D, 128)], "ffn2_wo": [(0, D, 128)]}
        self.Wg = {}
        for k, sg in self.segs.items():
            for si, (s0, sn, gs) in enumerate(sg):
                ng = (sn + gs - 1) // gs
                self.Wg[(k, si)] = self.dscr("g_%s_%d" % (k, si), [c.DEPTH, 128, ng, wshapes[k][0] // 128, gs], BF16)
        self.wab_scr = self.dscr("wab_scr", [c.DEPTH, D, 4 * H], BF16)
        def emit_casts(l):
            for k in ["ffn1_wi", "ffn1_wo", "w_in", "w_fourier", "w_sconv_out", "w_dn_out", "w_out", "ffn2_wi", "ffn2_wo"]:
                R = Region("W_%s_%d" % (k, l))
                self.WR[(k, l)] = R
                sem = "C_%s_%d" % (k, l)
                if k in self.segs:
                    src = Wf32[k][l].rearrange("(k p) n -> p k n", p=128)
                    sg = self.segs[k]
                    ngmax = max((sn + gs - 1) // gs for (_, sn, gs) in sg)
                    for g in range(ngmax):
                        for si, (s0, sn, gs) in enumerate(sg):
                            if g * gs >= sn:
                                continue
                            gw = min(gs, sn - g * gs)
                            if k == "ffn1_wi" and l == 0:
                                gsem = "C_%s_%d_g%d" % (k, l, g)
                                S.dma("pool", self.Wg[(k, si)][l, :, g, :, 0:gw], src[:, :, s0 + g * gs:s0 + g * gs + gw],
                                      reads=(), writes=(), sem=gsem)
                                Rg = self.WR.setdefault((k, l, g), Region("Wg_%s_%d_%d" % (k, l, g)))
                                Rg.w = (S.phys(gsem), S.tot[S.phys(gsem)])
                                continue
                            S.dma("pool", self.Wg[(k, si)][l, :, g, :, 0:gw], src[:, :, s0 + g * gs:s0 + g * gs + gw],
                                  reads=(), writes=(), sem=sem)
                    if k == "w_in":
                        S.dma("pool", self.wab_scr[l], Wf32[k][l, :, c.o_b:c.o_b + 4 * H], reads=(), writes=(), sem=sem, slow=True)
                else:
                    S.dma("pool", Wb[k][l], Wf32[k][l], reads=(), writes=(), sem=sem)
                if not (k == "ffn1_wi" and l == 0):
                    R.w = (S.phys(sem), S.tot[S.phys(sem)])
        self.emit_casts = emit_casts
        emit_casts(0)
        self.Wb = Wb

        self.onef = self.tile("onef", [128, 1])
        self.memset(self.onef, self.onef[:], 1.0)
        self.zero = self.tile("zero", [128, 64])
        self.memset(self.zero, self.zero[:], 0.0)
        self.cs64 = self.tile("cs64", [128, 256], BF16)
        self.load(self.cs64, self.cs64[:], cs64_in)
        self.sconv = self.tile("sconv", [128, c.DEPTH * 3 * (c.DSC // 128)])
        self.load(self.sconv, self.sconv[:], sconv_in)
        self.dnconv = self.tile("dnconv", [128, c.DEPTH * 3 * 3 * H])
        self.load(self.dnconv, self.dnconv[:], dnconv_in)
        self.dtb = self.tile("dtb", [128, c.DEPTH * 2 * H])
        self.load(self.dtb, self.dtb[:], dndt_in)
        self.negA = self.tile("negA", [128, c.DEPTH * 2 * H])
        self.load(self.negA, self.negA[:], dnA_in)
        self.act(self.negA[:], self.negA[:], AF.Exp, reads=(self.negA.R,), writes=(self.negA.R,))
        self.ts(self.negA[:], self.negA[:], -1.0, None, ALU.mult, None, reads=(self.negA.R,), writes=(self.negA.R,))
        self.dnnorm = self.tile("dnnorm", [128, c.DEPTH * H * 128])
        self.load(self.dnnorm, self.dnnorm[:], dnnorm_in)
        self.cls_in = cls_in
        self.dmask_in = dmask_in
        self.dsmall_in = dsmall_in

        self.UCS = self.dscr("UCS_scr", [c.NB * 128, 2 * c.DF], BF16)
        self.CH = self.dscr("CH_scr", [c.DSC, L + 2])
        self.BB = self.dscr("BB_scr", [c.DSC, L])
        self.QKV = self.dscr("QKV_scr", [3 * c.DDN, L + 2])
        self.BG = self.dscr("BG_scr", [64, c.NCH, 4 * H])
        self.YF = self.dscr("YF_scr", [c.DF, L], BF16)
        self.OF = self.dscr("OF_scr", [c.LP, c.DDN])
        self.OB = self.dscr("OB_scr", [c.LP, c.DDN])
        for scr, nk in ((self.CH, c.DSC // 128), (self.QKV, 3 * H)):
            v = scr.rearrange("(k p) t -> p k t", p=128)
            for col in (0, L + 1):
                S.dma("sp", v[:, :, col:col + 1], self.zero[:, 0:nk].unsqueeze(2), reads=(self.zero.R,), writes=(), sem="Z", slow=True)
        S.dma("sp", self.BG[0:48, 0, :], self.zero[0:48, 0:4 * H], reads=(self.zero.R,), writes=(), sem="Z")
        if self.stop_after == "nodn":
            for scr in (self.OF, self.OB):
                for r0 in range(0, c.LP, 64):
                    S.dma("sp", scr[r0:r0 + 64, :].rearrange("r (a b) -> r a b", b=64),
                          self.zero[0:64, :].unsqueeze(1).to_broadcast([64, c.DDN // 64, 64]),
                          reads=(self.zero.R,), writes=(), sem="Z")

        self.base_ptr = self.aptr
        self.phase_p0(x_in, meta_in)
        S.barrier()
        for l in range(c.DEPTH):
            if self.stop_after == "ffn":
                self.aptr = self.base_ptr
                self.phase_ffn_only(l)
                S.barrier()
                continue
            self.aptr = self.base_ptr
            self.phase_A(l)
            S.barrier()
            if l + 1 < c.DEPTH:
                self.emit_casts(l + 1)
            self.aptr = self.base_ptr
            self.phase_B(l)
            S.barrier()
            if self.stop_after != "nodn":
                self.aptr = self.base_ptr
                self.phase_C(l)
                S.barrier()
            self.aptr = self.base_ptr
            self.phase_D(l)
            S.barrier()
        S.emit()
        return nc

    def alloc_token_tiles(self):
        c = self.c
        KD, KF, TW, D = c.KD, c.KF, c.TW, c.D
        self.hbuf = [self.tile("h0", [128, KD, TW]), self.tile("h1", [128, KD, TW])]
        self.h = self.hbuf[0]
        self.hidx = 0
        self.hn = self.tile("hn", [128, KD, TW], BF16)
        self.rstd = self.tile("rstd", [128, TW])
        y_off = self.aptr
        self.y = self.tile("y", [128, KD, TW])
        nbk = max(1, (KD * TW) // D)
        self.yout = Tile("yout", self.arena[:, y_off:y_off + nbk * D].rearrange("p (b f) -> p b f", f=D))
        self.yout.R = self.y.R
        DFc, SCc, H = c.DF // 128, c.DSC // 128, c.H
        gcols = (max(KF, KD) * TW + 1) // 2
        ovcols = ((DFc + SCc + H) * TW + 1) // 2 + SCc * TW + 32
        self.g_off = self.aptr
        self.g = self.tile("g", [128, max(KF, KD), TW], BF16)
        self.aptr = self.g_off + max(gcols, ovcols) + 8
        assert self.aptr <= self.ARENA, ("SBUF arena overflow", "g", self.aptr)
        self.sq = Tile("sq", self.g.ap)
        self.sq.R = self.g.R
        self.woslots = RR([self.tile("wo%d" % i, [128, KF, 128], BF16) for i in range(3)])
        self.wslots = RR([self.tile("ws%d" % i, [128, KD, 512], BF16) for i in range(4)])
        self.sa = RR([self.tile("sa%d" % i, [128, 512]) for i in range(2)])
        self.tmpf = RR([self.tile("tmpf%d" % i, [128, 512]) for i in range(2)])

    def phase_p0(self, x_in, meta_in):
        c = self.c
        KD = c.KD
        xin = RR([self.tile("xin%d" % i, [128, c.D]) for i in range(2)])
        hst = RR([self.tile("hst%d" % i, [128, KD, 128]) for i in range(2)])
        blocks = [(i * 128, 128) for i in range(c.SEQ // 128)] + [(c.SEQ, 16)]
        for (t0, n) in blocks:
            xt = xin.next()
            src = x_in[t0:t0 + n, :] if t0 < c.SEQ else meta_in
            self.load(xt, xt[0:n, :], src)
            st = hst.next()
            for k0 in range(0, KD, 4):
                ps = self.banks.next()
                kn = min(4, KD - k0)
                for kk in range(kn):
                    kc = k0 + kk
                    self.S.op("pe", lambda e, ps=ps, kk=kk, xt=xt, kc=kc, n=n: e.transpose(
                        out=ps[:, kk * 128:kk * 128 + n], in_=xt[0:n, kc * 128:(kc + 1) * 128], identity=self.id32[0:n, 0:n]),
                        reads=(xt.R, self.id32.R), writes=(ps.R,))
                self.copy(st[:, k0:k0 + kn, 0:n], ps[:, 0:kn * 128].rearrange("p (k t) -> p k t", t=128)[:, :, 0:n],
                          reads=(ps.R,), writes=(st.R,))
            self.store(st, self.h_scr.rearrange("(k p) t -> p k t", p=128)[:, :, t0:t0 + n], st[:, :, 0:n])

    def phase_pf(self, out_d):
        c = self.c
        KD = c.KD
        hin = RR([self.tile("hin%d" % i, [128, KD, 128]) for i in range(2)])
        ost = RR([self.tile("ost%d" % i, [128, c.D]) for i in range(2)])
        for b in range(c.SEQ // 128):
            t0 = b * 128
            ht = hin.next()
            self.load(ht, ht[:], self.h_scr.rearrange("(k p) t -> p k t", p=128)[:, :, t0:t0 + 128])
            ot = ost.next()
            for k0 in range(0, KD, 4):
                ps = self.banks.next()
                kn = min(4, KD - k0)
                for kk in range(kn):
                    self.S.op("pe", lambda e, ps=ps, kk=kk, ht=ht, kc=k0 + kk: e.transpose(
                        out=ps[:, kk * 128:(kk + 1) * 128], in_=ht[:, kc, :], identity=self.id32[:]),
                        reads=(ht.R, self.id32.R), writes=(ps.R,))
                self.copy(ot[:, k0 * 128:(k0 + kn) * 128], ps[:, 0:kn * 128], reads=(ps.R,), writes=(ot.R,))
            self.store(ot, out_d[t0:t0 + 128, :], ot[:], sem="OUT")

    def gcol(self, l, s, kc):
        c = self.c
        i = (l * 6 + s) * c.KD + kc
        return self.gains[:, i:i + 1]

    def load_h(self, tile_slices):
        self.hidx += 1
        self.h = self.hbuf[self.hidx % 2]
        off = 0
        hv = self.h_scr.rearrange("(k p) t -> p k t", p=128)
        for (t0, n) in tile_slices:
            self.load(self.h, self.h[:, :, off:off + n], hv[:, :, t0:t0 + n])
            off += n

    def store_h(self, tile_slices):
        off = 0
        hv = self.h_scr.rearrange("(k p) t -> p k t", p=128)
        for (t0, n) in tile_slices:
            self.store(self.h, hv[:, :, t0:t0 + n], self.h[:, :, off:off + n])
            off += n

    def slices_off(self, tile_slices):
        out = []
        off = 0
        for (t0, n) in tile_slices:
            out.append((off, n, t0))
            off += n
        return out, off

    def stats_rstd(self, src, nk, W, slo, scale):
        self.act(self.sq[:, 0:nk, 0:W], src[:, 0:nk, 0:W], AF.Square, reads=(src.R,), writes=(self.sq.R,))
        for (off, n, _) in slo:
            ps = self.banks.next()
            for kc in range(nk):
                self.mm(ps, ps[:, 0:n], self.onesb[:], self.sq[:, kc, off:off + n], kc == 0, kc == nk - 1,
                        reads=(self.onesb.R, self.sq.R))
            self.act(self.rstd[:, off:off + n], ps[:, 0:n], AF.Sqrt, reads=(ps.R, self.epsb.R), writes=(self.rstd.R,),
                     bias=self.epsb[:], scale=scale)
        self.recip(self.rstd[:, 0:W], self.rstd[:, 0:W], reads=(self.rstd.R,), writes=(self.rstd.R,))

    def prenorm(self, l, s, W, slo):
        c = self.c
        self.stats_rstd(self.h, c.KD, W, slo, 1.0 / c.D)
        for kc in range(c.KD):
            self.stt(self.hn[:, kc, 0:W], self.h[:, kc, 0:W], self.gcol(l, s, kc), self.rstd[:, 0:W], ALU.mult, ALU.mult,
                     reads=(self.h.R, self.gains.R, self.rstd.R), writes=(self.hn.R,))

    def postnorm_add(self, l, s, W, slo, scale):
        c = self.c
        self.stats_rstd(self.y, c.KD, W, slo, 1.0 / c.D)
        for kc in range(c.KD):
            tmp = self.y
            self.stt(self.y[:, kc, 0:W], self.y[:, kc, 0:W], self.gcol(l, s, kc), self.rstd[:, 0:W], ALU.mult, ALU.mult,
                     reads=(self.y.R, self.gains.R, self.rstd.R), writes=(self.y.R,))
            self.stt(self.h[:, kc, 0:W], self.y[:, kc, 0:W], float(scale), self.h[:, kc, 0:W], ALU.mult, ALU.add,
                     reads=(self.y.R, self.h.R), writes=(self.h.R,))

    def wload(self, key, l, rows_kd, c0, cn):
        ws = self.wslots.next()
        er = [self.WR[(key, l)]]
        for si, (s0, sn, gs) in enumerate(self.segs[key]):
            if s0 <= c0 < s0 + sn and (key, l, (c0 - s0) // gs) in self.WR:
                er.append(self.WR[(key, l, (c0 - s0) // gs)])
        self.load(ws, ws[:, 0:rows_kd, 0:cn], self.wsrc(key, l, c0, cn), extra_reads=tuple(er))
        return ws

    def wsrc(self, key, l, c0, cn):
        for si, (s0, sn, gs) in enumerate(self.segs[key]):
            if s0 <= c0 < s0 + sn:
                break
        assert (c0 - s0) % gs == 0 and cn <= gs, (key, c0, cn, gs)
        return self.Wg[(key, si)][l, :, (c0 - s0) // gs, :, 0:cn]

    def ffn(self, l, which, slo, W):
        c = self.c
        KD, KF, DFF = c.KD, c.KF, c.DFF
        wi, wo = ("ffn1_wi", "ffn1_wo") if which == 1 else ("ffn2_wi", "ffn2_wo")
        s_pre, s_post = (0, 1) if which == 1 else (4, 5)
        self.prenorm(l, s_pre, W, slo)
        G = c.GS // 128
        for g0 in range(0, KF, G):
            gn = min(G, KF - g0)
            wa = self.wload(wi, l, KD, g0 * 128, gn * 128)
            wb = self.wload(wi, l, KD, DFF + g0 * 128, gn * 128)
            for j in range(gn):
                oc = g0 + j
                for (off, n, _) in slo:
                    pa = self.banks.next()
                    pb = self.banks.next()
                    for kc in range(KD):
                        self.mm(pa, pa[:, 0:n], wa[:, kc, j * 128:(j + 1) * 128], self.hn[:, kc, off:off + n],
                                kc == 0, kc == KD - 1, reads=(wa.R, self.hn.R))
                    for kc in range(KD):
                        self.mm(pb, pb[:, 0:n], wb[:, kc, j * 128:(j + 1) * 128], self.hn[:, kc, off:off + n],
                                kc == 0, kc == KD - 1, reads=(wb.R, self.hn.R))
                    sa = self.sa.next()
                    self.act(sa[:, 0:n], pa[:, 0:n], AF.Silu, reads=(pa.R,), writes=(sa.R,))
                    self.tt(self.g[:, oc, off:off + n], sa[:, 0:n], pb[:, 0:n], ALU.mult, reads=(sa.R, pb.R), writes=(self.g.R,))
        for oc0 in range(0, KD, 1):
            ocn = 1
            wt = self.woslots.next()
            self.load(wt, wt[:, :, 0:ocn * 128], self.wsrc(wo, l, oc0 * 128, ocn * 128), extra_reads=(self.WR[(wo, l)],))
            for q in range(ocn):
                oc = oc0 + q
                for (off, n, _) in slo:
                    ps = self.banks.next()
                    for kc in range(KF):
                        self.mm(ps, ps[:, 0:n], wt[:, kc, q * 128:(q + 1) * 128], self.g[:, kc, off:off + n],
                                kc == 0, kc == KF - 1, reads=(wt.R, self.g.R))
                    self.copy(self.y[:, oc, off:off + n], ps[:, 0:n], reads=(ps.R,), writes=(self.y.R,))
        self.postnorm_add(l, s_post, W, slo, 0.5)

    def load_wo(self, l, which):
        return

    def phase_ffn_only(self, l):
        self.alloc_token_tiles()
        for which in (1, 2):
            self.load_wo(l, which)
            for tl in self.c.tiles:
                slo, W = self.slices_off(tl)
                self.load_h(tl)
                self.ffn(l, which, slo, W)
                self.store_h(tl)
            self.S.barrier()


    def linear(self, key, l, c0, ncols, kd, rhs, slo, evac):
        noc = ncols // 128
        G = self.c.GS // 128
        for g0 in range(0, noc, G):
            gn = min(G, noc - g0)
            ws = self.wload(key, l, kd, c0 + g0 * 128, gn * 128)
            for j in range(gn):
                for (off, n, _) in slo:
                    ps = self.banks.next()
                    for kc in range(kd):
                        self.mm(ps, ps[:, 0:n], ws[:, kc, j * 128:(j + 1) * 128], rhs[:, kc, off:off + n],
                                kc == 0, kc == kd - 1, reads=(ws.R, rhs.R))
                    evac(g0 + j, off, n, ps)

    def phase_A(self, l):
        c = self.c
        S = self.S
        KD, H, DF, DSC, DDN, TW, L = c.KD, c.H, c.DF, c.DSC, c.DDN, c.TW, c.L
        DFc, SCc = DF // 128, DSC // 128
        self.alloc_token_tiles()
        uf = self.tile("uf", [128, DFc, TW], BF16)
        ucs_st = RR([self.tile("ucs_st%d" % i, [128, 2, DF], BF16) for i in range(2)])
        st = RR([self.tile("st%d" % i, [128, 4, TW]) for i in range(2)])
        wab = self.tile("wab", [128, KD, 4 * H], BF16)
        bgst = RR([self.tile("bgst%d" % i, [128, 4 * H]) for i in range(2)])
        xa = self.tile("xa", [128, 2 * H])
        ta = self.tile("ta", [128, 2 * H])
        self.load(wab, wab[:], self.wab_scr[l].rearrange("(k p) n -> p k n", p=128), extra_reads=(self.WR[("w_in", l)],))
        self.load_wo(l, 1)
        chv = self.CH.rearrange("(k p) t -> p k t", p=128)
        bbv = self.BB.rearrange("(k p) t -> p k t", p=128)
        qkvv = self.QKV.rearrange("(k p) t -> p k t", p=128)
        for tl in c.tiles:
            slo, W = self.slices_off(tl)
            self.load_h(tl)
            self.ffn(l, 1, slo, W)
            self.store_h(tl)
            self.prenorm(l, 2, W, slo)
            hn = self.hn
            self.linear("w_in", l, c.o_f, DF, KD, hn, slo,
                        lambda oc, off, n, ps: self.copy(uf[:, oc, off:off + n], ps[:, 0:n], reads=(ps.R,), writes=(uf.R,)))
            for (off, n, t0) in slo:
                for b0 in range(0, n, 128):
                    bn = min(128, n - b0)
                    us = ucs_st.next()
                    for fp in range(0, DFc, 2):
                        npair = min(2, DFc - fp)
                        ps = self.banks.next()
                        for q in range(npair):
                            self.mm(ps, ps[0:bn, q * 256:(q + 1) * 256], uf[:, fp + q, off + b0:off + b0 + bn], self.cs64[:],
                                    True, True, reads=(uf.R, self.cs64.R))
                        self.copy(us[0:bn, :, fp * 128:(fp + npair) * 128].rearrange("p c (f k) -> p c f k", k=128),
                                  ps[0:bn, 0:npair * 256].rearrange("p (f c k) -> p c f k", c=2, k=128),
                                  reads=(ps.R,), writes=(us.R,))
                    self.store(us, self.UCS[t0 + b0:t0 + b0 + bn, :], us[0:bn, :, :].rearrange("p c f -> p (c f)"))
            stb = st.next()
            self.linear("w_in", l, c.o_sc, DSC, KD, hn, slo,
                        lambda oc, off, n, ps: self.copy(stb[:, oc, off:off + n], ps[:, 0:n], reads=(ps.R,), writes=(stb.R,)))
            for (off, n, t0) in slo:
                self.store(stb, bbv[:, :, t0:t0 + n], stb[:, 0:SCc, off:off + n])
            stc = st.next()
            wc = self.wload("w_in", l, KD, c.o_sc + DSC, DSC)
            wh = self.wload("w_in", l, KD, c.o_sc + 2 * DSC, DSC)
            for j in range(SCc):
                for (off, n, _) in slo:
                    pc = self.banks.next()
                    ph = self.banks.next()
                    for kc in range(KD):
                        self.mm(pc, pc[:, 0:n], wc[:, kc, j * 128:(j + 1) * 128], hn[:, kc, off:off + n], kc == 0, kc == KD - 1,
                                reads=(wc.R, hn.R))
                    for kc in range(KD):
                        self.mm(ph, ph[:, 0:n], wh[:, kc, j * 128:(j + 1) * 128], hn[:, kc, off:off + n], kc == 0, kc == KD - 1,
                                reads=(wh.R, hn.R))
                    tf = self.tmpf.next()
                    self.copy(tf[:, 0:n], pc[:, 0:n], reads=(pc.R,), writes=(tf.R,), eng="act")
                    self.tt(stc[:, j, off:off + n], tf[:, 0:n], ph[:, 0:n], ALU.mult, reads=(tf.R, ph.R), writes=(stc.R,))
            for (off, n, t0) in slo:
                pc0 = self.pcol(t0)
                self.store(stc, chv[:, :, pc0:pc0 + n], stc[:, 0:SCc, off:off + n])
            for g0 in range(0, 3 * H, 4):
                gn = min(4, 3 * H - g0)
                stq = st.next()
                self.linear("w_in", l, c.o_qkv + g0 * 128, gn * 128, KD, hn, slo,
                            lambda oc, off, n, ps, stq=stq: self.copy(stq[:, oc, off:off + n], ps[:, 0:n], reads=(ps.R,), writes=(stq.R,)))
                for (off, n, t0) in slo:
                    pc0 = self.pcol(t0)
                    self.store(stq, qkvv[:, g0:g0 + gn, pc0:pc0 + n], stq[:, 0:gn, off:off + n])
            H2 = 2 * H
            for (off, n, t0) in slo:
                for b0 in range(0, n, 128):
                    bn = min(128, n - b0)
                    ps = self.banks.next()
                    for kc in range(KD):
                        self.mm(ps, ps[0:bn, 0:4 * H], hn[:, kc, off + b0:off + b0 + bn], wab[:, kc, :], kc == 0, kc == KD - 1,
                                reads=(hn.R, wab.R))
                    bg = bgst.next()
                    self.act(bg[0:bn, 0:H2], ps[0:bn, 0:H2], AF.Sigmoid, reads=(ps.R,), writes=(bg.R,))
                    self.tt(xa[0:bn, :], ps[0:bn, H2:2 * H2], self.dtb[0:bn, l * H2:(l + 1) * H2], ALU.add,
                            reads=(ps.R, self.dtb.R), writes=(xa.R,))
                    self.ts(ta[0:bn, :], xa[0:bn, :], 30.0, None, ALU.min, None, reads=(xa.R,), writes=(ta.R,))
                    self.tt(xa[0:bn, :], xa[0:bn, :], ta[0:bn, :], ALU.subtract, reads=(xa.R, ta.R), writes=(xa.R,))
                    self.act(ta[0:bn, :], ta[0:bn, :], AF.Exp, reads=(ta.R,), writes=(ta.R,))
                    self.act(ta[0:bn, :], ta[0:bn, :], AF.Ln, reads=(ta.R, self.onef.R), writes=(ta.R,), bias=self.onef[0:bn, :])
                    self.tt(xa[0:bn, :], xa[0:bn, :], ta[0:bn, :], ALU.add, reads=(xa.R, ta.R), writes=(xa.R,))
                    self.tt(bg[0:bn, H2:2 * H2], xa[0:bn, :], self.negA[0:bn, l * H2:(l + 1) * H2], ALU.mult,
                            reads=(xa.R, self.negA.R), writes=(bg.R,))
                    tau0 = t0 + b0
                    if tau0 >= c.SEQ:
                        self.store(bg, self.BG[48:64, 0, :], bg[0:16, :])
                    else:
                        ch0 = 1 + tau0 // 64
                        self.store(bg, self.BG[:, ch0, :], bg[0:64, :])
                        self.store(bg, self.BG[:, ch0 + 1, :], bg[64:128, :])

    def phase_B(self, l):
        c = self.c
        DF, NB, L = c.DF, c.NB, c.L
        DFc = DF // 128
        ucs = self.tile("ucs", [128, NB, 2, DF], BF16)
        uv = self.UCS.rearrange("(b p) c -> p b c", p=128)
        NBF = L // 128
        for b0 in range(0, NBF, 8):
            bn = min(8, NBF - b0)
            self.load(ucs, ucs[:, b0:b0 + bn, :, :].rearrange("p b c f -> p b (c f)"), uv[:, b0:b0 + bn, :])
        if NBF < NB:
            rn = L - NBF * 128
            self.load(ucs, ucs[0:rn, NBF, :, :].rearrange("p c f -> p (c f)"), self.UCS[NBF * 128:NBF * 128 + rn, :])
        GB = 4
        dft = RR([self.tile("dft%d" % i, [128, GB, 2, c.SW], BF16) for i in range(6)])
        sidx = 0
        yst = RR([self.tile("yst%d" % i, [128, DFc, 512], BF16) for i in range(2)])
        yfv = self.YF.rearrange("(k p) t -> p k t", p=128)
        for tl in c.tiles:
            for (t0, n) in tl:
                pss = [self.banks.next() for _ in range(DFc)]
                for i0 in range(0, NB, GB):
                    ni = min(GB, NB - i0)
                    dt_ = dft.next()
                    self.load(dt_, dt_[:, 0:ni, :, :], self.cls_in[sidx, :, i0:i0 + ni, :, :])
                    for fc in range(DFc):
                        for i in range(ni):
                            blk = i0 + i
                            rn = min(128, L - blk * 128)
                            for cs in range(2):
                                first = (blk == 0 and cs == 0)
                                last = (blk == NB - 1 and cs == 1)
                                self.mm(pss[fc], pss[fc][:, 0:n], ucs[0:rn, blk, cs, fc * 128:(fc + 1) * 128], dt_[0:rn, i, cs, 0:n],
                                        first, last, reads=(ucs.R, dt_.R))
                sidx += 1
                ys = yst.next()
                for fc in range(DFc):
                    self.copy(ys[:, fc, 0:n], pss[fc][:, 0:n], reads=(pss[fc].R,), writes=(ys.R,))
                self.store(ys, yfv[:, :, t0:t0 + n], ys[:, :, 0:n])

    def phase_D(self, l):
        c = self.c
        KD, H, DF, DSC, DDN, TW, L, D = c.KD, c.H, c.DF, c.DSC, c.DDN, c.TW, c.L, c.D
        DFc, SCc = DF // 128, DSC // 128
        self.alloc_token_tiles()
        yf = self.tile("yf", [128, DFc, TW], BF16, at=self.g_off)
        scin = self.tile("scin", [128, SCc, TW], BF16, at=self.last_end)
        dnin = self.tile("dnin", [128, H, TW], BF16, at=self.last_end)
        bb = self.tile("bb", [128, SCc, TW], at=self.last_end)
        for t_ in (yf, scin, dnin, bb):
            t_.R = self.g.R
        chw = RR([self.tile("chw%d" % i, [128, SCc, 514]) for i in range(1)])
        wbr = [self.tile("wbr%d" % i, [128, kk, D], BF16) for i, kk in enumerate((DFc, SCc, H))]
        wz = self.tile("wz", [128, KD, DDN], BF16)
        oft = RR([self.tile("oft%d" % i, [128, DDN]) for i in range(2)])
        obt = RR([self.tile("obt%d" % i, [128, DDN]) for i in range(2)])
        zs = self.tile("zs", [128, DDN])
        dtm = self.tile("dtm", [128, DDN], BF16)
        ss = self.tile("ss", [128, H])
        junk = self.tile("junk", [128, 128])
        for i, key in enumerate(("w_fourier", "w_sconv_out", "w_dn_out")):
            self.load(wbr[i], wbr[i][:], self.Wb[key][l].rearrange("(k p) n -> p k n", p=128), extra_reads=(self.WR[(key, l)],))
        for z0 in range(0, DDN, c.GS):
            self.load(wz, wz[:, :, z0:z0 + c.GS], self.wsrc("w_in", l, c.o_z + z0, c.GS), extra_reads=(self.WR[("w_in", l)],))
        self.load_wo(l, 2)
        yfv = self.YF.rearrange("(k p) t -> p k t", p=128)
        chv = self.CH.rearrange("(k p) t -> p k t", p=128)
        bbv = self.BB.rearrange("(k p) t -> p k t", p=128)
        nsc = c.DEPTH * 3 * SCc

        def scw(tap, j):
            i = (l * 3 + tap) * SCc + j
            return self.sconv[:, i:i + 1]
        for tl in c.tiles:
            slo, W = self.slices_off(tl)
            self.load_h(tl)
            self.prenorm(l, 2, W, slo)
            hn = self.hn
            blocks = []
            for (off, n, t0) in slo:
                self.load(yf, yf[:, :, off:off + n], yfv[:, :, t0:t0 + n])
                self.load(bb, bb[:, :, off:off + n], bbv[:, :, t0:t0 + n])
                cw = chw.next()
                pc0 = self.pcol(t0)
                self.load(cw, cw[:, :, 0:n + 2], chv[:, :, pc0 - 1:pc0 + n + 1])
                for j in range(SCc):
                    tf = self.tmpf.next()
                    self.ts(tf[:, 0:n], cw[:, j, 1:n + 1], scw(1, j), None, ALU.mult, None, reads=(cw.R, self.sconv.R), writes=(tf.R,))
                    self.stt(tf[:, 0:n], cw[:, j, 0:n], scw(0, j), tf[:, 0:n], ALU.mult, ALU.add,
                             reads=(cw.R, self.sconv.R, tf.R), writes=(tf.R,))
                    self.stt(tf[:, 0:n], cw[:, j, 2:n + 2], scw(2, j), tf[:, 0:n], ALU.mult, ALU.add,
                             reads=(cw.R, self.sconv.R, tf.R), writes=(tf.R,))
                    self.tt(scin[:, j, off:off + n], tf[:, 0:n], bb[:, j, off:off + n], ALU.mult, reads=(tf.R, bb.R), writes=(scin.R,))
                for b0 in range(0, n, 128):
                    blocks.append((off, b0, min(128, n - b0), t0))

            def dn_block(off, b0, bn, t0):
                p0 = self.ppos(t0 + b0)
                of_, ob_ = oft.next(), obt.next()
                self.load(of_, of_[0:bn, :], self.OF[p0:p0 + bn, :])
                self.load(ob_, ob_[0:bn, :], self.OB[p0:p0 + bn, :])
                self.tt(of_[0:bn, :], of_[0:bn, :], ob_[0:bn, :], ALU.add, reads=(of_.R, ob_.R), writes=(of_.R,))
                pz = self.banks.next()
                for kc in range(KD):
                    self.mm(pz, pz[0:bn, 0:DDN], hn[:, kc, off + b0:off + b0 + bn], wz[:, kc, :], kc == 0, kc == KD - 1,
                            reads=(hn.R, wz.R))
                self.act(zs[0:bn, :], pz[0:bn, 0:DDN], AF.Silu, reads=(pz.R,), writes=(zs.R,))
                for hh in range(H):
                    self.S.op("act", lambda e, hh=hh, of_=of_, bn=bn: e.activation(
                        out=junk[0:bn, :], in_=of_[0:bn, hh * 128:(hh + 1) * 128], func=AF.Square, accum_out=ss[0:bn, hh:hh + 1]),
                        reads=(of_.R,), writes=(junk.R, ss.R))
                self.act(ss[0:bn, :], ss[0:bn, :], AF.Sqrt, reads=(ss.R, self.epsb.R), writes=(ss.R,), bias=self.epsb[0:bn, :],
                         scale=1.0 / 128.0)
                self.recip(ss[0:bn, :], ss[0:bn, :], reads=(ss.R,), writes=(ss.R,))
                o3 = of_[0:bn, :].rearrange("p (h d) -> p h d", d=128)
                self.tt(o3, o3, ss[0:bn, :].unsqueeze(2).to_broadcast([bn, H, 128]), ALU.mult, reads=(of_.R, ss.R), writes=(of_.R,))
                self.tt(of_[0:bn, :], of_[0:bn, :], self.dnnorm[0:bn, l * DDN:(l + 1) * DDN], ALU.mult,
                        reads=(of_.R, self.dnnorm.R), writes=(of_.R,))
                self.tt(dtm[0:bn, :], of_[0:bn, :], zs[0:bn, :], ALU.mult, reads=(of_.R, zs.R), writes=(dtm.R,))
                yield
                pt = self.banks.next()
                for hh in range(H):
                    self.mm(pt, pt[:, hh * 128:hh * 128 + bn], dtm[0:bn, hh * 128:(hh + 1) * 128], self.idbf[0:bn, 0:bn], True, True,
                            reads=(dtm.R, self.idbf.R))
                self.copy(dnin[:, :, off + b0:off + b0 + bn], pt[:, 0:H * 128].rearrange("p (h t) -> p h t", t=128)[:, :, 0:bn],
                          reads=(pt.R,), writes=(dnin.R,))

            brin = (yf, scin, dnin)
            brk = (DFc, SCc, H)
            G = c.GS // 128

            def gate_unit(br, g0):
                gn = min(G, KD - g0)
                ws = self.wload("w_in", l, KD, c.o_g + br * D + g0 * 128, gn * 128)
                for j in range(gn):
                    oc = g0 + j
                    for (off, n, _) in slo:
                        pg = self.banks.next()
                        py = self.banks.next()
                        for kc in range(KD):
                            self.mm(pg, pg[:, 0:n], ws[:, kc, j * 128:(j + 1) * 128], hn[:, kc, off:off + n], kc == 0, kc == KD - 1,
                                    reads=(ws.R, hn.R))
                        for kb in range(brk[br]):
                            self.mm(py, py[:, 0:n], wbr[br][:, kb, oc * 128:(oc + 1) * 128], brin[br][:, kb, off:off + n],
                                    kb == 0, kb == brk[br] - 1, reads=(wbr[br].R, brin[br].R))
                        sg = self.sa.next()
                        self.act(sg[:, 0:n], pg[:, 0:n], AF.Sigmoid, reads=(pg.R,), writes=(sg.R,))
                        if br == 0:
                            self.tt(self.y[:, oc, off:off + n], sg[:, 0:n], py[:, 0:n], ALU.mult, reads=(sg.R, py.R), writes=(self.y.R,))
                        else:
                            self.tt(sg[:, 0:n], sg[:, 0:n], py[:, 0:n], ALU.mult, reads=(sg.R, py.R), writes=(sg.R,))
                            self.tt(self.y[:, oc, off:off + n], self.y[:, oc, off:off + n], sg[:, 0:n], ALU.add,
                                    reads=(self.y.R, sg.R), writes=(self.y.R,))

            units01 = [(br, g0) for br in (0, 1) for g0 in range(0, KD, G)]
            ui = 0
            for blk in blocks:
                gd = dn_block(*blk)
                next(gd)
                if ui < len(units01):
                    gate_unit(*units01[ui])
                    ui += 1
                for _ in gd:
                    pass
            while ui < len(units01):
                gate_unit(*units01[ui])
                ui += 1
            for g0 in range(0, KD, G):
                gate_unit(2, g0)
            self.copy(hn[:, :, 0:W], self.y[:, :, 0:W], reads=(self.y.R,), writes=(hn.R,))
            self.linear("w_out", l, 0, D, KD, hn, slo,
                        lambda oc, off, n, ps: self.copy(self.y[:, oc, off:off + n], ps[:, 0:n], reads=(ps.R,), writes=(self.y.R,)))
            self.postnorm_add(l, 3, W, slo, 1.0)
            self.ffn(l, 2, slo, W)
            if l < c.DEPTH - 1:
                self.store_h(tl)
            else:
                bi = 0
                for (off, n, t0) in slo:
                    if t0 >= c.SEQ:
                        continue
                    for b0 in range(0, n, 128):
                        for k0 in range(0, KD, 4):
                            ps = self.banks.next()
                            kn = min(4, KD - k0)
                            for kk in range(kn):
                                self.S.op("pe", lambda e, ps=ps, kk=kk, hh_=self.h, kc=k0 + kk, o0=off + b0: e.transpose(
                                    out=ps[:, kk * 128:(kk + 1) * 128], in_=hh_[:, kc, o0:o0 + 128], identity=self.id32[:]),
                                    reads=(self.h.R, self.id32.R), writes=(ps.R,))
                            self.copy(self.yout[:, bi, k0 * 128:(k0 + kn) * 128], ps[:, 0:kn * 128], reads=(ps.R,), writes=(self.yout.R,))
                        self.store(self.yout, self.out_d[t0 + b0:t0 + b0 + 128, :], self.yout[:, bi, :], sem="OUT")
                        bi += 1


    def phase_C(self, l):
        c = self.c
        S = self.S
        H, LP, NCH, L = c.H, c.LP, c.NCH, c.L
        H3, DH = 3 * H, 2 * H
        QT = self.tile("QT", [128, H, LP], BF16)
        KT = self.tile("KT", [128, H, LP], BF16)
        VT = self.tile("VT", [128, H, LP], BF16)
        bg = self.tile("bg", [128, NCH, 4 * H])
        bgs = self.tile("bgs", [128, NCH, 3, DH])
        self.load(bg, bg[0:64, :, :], self.BG)
        for T_ in (QT, KT, VT):
            self.memset(T_, T_[:, :, 0:48], 0.0, eng="pool")
        self.copy(bgs[0:64, :, 0, 0:H], bg[0:64, :, 0:H], reads=(bg.R,), writes=(bgs.R,), eng="pool")
        self.copy(bgs[0:64, :, 1, 0:H], bg[0:64, :, 2 * H:3 * H], reads=(bg.R,), writes=(bgs.R,), eng="pool")
        for j in range(NCH):
            cb = NCH - 1 - j
            self.copy(bgs[0:64, j, 0:2, H:DH], bg[0:64, cb, :].rearrange("p (q d h) -> p q d h", q=2, d=2)[:, :, 1, :],
                      reads=(bg.R,), writes=(bgs.R,), eng="pool")
        self.ts(bgs[0:64, :, 2, :], bgs[0:64, :, 0, :], -1.0, None, ALU.mult, None, reads=(bgs.R,), writes=(bgs.R,), eng="pool")
        mark = self.aptr
        WW = min(512, c.SEQ)
        raw = RR([self.tile("raw%d" % i, [128, H3, WW + 2]) for i in range(2)])
        acc = self.tile("acc", [128, H3, WW])
        sqq = RR([self.tile("sqq%d" % i, [128, WW], BF16) for i in range(2)])
        rsq = RR([self.tile("rsq%d" % i, [128, WW]) for i in range(2)])
        qkvv = self.QKV.rearrange("(k p) t -> p k t", p=128)
        self.banks = RR([self.bank(i) for i in range(8)])

        def cwt(tap, j):
            i = (l * 3 + tap) * H3 + j
            return self.dnconv[:, i:i + 1]
        wins = [(t0, WW) for t0 in range(0, c.SEQ, WW)] + [(c.SEQ, 16)]
        for (t0, n) in wins:
            rw = raw.next()
            pc0, pp0 = self.pcol(t0), self.ppos(t0)
            self.load(rw, rw[:, :, 0:n + 2], qkvv[:, :, pc0 - 1:pc0 + n + 1])
            for j in range(H3):
                eng = "dve"
                self.act(acc[:, j, 0:n], rw[:, j, 1:n + 1], AF.Copy, reads=(rw.R, self.dnconv.R), writes=(acc.R,), scale=cwt(1, j))
                self.stt(acc[:, j, 0:n], rw[:, j, 0:n], cwt(0, j), acc[:, j, 0:n], ALU.mult, ALU.add,
                         reads=(rw.R, self.dnconv.R, acc.R), writes=(acc.R,), eng=eng)
                self.stt(acc[:, j, 0:n], rw[:, j, 2:n + 2], cwt(2, j), acc[:, j, 0:n], ALU.mult, ALU.add,
                         reads=(rw.R, self.dnconv.R, acc.R), writes=(acc.R,), eng=eng)
            self.act(acc[:, :, 0:n], acc[:, :, 0:n], AF.Silu, reads=(acc.R,), writes=(acc.R,))
            for j in range(2 * H):
                sq_, rs_ = sqq.next(), rsq.next()
                self.act(sq_[:, 0:n], acc[:, j, 0:n], AF.Square, reads=(acc.R,), writes=(sq_.R,))
                ps = self.banks.next()
                self.mm(ps, ps[:, 0:n], self.onesb[:], sq_[:, 0:n], True, True, reads=(self.onesb.R, sq_.R))
                self.act(rs_[:, 0:n], ps[:, 0:n], AF.Sqrt, reads=(ps.R, self.epsb.R), writes=(rs_.R,), bias=self.epsb[:])
                self.recip(rs_[:, 0:n], rs_[:, 0:n], reads=(rs_.R,), writes=(rs_.R,))
                if j < H:
                    self.stt(QT[:, j, pp0:pp0 + n], acc[:, j, 0:n], float(128 ** -0.5), rs_[:, 0:n], ALU.mult, ALU.mult,
                             reads=(acc.R, rs_.R), writes=(QT.R,))
                else:
                    self.tt(KT[:, j - H, pp0:pp0 + n], acc[:, j, 0:n], rs_[:, 0:n], ALU.mult, reads=(acc.R, rs_.R), writes=(KT.R,))
            self.copy(VT[:, :, pp0:pp0 + n], acc[:, 2 * H:3 * H, 0:n], reads=(acc.R,), writes=(VT.R,), eng="pool")
        S.barrier()
        self.aptr = mark
        dm = self.tile("dm", [128, 2 * 2 * 64])
        self.load(dm, dm[:], self.dmask_in[:, 0:256])
        L1 = dm[:, 0:128].rearrange("p (d s) -> p d s", s=64)
        L2 = dm[:, 128:256].rearrange("p (d s) -> p d s", s=64)
        ds_ = self.tile("ds", [128, 2 * 64 * 4 + 128])
        self.load(ds_, ds_[0:64, :], self.dsmall_in)
        U1 = ds_[0:64, 0:128].rearrange("p (d s) -> p d s", s=64)
        U2 = ds_[0:64, 128:256].rearrange("p (d s) -> p d s", s=64)
        TC = ds_[0:64, 256:384].rearrange("p (d s) -> p d s", s=64)
        TN = ds_[0:64, 384:512].rearrange("p (d s) -> p d s", s=64)
        ON = ds_[0:64, 512:640]
        GU = []
        for v in range(2):
            pair = []
            for i in range(2):
                t_ = self.tile("GU%d_%d" % (v, i), [128, DH, 64])
                o0 = 256 + v * (2 * H * 64)
                self.load(t_, t_[64:128, :, :].rearrange("p a s -> p (a s)"), self.dmask_in[64:128, o0:o0 + DH * 64])
                pair.append(t_)
            GU.append(RR(pair))
        Sf = self.tile("Sf", [128, DH, 128])
        Sb = self.tile("Sb", [128, DH, 128], BF16)
        Stmp = self.tile("Stmp", [128, DH, 128])
        self.memset(Sf, Sf[:], 0.0)
        self.memset(Sb, Sb[:], 0.0)
        E12 = RR([self.tile("E12_%d" % i, [128, 2, DH, 64]) for i in range(2)])
        ex = RR([self.tile("ex%d" % i, [128, 3, DH]) for i in range(3)])
        egl = RR([self.tile("egl%d" % i, [128, DH]) for i in range(3)])
        cf0 = RR([self.tile("cf0_%d" % i, [128, DH]) for i in range(2)])
        kbg = RR([self.tile("kbg%d" % i, [128, DH, 128], BF16) for i in range(2)])
        kd = RR([self.tile("kd%d" % i, [128, DH, 128], BF16) for i in range(3)])
        vb = RR([self.tile("vb%d" % i, [128, DH, 128], BF16) for i in range(2)])
        tmpNs = [self.tile("tmpN%d" % i, [128, DH, 64]) for i in range(2)]
        Abufs = [[self.tile("A%d_%d" % (q, i), [128, DH, 64], BF16) for i in range(2)] for q in range(2)]
        BPbufs = [[self.tile("BP%d_%d" % (q, i), [128, DH, 2, 64], BF16) for i in range(2)] for q in range(2)]
        attnT = RR([self.tile("attnT%d" % i, [128, DH, 64], BF16) for i in range(3)])
        u_ = RR([self.tile("u%d" % i, [128, DH, 128]) for i in range(3)])
        wT = RR([self.tile("wT%d" % i, [128, DH, 64], BF16) for i in range(3)])
        vnew = RR([self.tile("vnew%d" % i, [128, DH, 128], BF16) for i in range(2)])
        ost = RR([self.tile("ost%d" % i, [128, DH, 128]) for i in range(2)])
        slots = RR([self.bank(0, 2), self.bank(2, 2), self.bank(4, 2), self.bank(6, 2)])
        idb, id32 = self.idbf, self.id32
        I64b = id32[0:64, 0:64].unsqueeze(1).to_broadcast([64, DH, 64])

        def cols(d, j):
            cd = j if d == 0 else NCH - 1 - j
            return cd * 64

        def prep(j, P):
            tmpN, Abuf, BPbuf = tmpNs[j % 2], Abufs[j % 2], BPbufs[j % 2]
            c0 = [cols(0, j), cols(1, j)]
            g2 = bgs[0:64, j, 1, :]
            b2 = bgs[0:64, j, 0, :]
            nb2 = bgs[0:64, j, 2, :]
            gu1, gu2 = GU[0].next(), GU[1].next()
            for (gu, U) in ((gu1, U1), (gu2, U2)):
                self.tt(gu[0:64, :, :].rearrange("p (d h) s -> p d h s", d=2),
                        g2.rearrange("p (d h) -> p d h", d=2).unsqueeze(3).to_broadcast([64, 2, H, 64]),
                        U.unsqueeze(2).to_broadcast([64, 2, H, 64]), ALU.mult, reads=(bgs.R, ds_.R), writes=(gu.R,))
            dps = slots.next()
            for v, (gu, LL) in enumerate(((gu1, L1), (gu2, L2))):
                for d in range(2):
                    o0 = (v * 2 + d) * H * 64
                    self.mm(dps, dps[0:64, o0:o0 + H * 64], LL[:, d, :], gu[:, d * H:(d + 1) * H, :].rearrange("p h s -> p (h s)"),
                            True, True, reads=(dm.R, gu.R))
            e12 = E12.next()
            self.act(e12[0:64, :, :, :].rearrange("p v a s -> p (v a s)"), dps[0:64, 0:2 * DH * 64], AF.Exp, reads=(dps.R,), writes=(e12.R,))
            gps = slots.next()
            for d in range(2):
                gd = g2[:, d * H:(d + 1) * H]
                for q, LT in enumerate((TC[:, d, :], TN[:, d, :], ON[:, 0:64])):
                    o0 = q * DH + d * H
                    self.mm(gps, gps[0:64, o0:o0 + H], LT, gd, True, True, reads=(ds_.R, bgs.R))
                self.mm(gps, gps[:, 512 + d * H:512 + (d + 1) * H], ON[:, 0:128], gd, True, True, reads=(ds_.R, bgs.R))
            ex_ = ex.next()
            self.act(ex_[0:64, :, :].rearrange("p q a -> p (q a)"), gps[0:64, 0:3 * DH], AF.Exp, reads=(gps.R,), writes=(ex_.R,))
            egl_ = egl.next()
            self.act(egl_[:, :], gps[:, 512:512 + DH], AF.Exp, reads=(gps.R,), writes=(egl_.R,))
            yield
            cf_ = cf0.next()
            self.tt(cf_[0:64, :], b2, ex_[0:64, 0, :], ALU.mult, reads=(bgs.R, ex_.R), writes=(cf_.R,))
            kps, vps = slots.next(), slots.next()
            for (ps, SRC) in ((kps, KT), (vps, VT)):
                for d in range(2):
                    for h in range(H):
                        dh = d * H + h
                        self.mm(ps, ps[0:64, dh * 128:(dh + 1) * 128], SRC[:, h, c0[d]:c0[d] + 64], idb[:], True, True,
                                reads=(SRC.R, idb.R))
            kbg_, kd_, vb_ = kbg.next(), kd.next(), vb.next()
            k3 = kps[0:64, 0:DH * 128].rearrange("p (a k) -> p a k", k=128)
            v3 = vps[0:64, 0:DH * 128].rearrange("p (a k) -> p a k", k=128)
            self.tt(kbg_[0:64, :, :], k3, cf_[0:64, :].unsqueeze(2).to_broadcast([64, DH, 128]), ALU.mult, reads=(kps.R, cf_.R), writes=(kbg_.R,))
            self.tt(kd_[0:64, :, :], k3, ex_[0:64, 1, :].unsqueeze(2).to_broadcast([64, DH, 128]), ALU.mult, reads=(kps.R, ex_.R), writes=(kd_.R,))
            self.tt(vb_[0:64, :, :], v3, b2.unsqueeze(2).to_broadcast([64, DH, 128]), ALU.mult, reads=(vps.R, bgs.R), writes=(vb_.R,))
            yield
            kq = slots.next()
            for d in range(2):
                for h in range(H):
                    dh = d * H + h
                    kk_ = KT[:, h, c0[d]:c0[d] + 64]
                    self.mm(kq, kq[0:64, dh * 64:(dh + 1) * 64], kk_, kk_, True, True, reads=(KT.R,))
                    self.mm(kq, kq[0:64, 512 + dh * 64:512 + (dh + 1) * 64], kk_, QT[:, h, c0[d]:c0[d] + 64], True, True, reads=(KT.R, QT.R))
            A0 = Abuf[0]
            self.tt(tmpN[0:64, :, :], kq[0:64, 0:DH * 64].rearrange("p (a s) -> p a s", s=64), e12[0:64, 0, :, :], ALU.mult,
                    reads=(kq.R, e12.R), writes=(tmpN.R,))
            self.tt(A0[0:64, :, :], tmpN[0:64, :, :], nb2.unsqueeze(2).to_broadcast([64, DH, 64]), ALU.mult,
                    reads=(tmpN.R, bgs.R), writes=(A0.R,))
            at_ = attnT.next()
            self.tt(at_[0:64, :, :], kq[0:64, 512:512 + DH * 64].rearrange("p (a s) -> p a s", s=64), e12[0:64, 1, :, :], ALU.mult,
                    reads=(kq.R, e12.R), writes=(at_.R,))
            yield
            bps = slots.next()
            for dh in range(DH):
                self.mm(bps, bps[0:64, dh * 64:(dh + 1) * 64], A0[0:64, dh, :], idb[0:64, 0:64], True, True, reads=(A0.R, idb.R))
            BP0 = BPbuf[0]
            self.copy(BP0[0:64, :, 0, :], bps[0:64, 0:DH * 64].rearrange("p (a s) -> p a s", s=64), reads=(bps.R,), writes=(BP0.R,), eng="act")
            self.copy(BP0[0:64, :, 1, :], I64b, reads=(id32.R,), writes=(BP0.R,), eng="pool")
            for i in range(6):
                yield
                Ac, BPc = Abuf[i % 2], BPbuf[i % 2]
                An, BPn = Abuf[(i + 1) % 2], BPbuf[(i + 1) % 2]
                xps = slots.next()
                if i < 5:
                    for dh in range(DH):
                        self.mm(xps, xps[0:64, dh * 128:(dh + 1) * 128], Ac[0:64, dh, :], BPc[0:64, dh, :, :].rearrange("p q s -> p (q s)"),
                                True, True, reads=(Ac.R, BPc.R))
                    yps = slots.next()
                    for dh in range(DH):
                        self.mm(yps, yps[0:64, dh * 64:(dh + 1) * 64], BPc[0:64, dh, 0, :], Ac[0:64, dh, :], True, True, reads=(Ac.R, BPc.R))
                    x4 = xps[0:64, 0:DH * 128].rearrange("p (a q s) -> p a q s", q=2, s=64)
                    self.copy(BPn[0:64, :, 0, :], x4[:, :, 0, :], reads=(xps.R,), writes=(BPn.R,), eng="act")
                    self.tt(BPn[0:64, :, 1, :], BPc[0:64, :, 1, :], x4[:, :, 1, :], ALU.add, reads=(BPc.R, xps.R), writes=(BPn.R,))
                    self.copy(An[0:64, :, :], yps[0:64, 0:DH * 64].rearrange("p (a s) -> p a s", s=64), reads=(yps.R,), writes=(An.R,), eng="act")
                else:
                    for dh in range(DH):
                        self.mm(xps, xps[0:64, dh * 64:(dh + 1) * 64], Ac[0:64, dh, :], BPc[0:64, dh, 1, :], True, True, reads=(Ac.R, BPc.R))
                    self.tt(BPn[0:64, :, 1, :], BPc[0:64, :, 1, :], xps[0:64, 0:DH * 64].rearrange("p (a s) -> p a s", s=64), ALU.add,
                            reads=(BPc.R, xps.R), writes=(BPn.R,))
            TT = BPbuf[0]
            yield
            ups = slots.next()
            for dh in range(DH):
                self.mm(ups, ups[0:64, dh * 128:(dh + 1) * 128], TT[0:64, dh, 1, :], vb_[0:64, dh, :], True, True, reads=(TT.R, vb_.R))
            uu = u_.next()
            self.copy(uu[0:64, :, :].rearrange("p a k -> p (a k)"), ups[0:64, 0:DH * 128], reads=(ups.R,), writes=(uu.R,), eng="act")
            wps = slots.next()
            for dh in range(DH):
                self.mm(wps, wps[:, dh * 64:(dh + 1) * 64], kbg_[0:64, dh, :], TT[0:64, dh, 1, :], True, True, reads=(kbg_.R, TT.R))
            wt_ = wT.next()
            self.copy(wt_[:, :, :].rearrange("p a s -> p (a s)"), wps[:, 0:DH * 64], reads=(wps.R,), writes=(wt_.R,), eng="act")
            P.update(c0=c0, ex=ex_, egl=egl_, kd=kd_, at=at_, u=uu, wT=wt_)

        def scan(j, P):
            c0 = P["c0"]
            wsp = slots.next()
            for dh in range(DH):
                self.mm(wsp, wsp[0:64, dh * 128:(dh + 1) * 128], P["wT"][:, dh, :], Sb[:, dh, :], True, True, reads=(P["wT"].R, Sb.R))
            vn = vnew.next()
            self.tt(vn[0:64, :, :].rearrange("p a k -> p (a k)"), P["u"][0:64, :, :].rearrange("p a k -> p (a k)"), wsp[0:64, 0:DH * 128],
                    ALU.subtract, reads=(P["u"].R, wsp.R), writes=(vn.R,))
            yield
            o1, o2 = slots.next(), slots.next()
            for d in range(2):
                for h in range(H):
                    dh = d * H + h
                    self.mm(o1, o1[0:64, dh * 128:(dh + 1) * 128], QT[:, h, c0[d]:c0[d] + 64], Sb[:, dh, :], True, True, reads=(QT.R, Sb.R))
            for dh in range(DH):
                self.mm(o2, o2[0:64, dh * 128:(dh + 1) * 128], P["at"][0:64, dh, :], vn[0:64, dh, :], True, True, reads=(P["at"].R, vn.R))
            os_ = ost.next()
            self.tt(os_[0:64, :, :], o1[0:64, 0:DH * 128].rearrange("p (a k) -> p a k", k=128),
                    P["ex"][0:64, 0, :].unsqueeze(2).to_broadcast([64, DH, 128]), ALU.mult, reads=(o1.R, P["ex"].R), writes=(os_.R,))
            self.tt(os_[0:64, :, :].rearrange("p a k -> p (a k)"), os_[0:64, :, :].rearrange("p a k -> p (a k)"), o2[0:64, 0:DH * 128],
                    ALU.add, reads=(os_.R, o2.R), writes=(os_.R,))
            self.store(os_, self.OF[c0[0]:c0[0] + 64, :], os_[0:64, 0:H, :].rearrange("p a k -> p (a k)"))
            self.store(os_, self.OB[c0[1]:c0[1] + 64, :], os_[0:64, H:DH, :].rearrange("p a k -> p (a k)"))
            yield
            dsp = slots.next()
            for dh in range(DH):
                self.mm(dsp, dsp[:, dh * 128:(dh + 1) * 128], P["kd"][0:64, dh, :], vn[0:64, dh, :], True, True, reads=(P["kd"].R, vn.R))
            self.tt(Stmp[:, :, :], Sf[:, :, :], P["egl"][:, :].unsqueeze(2).to_broadcast([128, DH, 128]), ALU.mult,
                    reads=(Sf.R, P["egl"].R), writes=(Stmp.R,))
            self.tt(Sf[:, :, :].rearrange("p a k -> p (a k)"), Stmp[:, :, :].rearrange("p a k -> p (a k)"), dsp[:, 0:DH * 128], ALU.add,
                    reads=(Stmp.R, dsp.R), writes=(Sf.R,))
            self.copy(Sb[:, :, :], Sf[:, :, :], reads=(Sf.R,), writes=(Sb.R,), eng="act")

        NST1 = 5
        Ps = {}
        gens = {}

        def start(j):
            if j < NCH:
                Ps[j] = {}
                gens[j] = prep(j, Ps[j])

        start(0)
        for _ in gens[0]:
            pass
        start(1)
        if 1 in gens:
            for _ in range(NST1):
                next(gens[1], None)
        for j in range(NCH):
            gB = gens.get(j + 1)
            start(j + 2)
            gA = gens.get(j + 2)
            gs = scan(j, Ps[j])
            for k in range(8):
                if gB is not None:
                    next(gB, None)
                if gA is not None and k < NST1:
                    next(gA, None)
                if k in (0, 2, 4):
                    next(gs, None)
            if gB is not None:
                for _ in gB:
                    pass
            for _ in gs:
                pass
        self.banks = RR([self.bank(i) for i in range(8)])


def host_consts(c):
    out = {}
    out["c_id32"] = np.eye(128, dtype=np.float32)
    out["c_idbf"] = np.eye(128, dtype=np.float32).astype(ml_dtypes.bfloat16)
    cc = np.arange(64)
    ang = 2 * np.pi * np.outer(cc, cc) / 64.0
    C64 = np.cos(ang) / 8.0
    S64 = np.sin(ang) / 8.0
    cs = np.zeros((128, 256), np.float64)
    for g in range(2):
        cs[g * 64:(g + 1) * 64, g * 64:(g + 1) * 64] = C64
        cs[g * 64:(g + 1) * 64, 128 + g * 64:128 + (g + 1) * 64] = S64
    out["c_cs64"] = cs.astype(np.float32).astype(ml_dtypes.bfloat16)
    L = c.L
    tau = np.arange(L)
    pos = np.where(tau < c.SEQ, tau + 16, tau - c.SEQ).astype(np.int64)
    m = (np.outer(pos, pos) % L).astype(np.float64)
    ang = 2 * np.pi * m / L
    cls = np.stack([np.cos(ang), -np.sin(ang)]) / np.sqrt(L)
    clsb = cls.astype(np.float32).astype(ml_dtypes.bfloat16)
    arr = np.zeros((len(c.slices), 128, c.NB, 2, c.SW), dtype=ml_dtypes.bfloat16)
    rows = np.zeros((2, c.NB * 128, L), dtype=ml_dtypes.bfloat16)
    rows[:, :L, :] = clsb
    rows = rows.reshape(2, c.NB, 128, L)
    for si, (t0, n) in enumerate(c.slices):
        arr[si, :, :, :, 0:n] = rows[:, :, :, t0:t0 + n].transpose(2, 1, 0, 3)
    out["c_cls"] = arr
    H = c.H
    p = np.arange(64)[:, None]
    q = np.arange(64)[None, :]
    tric = [(p <= q), (p >= q)]
    a2 = [(p > q), (p < q)]
    u1 = [(p > q), (p < q)]
    u2 = [(p <= q), (p >= q)]
    negm1 = [np.where(p > q, 0.0, NEG), np.where(p < q, 0.0, NEG)]
    negm2 = [np.where(q >= p, 0.0, NEG), np.where(q <= p, 0.0, NEG)]
    eye = np.eye(64)
    L1 = np.zeros((128, 2, 64))
    L2 = np.zeros((128, 2, 64))
    N1 = np.zeros((128, 2, H, 64))
    N2 = np.zeros((128, 2, H, 64))
    for d in range(2):
        L1[:64, d] = tric[d]
        L1[64:, d] = eye
        L2[:64, d] = a2[d]
        L2[64:, d] = eye
        for h in range(H):
            N1[64:, d, h] = negm1[d]
            N2[64:, d, h] = negm2[d]
    out["c_dmask"] = np.concatenate([L1.reshape(128, -1), L2.reshape(128, -1), N1.reshape(128, -1), N2.reshape(128, -1)],
                                    axis=1).astype(np.float32)
    U1 = np.stack(u1, 1).astype(np.float64)
    U2 = np.stack(u2, 1).astype(np.float64)
    TC = np.stack(tric, 1).astype(np.float64)
    TN = 1.0 - TC
    out["c_dsmall"] = np.concatenate([U1.reshape(64, -1), U2.reshape(64, -1), TC.reshape(64, -1), TN.reshape(64, -1),
                                      np.ones((64, 128))], axis=1).astype(np.float32)
    return out


def host_layout(c, inp):
    m = {}
    g = np.asarray(inp["norm_gains"], np.float32)
    m["gains"] = np.ascontiguousarray(g.reshape(c.DEPTH, 6, c.KD, 128).transpose(3, 0, 1, 2).reshape(128, -1))
    sw = np.asarray(inp["sconv_w"], np.float32)
    m["sconv_w"] = np.ascontiguousarray(sw.reshape(c.DEPTH, 3, c.DSC // 128, 128).transpose(3, 0, 1, 2).reshape(128, -1))
    dw = np.asarray(inp["dn_conv_w"], np.float32)
    m["dn_conv_w"] = np.ascontiguousarray(dw.reshape(c.DEPTH, 3, 3 * c.H, 128).transpose(3, 0, 1, 2).reshape(128, -1))
    m["dn_A_log"] = np.ascontiguousarray(np.broadcast_to(np.asarray(inp["dn_A_log"], np.float32).reshape(1, -1), (128, c.DEPTH * 2 * c.H)))
    m["dn_dt_bias"] = np.ascontiguousarray(np.broadcast_to(np.asarray(inp["dn_dt_bias"], np.float32).reshape(1, -1), (128, c.DEPTH * 2 * c.H)))
    dn = np.asarray(inp["dn_norm"], np.float32)
    dnt = np.tile(dn[:, None, :], (1, c.H, 1)).reshape(1, -1)
    m["dn_norm"] = np.ascontiguousarray(np.broadcast_to(dnt, (128, c.DEPTH * c.H * 128)))
    m["meta"] = np.ascontiguousarray(np.asarray(inp["meta_tokens"], np.float32))
    for k in ["ffn1_wi", "ffn1_wo", "ffn2_wi", "ffn2_wo", "w_in", "w_fourier", "w_sconv_out", "w_dn_out", "w_out"]:
        m[k] = np.ascontiguousarray(np.asarray(inp[k], np.float32))
    m.update(host_consts(c))
    return m


_CACHE = {}


def kernel(**inputs):
    c = FULL
    if "nc" not in _CACHE:
        _CACHE["nc"] = Builder(c).build()
    nc = _CACHE["nc"]
    shared = host_layout(c, inputs)
    x = np.asarray(inputs["x"], np.float32)
    B = x.shape[0]
    in_maps = []
    for b in range(B):
        mm_ = dict(shared)
        mm_["x"] = np.ascontiguousarray(x[b])
        in_maps.append(mm_)
    res = run_bass_kernel_spmd(nc, in_maps, core_ids=list(range(B)))
    return np.stack([np.asarray(r["out"], np.float32) for r in res.results], axis=0)
```
